# Optimizing a Trainium2 kernel written in Bass

```python
import functools
import jax, jax.numpy as jnp
from jax import lax
import numpy as np

D_MODEL = 1024
BATCH = 4
SEQ = 4096
DEPTH = 2
DEC_BATCH = 128
DEC_SEQ = 8
PAST_LEN = 16384
PAGE_SIZE = 128

HEAD_DIM = 64
N_HEADS = 8
N_KV_HEADS = 2
KV_WIDTH = N_KV_HEADS * HEAD_DIM
WINDOW = 128
ATTN_BLOCK = 128
ROPE_THETA = 500000.0
ROPE_DIMS = HEAD_DIM // 4
ATTN_WIDTH = N_HEADS * HEAD_DIM
GMLP_WIDTH = D_MODEL - ATTN_WIDTH
GMLP_GROUPS = 4
GMLP_GW = GMLP_WIDTH // GMLP_GROUPS
CHUNK = 128
MIX_WIDTH = ATTN_WIDTH + GMLP_WIDTH
IN_WIDTH = ATTN_WIDTH + 2 * KV_WIDTH + 2 * GMLP_WIDTH
IN_SPLITS = (ATTN_WIDTH, ATTN_WIDTH + KV_WIDTH, ATTN_WIDTH + 2 * KV_WIDTH,
             ATTN_WIDTH + 2 * KV_WIDTH + GMLP_WIDTH)
D_FF = 2816
CONV_W = 3
EPS = 1e-6
NEG = -1e30

kernel_name = 'hymba_swa_sink_gmlp_convffn_adaln_step'


def _rmsnorm(x, g):
    xf = x.astype(jnp.float32)
    y = xf * lax.rsqrt(jnp.mean(xf * xf, axis=-1, keepdims=True) + EPS)
    return (y * g.astype(jnp.float32)).astype(x.dtype)


def _layernorm(x, g, b):
    xf = x.astype(jnp.float32)
    mu = jnp.mean(xf, axis=-1, keepdims=True)
    xc = xf - mu
    y = xc * lax.rsqrt(jnp.mean(xc * xc, axis=-1, keepdims=True) + EPS)
    return (y * g.astype(jnp.float32) + b.astype(jnp.float32)).astype(x.dtype)


def _rope(x, pos):
    half = ROPE_DIMS // 2
    inv = ROPE_THETA ** (-jnp.arange(0, ROPE_DIMS, 2, dtype=jnp.float32) / ROPE_DIMS)
    ang = pos.astype(jnp.float32)[:, None] * inv[None, :]
    cos = jnp.cos(ang)[:, None, :]
    sin = jnp.sin(ang)[:, None, :]
    xr = x[..., :ROPE_DIMS].astype(jnp.float32)
    x1, x2 = xr[..., :half], xr[..., half:]
    rot = jnp.concatenate([x1 * cos - x2 * sin, x2 * cos + x1 * sin], axis=-1).astype(x.dtype)
    return jnp.concatenate([rot, x[..., ROPE_DIMS:]], axis=-1)


def _sink_attention(q, k, v, mask, sinks):
    lead = q.shape[:-3]
    tq = q.shape[-3]
    grp = N_HEADS // N_KV_HEADS
    qg = q.reshape(*lead, tq, N_KV_HEADS, grp, HEAD_DIM)
    s = jnp.einsum('...qkgd,...skd->...kgqs', qg, k).astype(jnp.float32) * (HEAD_DIM ** -0.5)
    s = jnp.where(mask, s, NEG)
    sink = jnp.broadcast_to(sinks.astype(jnp.float32).reshape(N_KV_HEADS, grp, 1, 1), s.shape[:-1] + (1,))
    p = jax.nn.softmax(jnp.concatenate([s, sink], axis=-1), axis=-1)[..., :-1]
    o = jnp.einsum('...kgqs,...skd->...qkgd', p.astype(v.dtype), v)
    return o.reshape(*lead, tq, N_HEADS * HEAD_DIM)


def _swa_prompt(q, k, v, sinks):
    b, t = q.shape[:2]
    nb = t // ATTN_BLOCK
    qb = q.reshape(b, nb, ATTN_BLOCK, N_HEADS, HEAD_DIM)

    def band(z):
        zb = z.reshape(b, nb, ATTN_BLOCK, N_KV_HEADS, HEAD_DIM)
        prev = jnp.pad(zb[:, :-1], ((0, 0), (1, 0), (0, 0), (0, 0), (0, 0)))
        return jnp.concatenate([prev, zb], axis=2)

    blk = jnp.arange(nb)[:, None]
    qpos = blk * ATTN_BLOCK + jnp.arange(ATTN_BLOCK)[None, :]
    kpos = (blk - 1) * ATTN_BLOCK + jnp.arange(2 * ATTN_BLOCK)[None, :]
    diff = qpos[:, :, None] - kpos[:, None, :]
    mask = (diff >= 0) & (diff < WINDOW) & (kpos[:, None, :] >= 0)
    o = _sink_attention(qb, band(k), band(v), mask[:, None, None], sinks)
    return o.reshape(b, t, ATTN_WIDTH)


def _swa_sample(q, k, v, sinks, k_cache, v_cache):
    b, t = q.shape[:2]
    wc = k_cache.shape[1]
    kk = jnp.concatenate([k_cache.astype(k.dtype), k], axis=1)
    vv = jnp.concatenate([v_cache.astype(v.dtype), v], axis=1)
    qpos = PAST_LEN + jnp.arange(t)
    kpos = jnp.concatenate([PAST_LEN - wc + jnp.arange(wc), qpos])
    diff = qpos[:, None] - kpos[None, :]
    mask = (diff >= 0) & (diff < WINDOW)
    return _sink_attention(q, kk, vv, mask, sinks).reshape(b, t, ATTN_WIDTH)


def _spatial_gate(u, v, w_s, b_s):
    b, l = u.shape[:2]
    lp = -(-l // CHUNK) * CHUNK
    vp = jnp.pad(v, ((0, 0), (0, lp - l), (0, 0))).reshape(b, lp // CHUNK, CHUNK, GMLP_GROUPS, GMLP_GW)
    causal = jnp.tril(jnp.ones((CHUNK, CHUNK), dtype=bool))
    w = jnp.where(causal[None], w_s, jnp.zeros((), w_s.dtype))
    z = jnp.einsum('gts,bcsgw->bctgw', w, vp) + b_s.T[None, None, :, :, None]
    return u * z.reshape(b, lp, GMLP_WIDTH)[:, :l]


def _layer(x, c, pos, attend, conv_prev, p):
    b, t, _ = x.shape
    mod = jnp.dot(jax.nn.silu(c), p['w_ada']) + p['b_ada']
    sh_a, sc_a, g_a, sh_f, sc_f, g_f = [m[:, None, :] for m in jnp.split(mod, 6, axis=-1)]
    h = _rmsnorm(x, p['g_attn']) * (1 + sc_a) + sh_a
    q, k, v, zu, zv = jnp.split(h @ p['w_in'], IN_SPLITS, axis=-1)
    q = _rope(_rmsnorm(q.reshape(b, t, N_HEADS, HEAD_DIM), p['g_q']), pos)
    k = _rope(_rmsnorm(k.reshape(b, t, N_KV_HEADS, HEAD_DIM), p['g_k']), pos)
    v = v.reshape(b, t, N_KV_HEADS, HEAD_DIM)
    a_out = attend(q, k, v, p['sinks'])
    u = jax.nn.gelu(zu)
    vg = _layernorm(jax.nn.gelu(zv).reshape(b, t, GMLP_GROUPS, GMLP_GW), p['ln_g'], p['ln_b'])
    vg = vg.reshape(b, t, GMLP_WIDTH)
    s_out = _spatial_gate(u, vg, p['w_s'], p['b_s'])
    x = x + g_a * (jnp.concatenate([a_out, s_out], axis=-1) @ p['w_out'])
    h = _rmsnorm(x, p['g_ffn']) * (1 + sc_f) + sh_f
    hu = h @ p['w_ffn_in']
    ext = jnp.concatenate([conv_prev.astype(hu.dtype), hu], axis=1)
    conv = p['conv_b'] + sum(p['conv_w'][j] * ext[:, j:j + t] for j in range(CONV_W))
    gate, up = jnp.split(conv, 2, axis=-1)
    x = x + g_f * ((jax.nn.silu(gate) * up) @ p['w_ffn_out'])
    return x, k, v, vg, ext[:, -(CONV_W - 1):]


def setup_inputs(seed: int = 0) -> dict:
    key = jax.random.key(seed)
    ks = jax.random.split(key, 32)
    f32 = jnp.float32
    nrm = lambda k, shape, s: jax.random.normal(k, shape, f32) * s
    return {
        'x_prompt': nrm(ks[0], (BATCH, SEQ, D_MODEL), 1.0),
        'x_sample': nrm(ks[1], (DEC_BATCH, DEC_SEQ, D_MODEL), 1.0),
        'cache_k': nrm(ks[2], (DEPTH, DEC_BATCH, WINDOW, N_KV_HEADS, HEAD_DIM), 1.0),
        'cache_v': nrm(ks[3], (DEPTH, DEC_BATCH, WINDOW, N_KV_HEADS, HEAD_DIM), 1.0),
        'cache_conv': nrm(ks[4], (DEPTH, DEC_BATCH, CONV_W - 1, 2 * D_FF), 1.0),
        'c_prompt': nrm(ks[5], (BATCH, D_MODEL), 1.0),
        'c_sample': nrm(ks[6], (DEC_BATCH, D_MODEL), 1.0),
        'w_ada': nrm(ks[7], (DEPTH, D_MODEL, 6 * D_MODEL), 0.5 * D_MODEL ** -0.5),
        'b_ada': nrm(ks[8], (DEPTH, 6 * D_MODEL), 0.02),
        'g_attn': 1.0 + nrm(ks[9], (DEPTH, D_MODEL), 0.02),
        'w_in': nrm(ks[10], (DEPTH, D_MODEL, IN_WIDTH), D_MODEL ** -0.5),
        'g_q': 1.0 + nrm(ks[11], (DEPTH, HEAD_DIM), 0.02),
        'g_k': 1.0 + nrm(ks[12], (DEPTH, HEAD_DIM), 0.02),
        'sinks': nrm(ks[13], (DEPTH, N_HEADS), 1.0),
        'ln_g': 1.0 + nrm(ks[14], (DEPTH, GMLP_GROUPS, GMLP_GW), 0.02),
        'ln_b': nrm(ks[15], (DEPTH, GMLP_GROUPS, GMLP_GW), 0.02),
        'w_s': nrm(ks[16], (DEPTH, GMLP_GROUPS, CHUNK, CHUNK), CHUNK ** -0.5),
        'b_s': 1.0 + nrm(ks[17], (DEPTH, GMLP_GROUPS, CHUNK), 0.1),
        'w_out': nrm(ks[18], (DEPTH, MIX_WIDTH, D_MODEL), MIX_WIDTH ** -0.5),
        'g_ffn': 1.0 + nrm(ks[19], (DEPTH, D_MODEL), 0.02),
        'w_ffn_in': nrm(ks[20], (DEPTH, D_MODEL, 2 * D_FF), D_MODEL ** -0.5),
        'conv_w': nrm(ks[21], (DEPTH, CONV_W, 2 * D_FF), CONV_W ** -0.5),
        'conv_b': nrm(ks[22], (DEPTH, 2 * D_FF), 0.02),
        'w_ffn_out': nrm(ks[23], (DEPTH, D_FF, D_MODEL), D_FF ** -0.5),
    }


def reference(x_prompt, x_sample, cache_k, cache_v, cache_conv, c_prompt, c_sample,
              w_ada, b_ada, g_attn, w_in, g_q, g_k, sinks, ln_g, ln_b, w_s, b_s, w_out,
              g_ffn, w_ffn_in, conv_w, conv_b, w_ffn_out):
    pos_p = jnp.arange(x_prompt.shape[1], dtype=jnp.int32)
    pos_s = PAST_LEN + jnp.arange(x_sample.shape[1], dtype=jnp.int32)
    xp, xs = x_prompt, x_sample
    kp_l, vp_l, cp_l, ks_l, vs_l, gs_l, cs_l = [], [], [], [], [], [], []
    for l in range(DEPTH):
        p = {'w_ada': w_ada[l], 'b_ada': b_ada[l], 'g_attn': g_attn[l], 'w_in': w_in[l],
             'g_q': g_q[l], 'g_k': g_k[l], 'sinks': sinks[l], 'ln_g': ln_g[l], 'ln_b': ln_b[l],
             'w_s': w_s[l], 'b_s': b_s[l], 'w_out': w_out[l], 'g_ffn': g_ffn[l],
             'w_ffn_in': w_ffn_in[l], 'conv_w': conv_w[l], 'conv_b': conv_b[l],
             'w_ffn_out': w_ffn_out[l]}
        zero_conv = jnp.zeros((xp.shape[0], CONV_W - 1, 2 * D_FF), xp.dtype)
        xp, kp, vp, _, cp = _layer(xp, c_prompt, pos_p, _swa_prompt, zero_conv, p)
        kp_l.append(kp[:, -WINDOW:])
        vp_l.append(vp[:, -WINDOW:])
        cp_l.append(cp)
        attend_s = functools.partial(_swa_sample, k_cache=cache_k[l], v_cache=cache_v[l])
        xs, ks_, vs_, gs_, cs_ = _layer(xs, c_sample, pos_s, attend_s, cache_conv[l], p)
        ks_l.append(ks_)
        vs_l.append(vs_)
        gs_l.append(gs_)
        cs_l.append(cs_)
    new_k_prompt = jnp.stack(kp_l)
    new_v_prompt = jnp.stack(vp_l)
    new_conv_prompt = jnp.stack(cp_l)
    new_k_sample = jnp.stack(ks_l)
    new_v_sample = jnp.stack(vs_l)
    new_gmlp_v_sample = jnp.stack(gs_l)
    new_conv_sample = jnp.stack(cs_l)
    return (xp, xs, new_k_prompt, new_v_prompt, new_conv_prompt,
            new_k_sample, new_v_sample, new_gmlp_v_sample, new_conv_sample)
```

```python
import contextlib
import os
import numpy as np
import concourse.bass as bass
import concourse.mybir as mybir
from concourse.bass_utils import run_bass_kernel_spmd

F32 = mybir.dt.float32
BF16 = mybir.dt.bfloat16
AF = mybir.ActivationFunctionType
ALU = mybir.AluOpType
AX = mybir.AxisListType

D = 1024
NB = 19
NPT = NB * 128
NT = NPT + 128
SOFF = NPT
DFF = 2816
NF = 22
EPS = 1e-6
ENGS = ("pe", "act", "dve", "pool", "sp")


class _Op:
    __slots__ = ("idx", "eng", "fn", "sdeps", "chan", "chan_val", "flag", "seq", "dma_waits", "cost", "lat", "barrier", "pos")


class Sched:
    BANKS = {"pj0": 0, "pj1": 1, "ps2b": 2}
    BANKS.update({"ps%d" % i: i for i in range(8)})
    WINDOW = {"pe": 200, "act": 72, "dve": 72, "pool": 1, "sp": 1}

    def __init__(self, nc):
        self.nc = nc
        self.ops = []
        self.per_eng = {e: [] for e in ENGS}
        self.last_writer = {}
        self.readers = {}
        self.chan_count = {}
        self.last_dma = {}
        self.last_access = {}
        self.region_start = 0
        self.regions = []
        self.reorder = True

    def op(self, eng, fn, reads=(), writes=(), chan=None, cost=0.3, lat=0.0):
        o = _Op()
        o.idx = len(self.ops)
        o.eng = eng
        o.fn = fn
        o.chan = chan
        o.chan_val = None
        o.flag = False
        o.seq = None
        o.cost = cost
        o.lat = lat
        o.barrier = False
        deps = set()
        for t in reads:
            w = self.last_writer.get(t)
            if w is not None:
                deps.add(w)
        for t in writes:
            w = self.last_writer.get(t)
            if w is not None:
                deps.add(w)
            deps.update(self.readers.get(t, ()))
        for t in list(reads) + list(writes):
            b = self.BANKS.get(t)
            if b is not None:
                p = self.last_access.get(b)
                if p is not None:
                    deps.add(p)
        for t in list(reads) + list(writes):
            b = self.BANKS.get(t)
            if b is not None:
                self.last_access[b] = o.idx
        deps.discard(o.idx)
        o.sdeps = set()
        o.dma_waits = []
        for d in deps:
            od = self.ops[d]
            if od.chan is not None:
                ld = self.last_dma[od.chan]
                o.sdeps.add(ld)
                o.dma_waits.append((od.chan, self.chan_count[od.chan]))
            else:
                o.sdeps.add(d)
        if chan is not None:
            self.chan_count[chan] = self.chan_count.get(chan, 0) + 16
            o.chan_val = self.chan_count[chan]
            self.last_dma[chan] = o.idx
        for t in reads:
            self.readers.setdefault(t, []).append(o.idx)
        for t in writes:
            self.last_writer[t] = o.idx
            self.readers[t] = []
        self.ops.append(o)
        self.per_eng[eng].append(o)
        return o

    def barrier(self):
        start, end = self.region_start, len(self.ops)
        self.regions.append((start, end, self.reorder))
        chans = [(c, v) for c, v in self.chan_count.items()]
        prev = [i for i in range(start, end) if self.ops[i].chan is None]
        for e in ENGS:
            o = _Op()
            o.idx = len(self.ops)
            o.eng = e
            o.fn = (lambda en: en.nop())
            o.chan = None
            o.chan_val = None
            o.flag = False
            o.seq = None
            o.cost = 0.05
            o.lat = 0.0
            o.barrier = True
            o.sdeps = set(prev)
            o.dma_waits = list(chans)
            self.ops.append(o)
            self.per_eng[e].append(o)
        self.region_start = len(self.ops)
        self.last_writer = {}
        self.readers = {}
        self.last_access = {}

    def _schedule_region(self, start, end, tstart, reorder):
        ops = self.ops
        queues = {e: [o for o in ops[start:end] if o.eng == e] for e in ENGS}
        head = {e: 0 for e in ENGS}
        done = {}
        fin = {}
        etime = {e: tstart for e in ENGS}
        out = {e: [] for e in ENGS}
        nleft = end - start
        indeg = {}
        for o in ops[start:end]:
            indeg[o.idx] = sum(1 for d in o.sdeps if d >= start)
        users = {}
        for o in ops[start:end]:
            for d in o.sdeps:
                if d >= start:
                    users.setdefault(d, []).append(o.idx)
        while nleft:
            best = None
            for e in ENGS:
                q = queues[e]
                h = head[e]
                n = len(q)
                while h < n and q[h].idx in done:
                    h += 1
                head[e] = h
                cnt = 0
                i = h
                win = self.WINDOW[e] if reorder else 1
                while i < n and cnt < win:
                    o = q[i]
                    i += 1
                    if o.idx in done:
                        continue
                    cnt += 1
                    if indeg[o.idx]:
                        continue
                    est = etime[e]
                    for d in o.sdeps:
                        if d >= start:
                            f = fin[d] + (0.12 if ops[d].eng != e else 0.0)
                            if f > est:
                                est = f
                    key = (est, o.idx)
                    if best is None or key < best[0]:
                        best = (key, o, e)
            assert best is not None, "scheduler stuck"
            (est, _), o, e = best
            done[o.idx] = True
            if o.chan is not None:
                etime[e] = est + o.cost
                fin[o.idx] = est + o.cost + o.lat
            else:
                etime[e] = est + o.cost
                fin[o.idx] = etime[e]
            out[e].append(o)
            for u in users.get(o.idx, ()):
                indeg[u] -= 1
            nleft -= 1
        return out, max(etime.values())

    def schedule(self):
        if self.region_start < len(self.ops):
            self.regions.append((self.region_start, len(self.ops), self.reorder))
            self.region_start = len(self.ops)
        new = {e: [] for e in ENGS}
        t = 0.0
        ri = 0
        i = 0
        n = len(self.ops)
        for (start, end, reorder) in self.regions:
            out, t = self._schedule_region(start, end, t, reorder)
            for e in ENGS:
                new[e].extend(out[e])
            j = end
            while j < n and self.ops[j].barrier:
                new[self.ops[j].eng].append(self.ops[j])
                j += 1
        self.per_eng = new
        self.est_total = t

    def emit(self, st):
        nc = self.nc
        self.schedule()
        ops = self.ops
        for e in ENGS:
            for p, o in enumerate(self.per_eng[e]):
                o.pos = p
        need = {}
        for o in ops:
            latest = {}
            for d in o.sdeps:
                od = ops[d]
                if od.chan is not None:
                    continue
                if od.eng == o.eng and o.eng == "pe":
                    continue
                if latest.get(od.eng, -1) < od.pos:
                    latest[od.eng] = od.pos
            need[o.idx] = [self.per_eng[e][p] for e, p in latest.items()]
            for od in need[o.idx]:
                od.flag = True
        for e in ENGS:
            c = 0
            for o in self.per_eng[e]:
                if o.chan is None and o.flag:
                    c += 1
                    o.seq = c
        esem = {e: st.enter_context(nc.semaphore("s_" + e)) for e in ENGS}
        csem = {c: st.enter_context(nc.semaphore("c_" + str(c))) for c in self.chan_count}
        block = st.enter_context(nc.Block())

        def run(ename, e):
            known = {}
            for o in self.per_eng[ename]:
                waits = {}
                for od in need[o.idx]:
                    s = esem[od.eng]
                    v = od.seq
                    if known.get(s.name, 0) >= v:
                        continue
                    if waits.get(s.name, (None, 0))[1] < v:
                        waits[s.name] = (s, v)
                for (c, v) in o.dma_waits:
                    s = csem[c]
                    if known.get(s.name, 0) >= v:
                        continue
                    if waits.get(s.name, (None, 0))[1] < v:
                        waits[s.name] = (s, v)
                wl = list(waits.values())
                for (s, v) in wl:
                    known[s.name] = v
                embed = None
                if wl and o.chan is None:
                    embed = wl.pop()
                for (s, v) in wl:
                    e.wait_ge(s, v)
                ins = o.fn(e)
                if embed is not None:
                    ins._wait_ge(embed[0], embed[1])
                if o.chan is not None:
                    ins.then_inc(csem[o.chan], 16)
                elif o.flag:
                    ins.then_inc(esem[ename], 1)
            if ename == "sp":
                for c, s in csem.items():
                    if known.get(s.name, 0) < self.chan_count[c]:
                        e.wait_ge(s, self.chan_count[c])

        @block.tensor
        def _(e):
            run("pe", e)

        @block.scalar
        def _(e):
            run("act", e)

        @block.vector
        def _(e):
            run("dve", e)

        @block.gpsimd
        def _(e):
            run("pool", e)

        @block.sync
        def _(e):
            run("sp", e)


IN_SPECS = [
    ("xT", [128, 8 * NT]), ("cT", [128, 8 * 17]), ("cosT", [128, NT]), ("sinT", [128, NT]),
    ("masks", [128, 5 * 128]), ("mcore", [128, 1]),
    ("w_ada0", [2 * 128, 8 * 1536]), ("w_ada0b", [24 * 128, 8 * 128]), ("w_ada1", [48 * 128, 8 * 128]), ("b_adaT", [128, 2 * 48]), ("g_attnT", [128, 16]), ("g_ffnT", [128, 16]),
    ("gqk", [128, 4]), ("w_in", [2 * 128, 8 * 1920]), ("w_out", [2 * 128, 8 * 1024]),
    ("w_ffn_in", [2 * 22 * 128, 8 * 256]), ("w_ffn_out", [2 * 8 * 128, 22 * 128]),
    ("PT0", [128, 128]), ("OB", [128, 128]), ("sinkT", [128, 8]), ("lngb", [128, 2 * 2 * 512]),
    ("wsT", [128, 2 * 4 * 128]), ("wbd", [128, 2 * 4 * 128]), ("bsrow", [1, 2 * 4 * 128]), ("bsrow_s", [1, 2 * 4 * 128]),
    ("conv_wT", [128, 2 * 3 * 44]), ("conv_bT", [128, 2 * 44]),
    ("kTc", [2 * 128, 2 * 2048]), ("vc", [2 * 128, 16 * 128]), ("ccT", [2 * 128, 44 * 32]),
]
OUT_SPECS = [
    ("yT", [128, 8 * 2048]), ("ysT", [128, 8 * 128]),
    ("kpT", [2 * 128, 256]), ("vp", [2 * 128, 128]), ("convp", [2 * 128, 88]),
    ("ksT", [2 * 128, 256]), ("vs", [2 * 128, 128]), ("gvs", [2 * 128, 512]), ("convs", [2 * 128, 44 * 32]),
]


def build_nc():
    nc = bass.Bass("TRN2", target_bir_lowering=False)
    I = {n: nc.dram_tensor(n, s, F32, kind="ExternalInput").ap() for n, s in IN_SPECS}
    O = {n: nc.dram_tensor(n, s, F32, kind="ExternalOutput").ap() for n, s in OUT_SPECS}
    st = contextlib.ExitStack()
    with st:
        S = Sched(nc)

        def sb(name, shape, dt=F32):
            return st.enter_context(nc.sbuf_tensor("sb_" + name, shape, dt))

        XT = sb("XT", [128, 8, NT])
        MOD = [sb("MOD%d" % l, [128, 48, 17]) for l in range(2)]
        WA = [sb("WA%d" % i, [128, 8, 128], BF16) for i in range(2)]
        scT = sb("scT", [128, 8, 17], BF16)
        cst = sb("cst", [128, 8 * 17])
        ones = sb("ones", [128, 128], BF16)
        OBb = sb("OBb", [128, 128], BF16)
        PMt = sb("PMt", [128, 4, 128], BF16)
        msk = sb("msk", [128, 5, 128], BF16)
        mcore = sb("mcore", [128, 1])
        small = sb("small", [128, 2 * 48 + 16 + 16 + 4 + 8 + 2 * 3 * 44 + 2 * 44 + 128])
        o0 = 0
        b_adaT = small[:, o0:o0 + 96].rearrange("p (l c) -> p l c", l=2); o0 += 96
        g_attnT = small[:, o0:o0 + 16].rearrange("p (l c) -> p l c", l=2); o0 += 16
        g_ffnT = small[:, o0:o0 + 16].rearrange("p (l c) -> p l c", l=2); o0 += 16
        gqk = small[:, o0:o0 + 4]; o0 += 4
        sinkE = small[:, o0:o0 + 8].rearrange("p (l c) -> p l c", l=2); o0 += 8
        conv_wT = small[:, o0:o0 + 264].rearrange("p (l j c) -> p l j c", l=2, j=3); o0 += 264
        conv_bT = small[:, o0:o0 + 88].rearrange("p (l c) -> p l c", l=2); o0 += 88
        gq8 = small[:, o0:o0 + 2]; o0 += 2
        tmpc = sb("tmpc", [128, 128])
        onesrow = sb("onesrow", [1, 128], BF16)
        bsr = sb("bsr", [1, 2, 4 * 128], BF16)

        OVB = (int(nc.sbuf_bytes_remaining) // 32) * 32 - 64
        OV = sb("OV", [128, OVB], mybir.dt.uint8)

        class Carver:
            def __init__(self):
                self.off = 0

            def raw(self, nbytes):
                nb = (nbytes + 31) // 32 * 32
                assert self.off + nb <= OVB, ("overlay overflow", self.off + nb, OVB)
                o = self.off
                self.off += nb
                return o

            def view(self, off, shape, dt):
                esz = 2 if dt == BF16 else 4
                n = 1
                for s_ in shape:
                    n *= s_
                ap = OV[:, off:off + n * esz].bitcast(dt)
                if len(shape) == 1:
                    return ap
                names = " ".join("d%d" % i for i in range(len(shape)))
                kw = {"d%d" % i: shape[i] for i in range(len(shape))}
                return ap.rearrange("p (%s) -> p %s" % (names, names), **kw)

            def take(self, shape, dt):
                esz = 2 if dt == BF16 else 4
                n = 1
                for s_ in shape:
                    n *= s_
                return self.view(self.raw(n * esz), shape, dt)

        cm = Carver()
        WAb = [cm.take([8, 1536], BF16) for _ in range(2)]
        ca = Carver()
        WIN = ca.take([8, 1920], BF16)
        WOUT = ca.take([8, 1024], BF16)
        hA = ca.take([8, 256], BF16)
        NSL = 6
        qT = [ca.take([2, 4, 128], BF16)]
        uT = [ca.take([4, 256], BF16)]
        mixT = ca.take([8, 256], BF16)
        qsq = [ca.take([256], BF16) for _ in range(2)]
        qraw = [ca.take([256], BF16) for _ in range(2)]
        t1 = [ca.take([256], F32)] * 2
        t2 = [ca.take([256], F32)] * 2
        rsq = [ca.take([256], F32) for _ in range(2)]
        lnq = rsq
        rstdA = ca.take([256], F32)
        lnA = rstdA
        tmpA = [ca.take([256], F32) for _ in range(2)]
        kf32 = ca.take([2, 256], F32)
        kTr = ca.take([2, NSL, 128], BF16)
        vr = ca.take([NSL, 128], BF16)
        vf32 = ca.take([128], F32)
        gzz = [ca.take([2, 512], F32)]
        wstmp = gzz[0]
        vgf = ca.take([512], F32)
        odc = vgf
        vgb = [ca.take([512], BF16) for _ in range(2)]
        lnst = ca.take([32], F32)
        PTb = [ca.take([2, 512], BF16) for _ in range(2)]
        dsm = [ca.take([256], F32) for _ in range(2)]
        tabc = ca.take([256], F32)
        tabs = ca.take([256], F32)
        lng = ca.take([2, 512], F32)
        WsT = ca.take([2, 4, 128], BF16)
        aloff = ca.raw(12288)
        kTcS = ca.view(aloff, [2, 2048], BF16)
        vcS = ca.view(aloff + 8192, [16, 128], BF16)
        qT.append(ca.view(aloff, [2, 4, 128], BF16))
        uT.append(ca.view(aloff + 2048, [4, 256], BF16))
        gzz.append(ca.view(aloff + 4096, [2, 512], F32))
        xsqA = ca.view(aloff + 8192, [8, 256], BF16)
        cb = Carver()
        actT = cb.take([NF, 510], BF16)
        NWG, NWD = 6, 3
        WG = [cb.take([8, 256], BF16) for _ in range(NWG)]
        WD = [cb.take([NF, 128], BF16) for _ in range(NWD)]
        hF1 = cb.take([8, 640], BF16)
        hsave = cb.take([8, 2], BF16)
        lnF = cb.take([510], F32)
        rstdF = cb.take([510], F32)
        tmpF = [cb.take([510], F32) for _ in range(2)]
        ccoff = cb.raw(4 * 2560)
        cgt = [cb.view(ccoff + (2 * i) * 2560, [640], F32) for i in range(2)]
        cut = [cb.view(ccoff + (2 * i + 1) * 2560, [640], F32) for i in range(2)]
        xsqF = cb.view(ccoff, [8, 510], BF16)
        CCT = ["c0_0", "c1_0", "c0_1", "c1_1"]
        ext = [cb.take([16, 10], F32) for _ in range(2)]
        ccS = cb.take([44, 32], F32)
        cvs = cb.take([44, 32], F32)
        cvp = cb.take([44, 2], F32)

        PS = [st.enter_context(nc.psum_tensor("ps%d" % i, [128, 512], F32)) for i in range(8)]

        def fsz(ap):
            n = 1
            for d_ in tuple(ap.shape)[1:]:
                n *= int(d_)
            return n

        def dma(eng, out, in_, reads=(), writes=(), chan=None):
            nbytes = fsz(in_) * int(tuple(in_.shape)[0]) * 4
            S.op(eng, lambda e: e.dma_start(out=out, in_=in_), reads, writes, chan=chan,
                 cost=(1.0 if eng == "pool" else 0.15), lat=2.0 + nbytes / 2.5e5)

        def mm(out, lhsT, rhs, start, stop, reads, writes):
            S.op("pe", lambda e: e.matmul(out, lhsT, rhs, start=start, stop=stop), reads, writes, cost=0.035 + fsz(rhs) * 0.0006)

        def act(out, in_, func, reads, writes, bias=None, scale=None):
            kw = {}
            if bias is not None:
                kw["bias"] = bias
            if scale is not None:
                kw["scale"] = scale
            S.op("act", lambda e: e.activation(out, in_, func, **kw), reads, writes, cost=0.22 + fsz(out) * 0.00085)

        def tt(out, in0, in1, op, reads, writes, eng="dve"):
            S.op(eng, lambda e: e.tensor_tensor(out, in0, in1, op), reads, writes, cost=0.2 + fsz(out) * 0.00105)

        def ts(out, in0, s1, s2, op0, op1, reads, writes, eng="dve"):
            if op1 is None:
                S.op(eng, lambda e: e.tensor_scalar(out, in0, s1, s2, op0), reads, writes, cost=0.2 + fsz(out) * 0.00105)
            else:
                S.op(eng, lambda e: e.tensor_scalar(out, in0, s1, s2, op0, op1), reads, writes, cost=0.2 + fsz(out) * 0.00105)

        def stt(out, in0, scalar, in1, op0, op1, reads, writes, eng="dve"):
            S.op(eng, lambda e: e.scalar_tensor_tensor(out, in0, scalar, in1, op0, op1), reads, writes, cost=0.2 + fsz(out) * 0.00105)

        def cp(out, in_, reads, writes, eng="dve"):
            S.op(eng, lambda e: e.tensor_copy(out, in_), reads, writes, cost=0.2 + fsz(out) * 0.00105)

        def memset(ap, val, writes, eng="dve"):
            S.op(eng, lambda e: e.memset(ap, val), (), writes)

        def rsum(out, in_, reads, writes):
            S.op("dve", lambda e: e.reduce_sum(out, in_, AX.X), reads, writes, cost=0.2 + fsz(in_) * 0.00105)

        def rsqrt_act(out, lnbuf, in_, scale, reads, writes, lntok):
            if lnbuf is out or lntok is None:
                act(out, in_, AF.Ln, reads, writes, bias=EPS, scale=scale)
                act(out, out, AF.Exp, writes, writes, scale=-0.5)
            else:
                act(lnbuf, in_, AF.Ln, reads, [lntok], bias=EPS, scale=scale)
                act(out, lnbuf, AF.Exp, [lntok], writes, scale=-0.5)

        XSPLIT = 640
        dma("sp", cst[:], I["cT"][:, :], (), ["cst"], chan="cst")
        dma("sp", mcore[:], I["mcore"][:, :], (), ["mcore"], chan="cin")
        so = 0
        for nm, n in (("b_adaT", 96), ("g_attnT", 16), ("g_ffnT", 16), ("gqk", 4), ("sinkT", 8), ("conv_wT", 264), ("conv_bT", 88)):
            dma("sp", small[:, so:so + n], I[nm][:, :], (), ["small"], chan="cin")
            so += n
        dma("pool", msk[:, :, :], I["masks"].rearrange("p (m n) -> p m n", m=5), (), ["msk"], chan="cin2")
        dma("pool", OBb[:], I["OB"][:, :], (), ["OBb"], chan="cin2")
        dma("sp", tmpc[:], I["PT0"][:, :], (), ["tmpc"], chan="cin")
        for k in range(8):
            dma("sp", XT[:, k, 0:XSPLIT], I["xT"][:, k * NT:k * NT + XSPLIT], (), ["XTall"], chan="xin")
        memset(ones[:], 1.0, ["ones"])
        memset(onesrow[:], 1.0, ["onesrow"])
        act(scT[:, :, :], cst[:].rearrange("p (k n) -> p k n", k=8), AF.Silu, ["cst"], ["scT"])
        act(small[:, 132:140], small[:, 132:140], AF.Exp, ["small"], ["small"])
        for l in range(2):
            ts(gq8[:, l:l + 1], gqk[:, 2 * l:2 * l + 1], 0.125, None, ALU.mult, None, ["small"], ["small"])
            ts(PMt[:, 2 * l, :], tmpc[:], gq8[:, l:l + 1], None, ALU.mult, None, ["tmpc", "small"], ["PMt"])
            ts(PMt[:, 2 * l + 1, :], tmpc[:], gqk[:, 2 * l + 1:2 * l + 2], None, ALU.mult, None, ["tmpc", "small"], ["PMt"])


        def mod_finish(l, attn=True, ffn=True, ftok=None):
            for k in range(8):
                if attn:
                    ts(MOD[l][:, 8 + k, :], MOD[l][:, 8 + k, :], 1.0, g_attnT[:, l, k:k + 1], ALU.add, ALU.mult, ["MOD%d" % l, "small"], ["MOD%d" % l])
                if ffn:
                    tk = ftok or ("MOD%d" % l)
                    ts(MOD[l][:, 32 + k, :], MOD[l][:, 32 + k, :], 1.0, g_ffnT[:, l, k:k + 1], ALU.add, ALU.mult, [tk, "small"], [tk])

        for pc in range(2):
            slot = pc % 2
            dma("pool", WAb[slot][:, :, :], I["w_ada0"][pc * 128:(pc + 1) * 128, :].rearrange("p (k c) -> p k c", k=8), (), ["WAb%d" % slot], chan="WAb%d" % slot)
            for cc in range(12):
                for k in range(8):
                    mm(PS[7][:, cc * 17:cc * 17 + 17], WAb[slot][:, k, cc * 128:cc * 128 + 128], scT[:, k, :], k == 0, k == 7,
                       ["WAb%d" % slot, "scT"], ["ps7"])
            tt(MOD[0][:, pc * 12:pc * 12 + 12, :], PS[7][:, 0:204].rearrange("p (c n) -> p c n", n=17),
               b_adaT[:, 0, pc * 12:pc * 12 + 12].rearrange("p (c o) -> p c o", o=1).to_broadcast([128, 12, 17]),
               ALU.add, ["ps7", "small"], ["MOD0"])
        mod_finish(0, attn=True, ffn=False)
        S.barrier()

        def mod0_ffn_half():
            for ch in range(24, 48):
                slot = ch % 2
                dma("pool", WA[slot][:, :, :], I["w_ada0b"][(ch - 24) * 128:(ch - 23) * 128, :].rearrange("p (k c) -> p k c", k=8), (),
                    ["WA%d" % slot], chan="WA%d" % slot)
                for k in range(8):
                    mm(PS[7][:, 0:17], WA[slot][:, k, :], scT[:, k, :], k == 0, k == 7, ["WA%d" % slot, "scT"], ["ps7"])
                ts(MOD[0][:, ch, :], PS[7][:, 0:17], b_adaT[:, 0, ch:ch + 1], None, ALU.add, None, ["ps7", "small"], ["MOD0f"])
            mod_finish(0, attn=False, ffn=True, ftok="MOD0f")

        modctr = [0]

        def mod1_piece():
            ch = modctr[0]
            if ch >= 48:
                return
            modctr[0] += 1
            slot = ch % 2
            dma("pool", WA[slot][:, :, :], I["w_ada1"][ch * 128:(ch + 1) * 128, :].rearrange("p (k c) -> p k c", k=8), (), ["WA%d" % slot], chan="WA%d" % slot)
            for k in range(8):
                mm(PS[7][:, 0:17], WA[slot][:, k, :], scT[:, k, :], k == 0, k == 7, ["WA%d" % slot, "scT"], ["ps7"])
            ts(MOD[1][:, ch, :], PS[7][:, 0:17], b_adaT[:, 1, ch:ch + 1], None, ALU.add, None, ["ps7", "small"], ["MOD1"])

        def norm_mod(l, xtok, t0, W, samp, base, hdst, hoff, htok, xsq, xsqtoks, lnb, rstd, tmp2, psb, pstok, mid=None, sdst=None):
            act(xsq[:, :, 0:W], XT[:, :, t0:t0 + W], AF.Square, [xtok], xsqtoks)
            if mid is not None:
                mid()
            for k in range(8):
                mm(psb[:, 0:W], ones[:], xsq[:, k, 0:W], k == 0, k == 7, ["ones"] + xsqtoks, [pstok])
            if lnb is rstd:
                rsqrt_act(rstd[:, 0:W], None, psb[:, 0:W], 1.0 / D, [pstok], ["rstd"], None)
            else:
                rsqrt_act(rstd[:, 0:W], lnb[:, 0:W], psb[:, 0:W], 1.0 / D, [pstok], ["rstd"], "lnn")
            for k in range(8):
                tmp = tmp2[k % 2]
                tk = "tmpn%d" % (k % 2)
                if not samp:
                    stt(tmp[:, 0:W], XT[:, k, t0:t0 + W], MOD[l][:, base + 8 + k, 0:1], rstd[:, 0:W], ALU.mult, ALU.mult,
                        [xtok, "rstd", "MOD%d" % l], [tk])
                    act(hdst[:, k, hoff:hoff + W], tmp[:, 0:W], AF.Identity, [tk, "MOD%d" % l], [htok], bias=MOD[l][:, base + k, 0:1])
                else:
                    tt(tmp[:, 0:W], XT[:, k, t0:t0 + W], rstd[:, 0:W], ALU.mult, [xtok, "rstd"], [tk])
                    t3 = tmp[:, 0:W].rearrange("p (s t) -> p s t", t=8)
                    tt(t3, t3, MOD[l][:, base + 8 + k, 1:17].rearrange("p (s o) -> p s o", o=1).to_broadcast([128, 16, 8]), ALU.mult,
                       [tk, "MOD%d" % l], [tk])
                    dst3 = sdst(k) if sdst is not None else hdst[:, k, hoff:hoff + W].rearrange("p (s t) -> p s t", t=8)
                    tt(dst3, t3,
                       MOD[l][:, base + k, 1:17].rearrange("p (s o) -> p s o", o=1).to_broadcast([128, 16, 8]), ALU.add,
                       [tk, "MOD%d" % l], [htok])

        def xupdate(l, xtok, m, gbase, psap, t0, W, samp, pstok, tmpb):
            if not samp:
                stt(XT[:, m, t0:t0 + W], psap, MOD[l][:, gbase + m, 0:1], XT[:, m, t0:t0 + W], ALU.mult, ALU.add,
                    [pstok, xtok, "MOD%d" % l], [xtok])
            else:
                ps3 = psap if len(tuple(psap.shape)) == 3 else psap.rearrange("p (s t) -> p s t", t=8)
                tt(tmpb[:, 0:W].rearrange("p (s t) -> p s t", t=8), ps3,
                   MOD[l][:, gbase + m, 1:17].rearrange("p (s o) -> p s o", o=1).to_broadcast([128, 16, 8]), ALU.mult,
                   [pstok, "MOD%d" % l], ["tmpx"])
                tt(XT[:, m, t0:t0 + W], XT[:, m, t0:t0 + W], tmpb[:, 0:W], ALU.add, ["tmpx", xtok], [xtok])

        pjctr = [0]
        chctr = [0]

        def attn_front(l, blocks, samp, kvonly, outs, bs, xsq, xsqtoks):
            nb = len(blocks)
            W = 128 * nb
            t0 = blocks[0] * 128
            xtok = "XTs" if samp else "XTa%d" % blocks[0]
            qTb, uTb, gzb = qT[bs], uT[bs], gzz[bs]
            sfb = "_b%d" % bs
            dma("sp", tabc[:, 0:W], I["cosT"][:, t0:t0 + W], (), ["tabc"], chan="tabc")
            dma("sp", tabs[:, 0:W], I["sinT"][:, t0:t0 + W], (), ["tabs"], chan="tabs")
            norm_mod(l, xtok, t0, W, samp, 0, hA, 0, "h", xsq, xsqtoks, lnA, rstdA, tmpA, PS[2], "ps2")
            yield

            def proj_fm(c0):
                par = pjctr[0] % 2
                pjctr[0] += 1
                pj = PS[par][:, 0:W]
                for k in range(8):
                    mm(pj, WIN[:, k, c0:c0 + 128], hA[:, k, 0:W], k == 0, k == 7, ["WIN", "h"], ["pj%d" % par])
                return pj, "pj%d" % par

            chunks = ([] if kvonly else [("q", c) for c in range(4)]) + [("k", 0), ("k", 1)]
            for (kind, c) in chunks:
                cp_ = chctr[0] % 2
                chctr[0] += 1
                sfx = "_%d" % cp_
                c0 = c * 128 if kind == "q" else 512 + c * 128
                pj, ptok = proj_fm(c0)
                spb = PS[2 + cp_]
                sptok = "ps%d" % (2 + cp_)
                act(qsq[cp_][:, 0:W], pj, AF.Square, [ptok], ["qsq" + sfx])
                act(qraw[cp_][:, 0:W], pj, AF.Copy, [ptok], ["qraw" + sfx])
                mm(spb[:, 0:W], OBb[:], qsq[cp_][:, 0:W], True, True, ["OBb", "qsq" + sfx], [sptok])
                mm(spb[:, 256:256 + W], PMt[:, 2 * l + (0 if kind == "q" else 1), :], qraw[cp_][:, 0:W], True, True, ["PMt", "qraw" + sfx], [sptok])
                rsqrt_act(rsq[cp_][:, 0:W], None, spb[:, 0:W], 1.0 / 64, [sptok], ["rsq" + sfx], None)
                gcol = gq8[:, l:l + 1] if kind == "q" else gqk[:, 2 * l + 1:2 * l + 2]
                stt(t1[cp_][:, 0:W], pj, gcol, tabc[:, 0:W], ALU.mult, ALU.mult, [ptok, "small", "tabc"], ["t1"])
                tt(t2[cp_][:, 0:W], spb[:, 256:256 + W], tabs[:, 0:W], ALU.mult, [sptok, "tabs"], ["t2"])
                tt(t1[cp_][:, 0:W], t1[cp_][:, 0:W], t2[cp_][:, 0:W], ALU.add, ["t1", "t2"], ["t1"])
                if kind == "q":
                    tt(qTb[:, 0:nb, c, :], t1[cp_][:, 0:W].rearrange("p (b q) -> p b q", q=128),
                       rsq[cp_][:, 0:W].rearrange("p (b q) -> p b q", q=128), ALU.mult, ["t1", "rsq" + sfx], ["qT" + sfb])
                else:
                    tt(kf32[:, c, 0:W], t1[cp_][:, 0:W], rsq[cp_][:, 0:W], ALU.mult, ["t1", "rsq" + sfx], ["kf32_%d" % c])
                    for bi, b in enumerate(blocks):
                        act(kTr[:, c, b % NSL, :], kf32[:, c, bi * 128:bi * 128 + 128], AF.Copy, ["kf32_%d" % c], ["kT%d" % (b % NSL)])
                yield
            if "k" in outs:
                bi = outs["k"][1]
                dma("sp", outs["k"][0], kf32[:, :, bi * 128:bi * 128 + 128], ["kf32_0", "kf32_1"], (), chan="okv")
            for bi, b in enumerate(blocks):
                sl = b % NSL
                par = pjctr[0] % 2
                pjctr[0] += 1
                pv = PS[par][:, 0:128]
                pvt = "pj%d" % par
                for k in range(8):
                    mm(pv, hA[:, k, bi * 128:bi * 128 + 128], WIN[:, k, 768:896], k == 0, k == 7, ["h", "WIN"], [pvt])
                act(vr[:, sl, :], pv, AF.Copy, [pvt], ["v%d" % sl])
                if "v" in outs and outs["v"][1] == bi:
                    cp(vf32[:], pv, [pvt], ["vf32"])
                    dma("sp", outs["v"][0], vf32[:], ["vf32"], (), chan="okv")
            yield
            if kvonly:
                return
            for c in range(4):
                pj, ptok = proj_fm(896 + c * 128)
                act(uTb[:, c, 0:W], pj, AF.Gelu_apprx_tanh, [ptok], ["uT" + sfb])
                if c % 2 == 1:
                    yield
            for bi, b in enumerate(blocks):
                zb = PS[2 + bi]
                zt = "ps%d" % (2 + bi)
                for k in range(8):
                    mm(zb[:, :], hA[:, k, bi * 128:bi * 128 + 128], WIN[:, k, 1408:1920], k == 0, k == 7, ["h", "WIN"], [zt])
                act(gzb[:, bi, :], zb[:, :], AF.Gelu_apprx_tanh, [zt], ["gz%d" % bi + sfb])
                yield

        def attn_back(l, blocks, samp, outs, bs):
            nb = len(blocks)
            W = 128 * nb
            t0 = blocks[0] * 128
            xtok = "XTs" if samp else "XTa%d" % blocks[0]
            qTb, uTb, gzb = qT[bs], uT[bs], gzz[bs]
            sfb = "_b%d" % bs
            for bi, b in enumerate(blocks):
                gz = gzb[:, bi, :]
                gt = "gz%d" % bi + sfb
                rsum(lnst[:, 0:4], gz.rearrange("p (g w) -> p g w", g=4), [gt], ["ln_s1"])
                tt(vgf[:], gz, gz, ALU.mult, [gt], ["vgf"])
                rsum(lnst[:, 4:8], vgf[:].rearrange("p (g w) -> p g w", g=4), ["vgf"], ["ln_s2"])
                ts(lnst[:, 8:12], lnst[:, 0:4], 1.0 / 128, None, ALU.mult, None, ["ln_s1"], ["ln_mean"])
                tt(lnst[:, 12:16], lnst[:, 8:12], lnst[:, 8:12], ALU.mult, ["ln_mean"], ["ln_m2"])
                stt(lnst[:, 16:20], lnst[:, 4:8], 1.0 / 128, lnst[:, 12:16], ALU.mult, ALU.subtract, ["ln_s2", "ln_m2"], ["ln_var"])
                rsqrt_act(lnst[:, 24:28], lnst[:, 20:24], lnst[:, 16:20], 1.0, ["ln_var"], ["ln_rstd"], "ln_ln")
                for g in range(4):
                    ts(gz[:, g * 128:g * 128 + 128], gz[:, g * 128:g * 128 + 128], lnst[:, 8 + g:9 + g], lnst[:, 24 + g:25 + g],
                       ALU.subtract, ALU.mult, [gt, "ln_mean", "ln_rstd"], [gt])
                tt(gz, gz, lng[:, 0, :], ALU.mult, [gt, "lng"], [gt])
                if "gv" in outs:
                    tt(vgf[:], gz, lng[:, 1, :], ALU.add, [gt, "lng"], ["vgf"])
                    act(vgb[bi][:], vgf[:], AF.Copy, ["vgf"], ["vgb%d" % bi])
                    dma("sp", outs["gv"], vgf[:], ["vgf"], (), chan="okv")
                else:
                    tt(vgb[bi][:], gz, lng[:, 1, :], ALU.add, [gt, "lng"], ["vgb%d" % bi])
                yield
            for bi, b in enumerate(blocks):
                sl = b % NSL
                kbs = ([] if samp else [(0, (b - 1) % NSL, 2 if b == 3 else 1)]) + [(1, sl, 3 if samp else 0)]
                kb0 = kbs[0][0]

                def scores(kv):
                    for (kb, ksl, mi) in kbs:
                        for p in range(2):
                            mm(PS[4 + p][:, kb * 256:kb * 256 + 256], kTr[64 * p:64 * p + 64, kv, ksl, :],
                               qTb[64 * p:64 * p + 64, bi, 2 * kv:2 * kv + 2, :], True, True, ["kT%d" % ksl, "qT" + sfb], ["ps%d" % (4 + p)])

                def expo(kv):
                    for p in range(2):
                        act(PTb[kv][:, kb0:2, p * 256:p * 256 + 256], PS[4 + p][:, kb0 * 256:512].rearrange("k (b n) -> k b n", n=256), AF.Exp,
                            ["ps%d" % (4 + p)], ["PT%d_%d" % (kb_, kv) for kb_ in range(kb0, 2)])

                def maskm(kv):
                    for (kb, ksl, mi) in kbs:
                        tt(PTb[kv][:, kb, :].rearrange("k (h q) -> k h q", h=4), PTb[kv][:, kb, :].rearrange("k (h q) -> k h q", h=4),
                           msk[:, mi:mi + 1, :].to_broadcast([128, 4, 128]), ALU.mult, ["PT%d_%d" % (kb, kv), "msk"], ["PT%d_%d" % (kb, kv)])

                def pv(kv):
                    odb = PS[6 + kv]
                    odt = "ps%d" % (6 + kv)
                    for p in range(2):
                        for i_, (kb, ksl, mi) in enumerate(kbs):
                            mm(odb[64 * p:64 * p + 64, 0:256], vr[:, ksl, 64 * kv:64 * kv + 64], PTb[kv][:, kb, p * 256:p * 256 + 256],
                               i_ == 0, i_ == len(kbs) - 1, ["v%d" % ksl, "PT%d_%d" % (kb, kv)], [odt])
                        for i_, (kb, ksl, mi) in enumerate(kbs):
                            mm(odb[64 * p:64 * p + 64, 256:512], ones[:, 0:64], PTb[kv][:, kb, p * 256:p * 256 + 256],
                               i_ == 0, i_ == len(kbs) - 1, ["ones", "PT%d_%d" % (kb, kv)], [odt])

                def cache_part(kv):
                    odb = PS[6 + kv]
                    odt = "ps%d" % (6 + kv)
                    pt0 = "PT0_%d" % kv
                    for sq in range(16):
                        for p in range(2):
                            for cc in range(2):
                                mm(PS[4 + p][:, sq * 16 + cc * 8:sq * 16 + cc * 8 + 8],
                                   kTcS[64 * p:64 * p + 64, kv, sq * 128:sq * 128 + 128],
                                   qTb[64 * p:64 * p + 64, 0, 2 * kv + cc, sq * 8:sq * 8 + 8], True, True, ["kTcS", "qT" + sfb], ["ps%d" % (4 + p)])
                    for p in range(2):
                        act(PTb[kv][:, 0, p * 256:p * 256 + 256], PS[4 + p][:, 0:256], AF.Exp, ["ps%d" % (4 + p)], [pt0])
                    ocb = PS[7 - kv]
                    oct_ = "ps%d" % (7 - kv)
                    tt(PTb[kv][:, 0, :].rearrange("k (h t) -> k h t", t=8), PTb[kv][:, 0, :].rearrange("k (h t) -> k h t", t=8),
                       msk[:, 4:5, 0:8].to_broadcast([128, 64, 8]), ALU.mult, [pt0, "msk"], [pt0])
                    for sq in range(16):
                        for p in range(2):
                            for cc in range(2):
                                rhs = PTb[kv][:, 0, p * 256 + sq * 16 + cc * 8:p * 256 + sq * 16 + cc * 8 + 8]
                                mm(ocb[64 * p:64 * p + 64, cc * 128 + sq * 8:cc * 128 + sq * 8 + 8], vcS[:, sq, 64 * kv:64 * kv + 64], rhs,
                                   True, True, ["vcS", pt0], [oct_])
                                mm(ocb[64 * p:64 * p + 64, 256 + cc * 128 + sq * 8:256 + cc * 128 + sq * 8 + 8], ones[:, 0:64], rhs,
                                   True, True, ["ones", pt0], [oct_])
                    act(odc[:], ocb[:, :], AF.Copy, [oct_, "vgf"], ["odc", "vgf"])
                    tt(odc[:], odb[:, :], odc[:], ALU.add, [odt, "odc"], ["odc", "vgf"])

                def normo(kv):
                    odb = PS[6 + kv]
                    odt = "ps%d" % (6 + kv)
                    if samp:
                        osrc, dsrc, otoks = odc[:, 0:256], odc[:, 256:512], ["odc"]
                    else:
                        osrc, dsrc, otoks = odb[:, 0:256], odb[:, 256:512], [odt]
                    dk = dsm[kv]
                    dtk = "dsm%d" % kv
                    tt(dk[:].rearrange("p (c q) -> p c q", c=2), dsrc.rearrange("p (c q) -> p c q", c=2),
                       sinkE[:, l, 2 * kv:2 * kv + 2].rearrange("p (c o) -> p c o", o=1).to_broadcast([128, 2, 128]), ALU.add,
                       otoks + ["small"], [dtk])
                    act(dk[:], dk[:], AF.Ln, [dtk], [dtk])
                    act(dk[:], dk[:], AF.Exp, [dtk], [dtk], scale=-1.0)
                    tt(mixT[:, 2 * kv:2 * kv + 2, bi * 128:bi * 128 + 128], osrc.rearrange("p (c q) -> p c q", c=2),
                       dk[:].rearrange("p (c q) -> p c q", c=2), ALU.mult, otoks + [dtk], ["mixT"])

                if samp:
                    for kv in range(2):
                        scores(kv); expo(kv); maskm(kv); pv(kv); cache_part(kv); normo(kv)
                        yield
                else:
                    scores(0); expo(0); scores(1); maskm(0); expo(1)
                    yield
                    pv(0); maskm(1); normo(0)
                    yield
                    pv(1); normo(1)
                    yield
                wi = 1 if samp else 0
                zb = PS[4 + bi]
                zt = "ps%d" % (4 + bi)
                for g in range(4):
                    mm(zb[:, g * 128:g * 128 + 128], vgb[bi][:, g * 128:g * 128 + 128], WsT[:, wi, g, :], True, False, ["vgb%d" % bi, "WsT"], [zt])
                    mm(zb[:, g * 128:g * 128 + 128], onesrow[:, :], bsr[:, wi, g * 128:g * 128 + 128], False, True,
                       ["onesrow", "bsr"], [zt])
                tt(mixT[:, 4:8, bi * 128:bi * 128 + 128], zb[:, :].rearrange("p (g t) -> p g t", g=4), uTb[:, :, bi * 128:bi * 128 + 128],
                   ALU.mult, [zt, "uT" + sfb], ["mixT"])
                yield
            for m in range(8):
                par = pjctr[0] % 2
                pjctr[0] += 1
                pj = PS[par][:, 0:W]
                for k in range(8):
                    mm(pj, WOUT[:, k, m * 128:m * 128 + 128], mixT[:, k, 0:W], k == 0, k == 7, ["WOUT", "mixT"], ["pj%d" % par])
                xupdate(l, xtok, m, 16, pj, t0, W, samp, "pj%d" % par, tmpA[0])
                if m % 2 == 1:
                    yield


        def run_gens(gens):
            gens = [g for g in gens if g is not None]
            while gens:
                for g in list(gens):
                    try:
                        next(g)
                    except StopIteration:
                        gens.remove(g)


        wgctr = [0]
        wdctr = [0]

        def ffn_norm(l, ti, t0, W, first, samp_too, mid=None):
            xtok = "XTf%d" % ti
            if first:
                norm_mod(l, xtok, t0 - 2, W + 2, False, 24, hF1, 0, "hF", xsqF, CCT, lnF, rstdF, tmpF, PS[6], "ps6", mid=mid)
                c0 = 384 - t0
                ts(hF1[:, :, c0:c0 + 2], hF1[:, :, c0:c0 + 2], mcore[:, 0:1], None, ALU.mult, None, ["hF", "mcore"], ["hF"])
            else:
                cp(hF1[:, :, 0:2], hsave[:, :, :], ["hsave"], ["hF"])
                norm_mod(l, xtok, t0, W, False, 24, hF1, 2, "hF", xsqF, CCT, lnF, rstdF, tmpF, PS[6], "ps6", mid=mid)
            cp(hsave[:, :, :], hF1[:, :, W:W + 2], ["hF"], ["hsave"])
            if samp_too:
                SB = W + 2
                memset(hF1[:, :, SB:SB + 160], 0.0, ["hF"])
                norm_mod(l, "XTs", SOFF, 128, True, 24, hF1, SB, "hF", xsqF, CCT, lnF, rstdF, tmpF, PS[6], "ps6",
                         sdst=(lambda k, SB=SB: hF1[:, k, SB:SB + 160].rearrange("p (s t) -> p s t", t=10)[:, :, 2:10]))

        def ffn_pass1(l, ti, t0, W, last, samp_too):
            NW = W + 2
            SB = NW
            NWS = NW + (160 if samp_too else 0)
            L = NWS - 2
            ht = "hF"
            for f in range(NF):
                if l == 0 and f % 2 == 0:
                    mod1_piece()
                ws = wgctr[0] % NWG
                wgctr[0] += 1
                wt = "WG%d" % ws
                dma("pool", WG[ws][:, :, :], I["w_ffn_in"][(l * NF + f) * 128:(l * NF + f + 1) * 128, :].rearrange("p (k c) -> p k c", k=8),
                    (), [wt], chan=wt)
                par = f % 2
                cg, cu = cgt[par], cut[par]
                for half, (psb, ct, dst) in enumerate(((PS[2 * par], "ps%d" % (2 * par), cg), (PS[2 * par + 1], "ps%d" % (2 * par + 1), cu))):
                    j = half * NF + f
                    for k in range(8):
                        mm(psb[:, 0:NWS], WG[ws][:, k, half * 128:half * 128 + 128], hF1[:, k, 0:NWS], k == 0, k == 7, [wt, ht], [ct])
                    dtok = "c%d_%d" % (half, par)
                    if samp_too:
                        pss = psb[:, SB:SB + 160].rearrange("p (s t) -> p s t", t=10)
                        act(pss[:, :, 0:2], ccS[:, j, :].rearrange("p (s r) -> p s r", r=2), AF.Copy, [ct, "ccS"], [ct])
                    act(dst[:, 0:L], psb[:, 2:NWS], AF.Identity, [ct, "small"], [dtok], bias=conv_bT[:, l, j:j + 1], scale=conv_wT[:, l, 2, j:j + 1])
                    stt(dst[:, 0:L], psb[:, 1:NWS - 1], conv_wT[:, l, 1, j:j + 1], dst[:, 0:L], ALU.mult, ALU.add, [ct, dtok, "small"], [dtok])
                    stt(dst[:, 0:L], psb[:, 0:NWS - 2], conv_wT[:, l, 0, j:j + 1], dst[:, 0:L], ALU.mult, ALU.add, [ct, dtok, "small"], [dtok])
                    if samp_too:
                        cp(cvs[:, j, :].rearrange("p (s r) -> p s r", r=2), pss[:, :, 8:10], [ct], ["cvs"])
                    if last:
                        cp(cvp[:, j, :], psb[:, NW - 2:NW], [ct], ["cvp"])
                act(cg[:, 0:L], cg[:, 0:L], AF.Silu, ["c0_%d" % par], ["c0_%d" % par])
                tt(actT[:, f, 0:L], cg[:, 0:L], cu[:, 0:L], ALU.mult, ["c0_%d" % par, "c1_%d" % par], ["actT%d" % f])

        def ffn_pass2(l, ti, t0, W, samp_too, m0=0, m1=8):
            xtok = "XTf%d" % ti
            WT = W + (160 if samp_too else 0)
            for m in range(m0, m1):
                ds_ = wdctr[0] % NWD
                wdctr[0] += 1
                dtk = "WD%d" % ds_
                dma("pool", WD[ds_][:, :, :], I["w_ffn_out"][(l * 8 + m) * 128:(l * 8 + m + 1) * 128, :].rearrange("p (f c) -> p f c", f=NF),
                    (), [dtk], chan=dtk)
                yb = PS[4 + m % 2]
                yt = "ps%d" % (4 + m % 2)
                for f in range(NF):
                    mm(yb[:, 0:WT], WD[ds_][:, f, :], actT[:, f, 0:WT], f == 0, f == NF - 1, [dtk, "actT%d" % f], [yt])
                xupdate(l, xtok, m, 40, yb[:, 0:W], t0, W, False, yt, None)
                if samp_too:
                    xupdate(l, "XTs", m, 40, yb[:, W + 2:W + 162].rearrange("p (s t) -> p s t", t=10)[:, :, 0:8], SOFF, 128, True, yt, tmpF[0])

        STAGE = int(os.environ.get("KSTAGE", "99"))
        for l in range(2):
            if STAGE < 1 + 4 * l:
                break
            S.reorder = True
            for k in range(0, 8, 2):
                dma("pool", WIN[:, k:k + 2, :], I["w_in"][l * 128:(l + 1) * 128, k * 1920:(k + 2) * 1920].rearrange("p (k c) -> p k c", k=2),
                    (), ["WIN"], chan="WIN")
            dma("pool", WOUT[:, :, :], I["w_out"][l * 128:(l + 1) * 128, :].rearrange("p (k c) -> p k c", k=8), (), ["WOUT"], chan="WOUT")
            dma("sp", lng[:, :, :], I["lngb"][:, l * 1024:(l + 1) * 1024].rearrange("p (a n) -> p a n", a=2), (), ["lng"], chan="lng")
            dma("sp", wstmp[:, 0, :], I["wsT"][:, l * 512:(l + 1) * 512], (), ["wstmp", "gz0_b0", "gz1_b0"], chan="wst")
            dma("sp", wstmp[:, 1, :], I["wbd"][:, l * 512:(l + 1) * 512], (), ["wstmp", "gz0_b0", "gz1_b0"], chan="wst")
            tt(WsT[:, 0, :, :], wstmp[:, 0, :].rearrange("p (g t) -> p g t", g=4), msk[:, 0:1, :].to_broadcast([128, 4, 128]), ALU.mult,
               ["wstmp", "msk", "gz0_b0", "gz1_b0"], ["WsT"])
            tt(WsT[:, 1, :, :], wstmp[:, 1, :].rearrange("p (g t) -> p g t", g=4), msk[:, 3:4, :].to_broadcast([128, 4, 128]), ALU.mult,
               ["wstmp", "msk", "gz0_b0", "gz1_b0"], ["WsT"])
            dma("pool", bsr[:, 0, :], I["bsrow"][:, l * 512:(l + 1) * 512], (), ["bsr"], chan="bsr")
            dma("pool", bsr[:, 1, :], I["bsrow_s"][:, l * 512:(l + 1) * 512], (), ["bsr"], chan="bsr")
            first_full = 1 + l
            if l == 0:
                late = ["XTa%d" % b_ for b_ in range(5, 19)] + ["XTs"]
                for k in range(8):
                    dma("sp", XT[:, k, XSPLIT:NT], I["xT"][:, k * NT + XSPLIT:(k + 1) * NT], (), late, chan="xin2")
            run_gens([attn_front(l, [l], False, True, {}, 0, xsqA, ["xsqA"])])
            ptiles = []
            b = first_full
            while b <= 18:
                blocks = [b] if b == 18 else [b, b + 1]
                outs = {}
                if blocks[-1] == 18:
                    bi = len(blocks) - 1
                    outs["k"] = (O["kpT"][l * 128:(l + 1) * 128, :].rearrange("p (a n) -> p a n", a=2), bi)
                    outs["v"] = (O["vp"][l * 128:(l + 1) * 128, :], bi)
                ptiles.append((blocks, outs))
                b += len(blocks)
            run_gens([attn_front(l, ptiles[0][0], False, False, ptiles[0][1], 0, xsqA, ["xsqA"])])
            if l == 0:
                mod0_ffn_half()
            for i, (blocks, outs) in enumerate(ptiles):
                nxt = None
                if i + 1 < len(ptiles):
                    nxt = attn_front(l, ptiles[i + 1][0], False, False, ptiles[i + 1][1], (i + 1) % 2, xsqA, ["xsqA"])
                run_gens([attn_back(l, blocks, False, outs, i % 2), nxt])
            S.barrier()
            dma("pool", kTcS[:, :, :], I["kTc"][l * 128:(l + 1) * 128, :].rearrange("p (a n) -> p a n", a=2), (), ["kTcS"], chan="kTcS")
            dma("pool", vcS[:, :, :], I["vc"][l * 128:(l + 1) * 128, :].rearrange("p (a n) -> p a n", a=16), (), ["vcS"], chan="vcS")
            outs = {"k": (O["ksT"][l * 128:(l + 1) * 128, :].rearrange("p (a n) -> p a n", a=2), 0),
                    "v": (O["vs"][l * 128:(l + 1) * 128, :], 0), "gv": O["gvs"][l * 128:(l + 1) * 128, :]}
            run_gens([attn_front(l, [19], True, False, outs, 0, mixT, ["mixT"])])
            run_gens([attn_back(l, [19], True, outs, 0)])
            S.barrier()
            if STAGE < 4 + 4 * l:
                break
            S.reorder = False
            dma("sp", ccS[:, :, :], I["ccT"][l * 128:(l + 1) * 128, :].rearrange("p (j n) -> p j n", j=44), (), ["ccS"], chan="ccS")
            tiles = []
            t = 255 if l == 0 else 384
            total = NPT - t
            NTL = 5
            wo = -(-(total + 162) // NTL)
            widths = [wo] * (NTL - 1) + [total - wo * (NTL - 1)]
            assert max(widths) <= 510 and 0 < widths[-1] <= 348, widths
            for W in widths:
                tiles.append((t, W))
                t += W
            nt_ = len(tiles)
            ffn_norm(l, 0, tiles[0][0], tiles[0][1], True, nt_ == 1)
            for ti, (t, W) in enumerate(tiles):
                last = (ti == nt_ - 1)
                ffn_pass1(l, ti, t, W, last, last)
                if not last:
                    ffn_norm(l, ti + 1, tiles[ti + 1][0], tiles[ti + 1][1], False, ti + 1 == nt_ - 1,
                             mid=(lambda l=l, ti=ti, t=t, W=W, last=last: ffn_pass2(l, ti, t, W, last, 0, 3)))
                    ffn_pass2(l, ti, t, W, last, 3, 8)
                else:
                    ffn_pass2(l, ti, t, W, last)
                if l == 1:
                    a_ = max(t, 384)
                    if a_ < t + W:
                        dma("sp", O["yT"].rearrange("p (k n) -> p k n", k=8)[:, :, a_ - 384:t + W - 384], XT[:, :, a_:t + W], ["XTf%d" % ti], (), chan="oy")
                    if last:
                        dma("sp", O["ysT"].rearrange("p (k n) -> p k n", k=8), XT[:, :, SOFF:NT], ["XTs"], (), chan="oy")
            if l == 0:
                while modctr[0] < 48:
                    mod1_piece()
                mod_finish(1)
            dma("sp", O["convp"][l * 128:(l + 1) * 128, :], cvp[:, :, :].rearrange("p j r -> p (j r)"), ["cvp"], (), chan="ocv")
            dma("sp", O["convs"][l * 128:(l + 1) * 128, :], cvs[:, :, :].rearrange("p j n -> p (j n)"), ["cvs"], (), chan="ocv")
            S.barrier()
        S.emit(st)
    return nc


def _consts(core):
    half = core % 2
    pos = np.zeros(NT, np.int64)
    j = np.arange(NPT)
    pos[:NPT] = np.maximum(j - 384, 0) if half == 0 else 1664 + j
    pos[NPT:] = 16384 + (np.arange(128) % 8)
    inv = (np.float32(500000.0) ** (-np.arange(0, 16, 2, dtype=np.float32) / np.float32(16))).astype(np.float32)
    ang = pos.astype(np.float32)[None, :] * inv[:, None]
    cosv, sinv = np.cos(ang).astype(np.float32), np.sin(ang).astype(np.float32)
    cosT = np.ones((128, NT), np.float32)
    sinT = np.zeros((128, NT), np.float32)
    PT0 = np.zeros((128, 128), np.float32)
    for r in range(128):
        d = r % 64
        if d < 16:
            cosT[r] = cosv[d % 8]
            sinT[r] = sinv[d % 8]
            if d < 8:
                PT0[r + 8, r] = -1.0
            else:
                PT0[r - 8, r] = 1.0
    OB = (np.arange(128)[:, None] // 64 == np.arange(128)[None, :] // 64).astype(np.float32)
    k = np.arange(128)[:, None]
    q = np.arange(128)[None, :]
    masks = np.zeros((128, 5, 128), np.float32)
    masks[:, 0] = (k <= q)
    masks[:, 1] = (k > q)
    masks[:, 2] = (k > q) if half == 1 else 0.0
    masks[:, 3] = ((k // 8 == q // 8) & (k % 8 <= q % 8))
    masks[:, 4, 0:8] = (k > np.arange(8)[None, :])
    mcore = np.full((128, 1), 1.0 if half == 1 else 0.0, np.float32)
    return cosT, sinT, PT0, OB, masks.reshape(128, 640), mcore


def _prep(inp):
    f = lambda a: np.ascontiguousarray(np.asarray(a, dtype=np.float32))
    x_prompt, x_sample = f(inp["x_prompt"]), f(inp["x_sample"])
    shared = {}
    wa = f(inp["w_ada"])
    shared["w_ada0"] = f(wa[0][:, 0:3072].reshape(8, 128, 2, 1536).transpose(2, 1, 0, 3).reshape(256, 8 * 1536))
    shared["w_ada0b"] = f(wa[0][:, 3072:6144].reshape(8, 128, 24, 128).transpose(2, 1, 0, 3).reshape(24 * 128, 1024))
    shared["w_ada1"] = f(wa[1].reshape(8, 128, 48, 128).transpose(2, 1, 0, 3).reshape(48 * 128, 1024))
    cidx = np.concatenate([np.arange(0, 512), np.arange(512, 576), np.arange(512, 576), np.arange(576, 640), np.arange(576, 640),
                           np.arange(640, 1792)])
    wi = f(inp["w_in"])[:, :, cidx]
    shared["w_in"] = f(wi.reshape(2, 8, 128, 1920).transpose(0, 2, 1, 3).reshape(256, 8 * 1920))
    shared["w_out"] = f(f(inp["w_out"]).reshape(2, 8, 128, 1024).transpose(0, 2, 1, 3).reshape(256, 8192))
    wfi = f(inp["w_ffn_in"]).reshape(2, 8, 128, 2, 22, 128)
    shared["w_ffn_in"] = f(wfi.transpose(0, 4, 2, 1, 3, 5).reshape(2 * 22 * 128, 8 * 256))
    wfo = f(inp["w_ffn_out"]).reshape(2, 22, 128, 8, 128)
    shared["w_ffn_out"] = f(wfo.transpose(0, 3, 2, 1, 4).reshape(2 * 8 * 128, 22 * 128))
    shared["b_adaT"] = f(f(inp["b_ada"]).reshape(2, 48, 128).transpose(2, 0, 1).reshape(128, 96))
    shared["g_attnT"] = f(f(inp["g_attn"]).reshape(2, 8, 128).transpose(2, 0, 1).reshape(128, 16))
    shared["g_ffnT"] = f(f(inp["g_ffn"]).reshape(2, 8, 128).transpose(2, 0, 1).reshape(128, 16))
    gq, gk = f(inp["g_q"]), f(inp["g_k"])
    gqk = np.zeros((128, 4), np.float32)
    for l in range(2):
        gqk[:, 2 * l] = np.tile(gq[l], 2)
        gqk[:, 2 * l + 1] = np.tile(gk[l], 2)
    shared["gqk"] = gqk
    sk = f(inp["sinks"])
    sinkT = np.zeros((128, 8), np.float32)
    for l in range(2):
        for c in range(4):
            sinkT[0:64, l * 4 + c] = sk[l, 2 * c]
            sinkT[64:128, l * 4 + c] = sk[l, 2 * c + 1]
    shared["sinkT"] = sinkT
    lngb = np.stack([f(inp["ln_g"]).reshape(2, 512), f(inp["ln_b"]).reshape(2, 512)], axis=1)
    shared["lngb"] = f(np.broadcast_to(lngb.reshape(1, 2048), (128, 2048)))
    ws = f(inp["w_s"])
    shared["wsT"] = f(ws.transpose(3, 0, 1, 2).reshape(128, 1024))
    wbd = np.zeros((128, 2, 4, 128), np.float32)
    for sq in range(16):
        wbd[sq * 8:sq * 8 + 8, :, :, sq * 8:sq * 8 + 8] = ws[:, :, 0:8, 0:8].transpose(3, 0, 1, 2)
    shared["wbd"] = wbd.reshape(128, 1024)
    bs = f(inp["b_s"])
    shared["bsrow"] = f(bs.reshape(1, 1024))
    shared["bsrow_s"] = f(np.tile(bs[:, :, 0:8], (1, 1, 16)).reshape(1, 1024))
    shared["conv_wT"] = f(f(inp["conv_w"]).reshape(2, 3, 44, 128).transpose(3, 0, 1, 2).reshape(128, 264))
    shared["conv_bT"] = f(f(inp["conv_b"]).reshape(2, 44, 128).transpose(2, 0, 1).reshape(128, 88))
    ck, cv, cc = f(inp["cache_k"]), f(inp["cache_v"]), f(inp["cache_conv"])
    cp_, cs_ = f(inp["c_prompt"]), f(inp["c_sample"])
    maps = []
    for core in range(8):
        b, half = core // 2, core % 2
        m = dict(shared)
        cosT, sinT, PT0, OB, masks, mcore = _consts(core)
        m.update(cosT=cosT, sinT=sinT, PT0=PT0, OB=OB, masks=masks, mcore=mcore)
        xs = np.zeros((NT, D), np.float32)
        if half == 0:
            xs[384:NPT] = x_prompt[b, 0:2048]
        else:
            xs[0:NPT] = x_prompt[b, 1664:4096]
        sq0 = core * 16
        xs[NPT:] = x_sample[sq0:sq0 + 16].reshape(128, D)
        m["xT"] = f(xs.reshape(NT, 8, 128).transpose(2, 1, 0).reshape(128, 8 * NT))
        cs = np.concatenate([cp_[b:b + 1], cs_[sq0:sq0 + 16]], axis=0)
        m["cT"] = f(cs.reshape(17, 8, 128).transpose(2, 1, 0).reshape(128, 136))
        kc = ck[:, sq0:sq0 + 16]
        kt = kc.transpose(0, 4, 3, 1, 2)
        kt = np.concatenate([kt, kt], axis=1)
        m["kTc"] = f(kt.reshape(256, 4096))
        vcs = cv[:, sq0:sq0 + 16].transpose(0, 2, 1, 3, 4)
        m["vc"] = f(vcs.reshape(256, 2048))
        ccs = cc[:, sq0:sq0 + 16].reshape(2, 16, 2, 44, 128).transpose(0, 4, 3, 1, 2)
        m["ccT"] = f(ccs.reshape(256, 44 * 32))
        maps.append(m)
    return maps


_NC = None


def kernel(**inputs):
    global _NC
    maps = _prep(inputs)
    if _NC is None:
        _NC = build_nc()
    res = run_bass_kernel_spmd(_NC, maps, core_ids=list(range(8)))
    R = [{k: np.asarray(v, dtype=np.float32) for k, v in r.items()} for r in res.results]
    y_prompt = np.zeros((4, 4096, D), np.float32)
    y_sample = np.zeros((128, 8, D), np.float32)
    nkp = np.zeros((2, 4, 128, 2, 64), np.float32)
    nvp = np.zeros((2, 4, 128, 2, 64), np.float32)
    ncp = np.zeros((2, 4, 2, 2 * DFF), np.float32)
    nks = np.zeros((2, 128, 8, 2, 64), np.float32)
    nvs = np.zeros((2, 128, 8, 2, 64), np.float32)
    ngs = np.zeros((2, 128, 8, 512), np.float32)
    ncs = np.zeros((2, 128, 2, 2 * DFF), np.float32)
    for core in range(8):
        b, half = core // 2, core % 2
        r = R[core]
        y_prompt[b, half * 2048:(half + 1) * 2048] = r["yT"].reshape(128, 8, 2048).transpose(2, 1, 0).reshape(2048, D)
        sq0 = core * 16
        y_sample[sq0:sq0 + 16] = r["ysT"].reshape(128, 8, 128).transpose(2, 1, 0).reshape(16, 8, D)
        ks = r["ksT"].reshape(2, 128, 2, 128)[:, 0:64]
        nks[:, sq0:sq0 + 16] = ks.transpose(0, 3, 2, 1).reshape(2, 16, 8, 2, 64)
        nvs[:, sq0:sq0 + 16] = r["vs"].reshape(2, 16, 8, 2, 64)
        ngs[:, sq0:sq0 + 16] = r["gvs"].reshape(2, 16, 8, 512)
        ncs[:, sq0:sq0 + 16] = r["convs"].reshape(2, 128, 44, 16, 2).transpose(0, 3, 4, 2, 1).reshape(2, 16, 2, 2 * DFF)
        if half == 1:
            kp = r["kpT"].reshape(2, 128, 2, 128)[:, 0:64]
            nkp[:, b] = kp.transpose(0, 3, 2, 1)
            nvp[:, b] = r["vp"].reshape(2, 128, 2, 64)
            ncp[:, b] = r["convp"].reshape(2, 128, 44, 2).transpose(0, 3, 2, 1).reshape(2, 2, 2 * DFF)
    return (y_prompt, y_sample, nkp, nvp, ncp, nks, nvs, ngs, ncs)
```

```python
import contextlib
import os
import numpy as np
import concourse.bass as bass
import concourse.mybir as mybir
from concourse.bass_utils import run_bass_kernel_spmd

F32 = mybir.dt.float32
BF16 = mybir.dt.bfloat16
AF = mybir.ActivationFunctionType
ALU = mybir.AluOpType
AX = mybir.AxisListType

D = 1024
NB = 19
NPT = NB * 128
NT = NPT + 128
SOFF = NPT
DFF = 2816
NF = 22
EPS = 1e-6
ENGS = ("pe", "act", "dve", "pool", "sp")


class _Op:
    __slots__ = ("idx", "eng", "fn", "sdeps", "chan", "chan_val", "flag", "seq", "dma_waits", "cost", "lat", "barrier", "pos")


class Sched:
    BANKS = {"pj0": 0, "pj1": 1, "ps2b": 2}
    BANKS.update({"ps%d" % i: i for i in range(8)})
    WINDOW = {"pe": 200, "act": 72, "dve": 72, "pool": 1, "sp": 1}

    def __init__(self, nc):
        self.nc = nc
        self.ops = []
        self.per_eng = {e: [] for e in ENGS}
        self.last_writer = {}
        self.readers = {}
        self.chan_count = {}
        self.last_dma = {}
        self.last_access = {}
        self.region_start = 0
        self.regions = []
        self.reorder = True

    def op(self, eng, fn, reads=(), writes=(), chan=None, cost=0.3, lat=0.0):
        o = _Op()
        o.idx = len(self.ops)
        o.eng = eng
        o.fn = fn
        o.chan = chan
        o.chan_val = None
        o.flag = False
        o.seq = None
        o.cost = cost
        o.lat = lat
        o.barrier = False
        deps = set()
        for t in reads:
            w = self.last_writer.get(t)
            if w is not None:
                deps.add(w)
        for t in writes:
            w = self.last_writer.get(t)
            if w is not None:
                deps.add(w)
            deps.update(self.readers.get(t, ()))
        for t in list(reads) + list(writes):
            b = self.BANKS.get(t)
            if b is not None:
                p = self.last_access.get(b)
                if p is not None:
                    deps.add(p)
        for t in list(reads) + list(writes):
            b = self.BANKS.get(t)
            if b is not None:
                self.last_access[b] = o.idx
        deps.discard(o.idx)
        o.sdeps = set()
        o.dma_waits = []
        for d in deps:
            od = self.ops[d]
            if od.chan is not None:
                ld = self.last_dma[od.chan]
                o.sdeps.add(ld)
                o.dma_waits.append((od.chan, self.chan_count[od.chan]))
            else:
                o.sdeps.add(d)
        if chan is not None:
            self.chan_count[chan] = self.chan_count.get(chan, 0) + 16
            o.chan_val = self.chan_count[chan]
            self.last_dma[chan] = o.idx
        for t in reads:
            self.readers.setdefault(t, []).append(o.idx)
        for t in writes:
            self.last_writer[t] = o.idx
            self.readers[t] = []
        self.ops.append(o)
        self.per_eng[eng].append(o)
        return o

    def barrier(self):
        start, end = self.region_start, len(self.ops)
        self.regions.append((start, end, self.reorder))
        chans = [(c, v) for c, v in self.chan_count.items()]
        prev = [i for i in range(start, end) if self.ops[i].chan is None]
        for e in ENGS:
            o = _Op()
            o.idx = len(self.ops)
            o.eng = e
            o.fn = (lambda en: en.nop())
            o.chan = None
            o.chan_val = None
            o.flag = False
            o.seq = None
            o.cost = 0.05
            o.lat = 0.0
            o.barrier = True
            o.sdeps = set(prev)
            o.dma_waits = list(chans)
            self.ops.append(o)
            self.per_eng[e].append(o)
        self.region_start = len(self.ops)
        self.last_writer = {}
        self.readers = {}
        self.last_access = {}

    def _schedule_region(self, start, end, tstart, reorder):
        ops = self.ops
        queues = {e: [o for o in ops[start:end] if o.eng == e] for e in ENGS}
        head = {e: 0 for e in ENGS}
        done = {}
        fin = {}
        etime = {e: tstart for e in ENGS}
        out = {e: [] for e in ENGS}
        nleft = end - start
        indeg = {}
        for o in ops[start:end]:
            indeg[o.idx] = sum(1 for d in o.sdeps if d >= start)
        users = {}
        for o in ops[start:end]:
            for d in o.sdeps:
                if d >= start:
                    users.setdefault(d, []).append(o.idx)
        while nleft:
            best = None
            for e in ENGS:
                q = queues[e]
                h = head[e]
                n = len(q)
                while h < n and q[h].idx in done:
                    h += 1
                head[e] = h
                cnt = 0
                i = h
                win = self.WINDOW[e] if reorder else 1
                while i < n and cnt < win:
                    o = q[i]
                    i += 1
                    if o.idx in done:
                        continue
                    cnt += 1
                    if indeg[o.idx]:
                        continue
                    est = etime[e]
                    for d in o.sdeps:
                        if d >= start:
                            f = fin[d] + (0.12 if ops[d].eng != e else 0.0)
                            if f > est:
                                est = f
                    key = (est, o.idx)
                    if best is None or key < best[0]:
                        best = (key, o, e)
            assert best is not None, "scheduler stuck"
            (est, _), o, e = best
            done[o.idx] = True
            if o.chan is not None:
                etime[e] = est + o.cost
                fin[o.idx] = est + o.cost + o.lat
            else:
                etime[e] = est + o.cost
                fin[o.idx] = etime[e]
            out[e].append(o)
            for u in users.get(o.idx, ()):
                indeg[u] -= 1
            nleft -= 1
        return out, max(etime.values())

    def schedule(self):
        if self.region_start < len(self.ops):
            self.regions.append((self.region_start, len(self.ops), self.reorder))
            self.region_start = len(self.ops)
        new = {e: [] for e in ENGS}
        t = 0.0
        ri = 0
        i = 0
        n = len(self.ops)
        for (start, end, reorder) in self.regions:
            out, t = self._schedule_region(start, end, t, reorder)
            for e in ENGS:
                new[e].extend(out[e])
            j = end
            while j < n and self.ops[j].barrier:
                new[self.ops[j].eng].append(self.ops[j])
                j += 1
        self.per_eng = new
        self.est_total = t

    def emit(self, st):
        nc = self.nc
        self.schedule()
        ops = self.ops
        for e in ENGS:
            for p, o in enumerate(self.per_eng[e]):
                o.pos = p
        need = {}
        for o in ops:
            latest = {}
            for d in o.sdeps:
                od = ops[d]
                if od.chan is not None:
                    continue
                if od.eng == o.eng and o.eng == "pe":
                    continue
                if latest.get(od.eng, -1) < od.pos:
                    latest[od.eng] = od.pos
            need[o.idx] = [self.per_eng[e][p] for e, p in latest.items()]
            for od in need[o.idx]:
                od.flag = True
        for e in ENGS:
            c = 0
            for o in self.per_eng[e]:
                if o.chan is None and o.flag:
                    c += 1
                    o.seq = c
        esem = {e: st.enter_context(nc.semaphore("s_" + e)) for e in ENGS}
        csem = {c: st.enter_context(nc.semaphore("c_" + str(c))) for c in self.chan_count}
        block = st.enter_context(nc.Block())

        def run(ename, e):
            known = {}
            for o in self.per_eng[ename]:
                waits = {}
                for od in need[o.idx]:
                    s = esem[od.eng]
                    v = od.seq
                    if known.get(s.name, 0) >= v:
                        continue
                    if waits.get(s.name, (None, 0))[1] < v:
                        waits[s.name] = (s, v)
                for (c, v) in o.dma_waits:
                    s = csem[c]
                    if known.get(s.name, 0) >= v:
                        continue
                    if waits.get(s.name, (None, 0))[1] < v:
                        waits[s.name] = (s, v)
                wl = list(waits.values())
                for (s, v) in wl:
                    known[s.name] = v
                embed = None
                if wl and o.chan is None:
                    embed = wl.pop()
                for (s, v) in wl:
                    e.wait_ge(s, v)
                ins = o.fn(e)
                if embed is not None:
                    ins._wait_ge(embed[0], embed[1])
                if o.chan is not None:
                    ins.then_inc(csem[o.chan], 16)
                elif o.flag:
                    ins.then_inc(esem[ename], 1)
            if ename == "sp":
                for c, s in csem.items():
                    if known.get(s.name, 0) < self.chan_count[c]:
                        e.wait_ge(s, self.chan_count[c])

        @block.tensor
        def _(e):
            run("pe", e)

        @block.scalar
        def _(e):
            run("act", e)

        @block.vector
        def _(e):
            run("dve", e)

        @block.gpsimd
        def _(e):
            run("pool", e)

        @block.sync
        def _(e):
            run("sp", e)


IN_SPECS = [
    ("xT", [128, 8 * NT]), ("cT", [128, 8 * 17]), ("cosT", [128, NT]), ("sinT", [128, NT]),
    ("masks", [128, 5 * 128]), ("mcore", [128, 1]),
    ("w_ada0", [4 * 128, 8 * 1536]), ("w_ada1", [48 * 128, 8 * 128]), ("b_adaT", [128, 2 * 48]), ("g_attnT", [128, 16]), ("g_ffnT", [128, 16]),
    ("gqk", [128, 4]), ("w_in", [2 * 128, 8 * 1920]), ("w_out", [2 * 128, 8 * 1024]),
    ("w_ffn_in", [2 * 22 * 128, 8 * 256]), ("w_ffn_out", [2 * 8 * 128, 22 * 128]),
    ("PT0", [128, 128]), ("OB", [128, 128]), ("sinkT", [128, 8]), ("lngb", [128, 2 * 2 * 512]),
    ("wsT", [128, 2 * 4 * 128]), ("wbd", [128, 2 * 4 * 128]), ("bsrow", [1, 2 * 4 * 128]), ("bsrow_s", [1, 2 * 4 * 128]),
    ("conv_wT", [128, 2 * 3 * 44]), ("conv_bT", [128, 2 * 44]),
    ("kTc", [2 * 128, 2 * 2048]), ("vc", [2 * 128, 16 * 128]), ("ccT", [2 * 128, 44 * 32]),
]
OUT_SPECS = [
    ("yT", [128, 8 * 2048]), ("ysT", [128, 8 * 128]),
    ("kpT", [2 * 128, 256]), ("vp", [2 * 128, 128]), ("convp", [2 * 128, 88]),
    ("ksT", [2 * 128, 256]), ("vs", [2 * 128, 128]), ("gvs", [2 * 128, 512]), ("convs", [2 * 128, 44 * 32]),
]


def build_nc():
    nc = bass.Bass("TRN2", target_bir_lowering=False)
    I = {n: nc.dram_tensor(n, s, F32, kind="ExternalInput").ap() for n, s in IN_SPECS}
    O = {n: nc.dram_tensor(n, s, F32, kind="ExternalOutput").ap() for n, s in OUT_SPECS}
    st = contextlib.ExitStack()
    with st:
        S = Sched(nc)

        def sb(name, shape, dt=F32):
            return st.enter_context(nc.sbuf_tensor("sb_" + name, shape, dt))

        XT = sb("XT", [128, 8, NT])
        MOD = [sb("MOD%d" % l, [128, 48, 17]) for l in range(2)]
        WA = [sb("WA%d" % i, [128, 8, 128], BF16) for i in range(2)]
        scT = sb("scT", [128, 8, 17], BF16)
        cst = sb("cst", [128, 8 * 17])
        ones = sb("ones", [128, 128], BF16)
        OBb = sb("OBb", [128, 128], BF16)
        PMt = sb("PMt", [128, 4, 128], BF16)
        msk = sb("msk", [128, 5, 128], BF16)
        mcore = sb("mcore", [128, 1])
        small = sb("small", [128, 2 * 48 + 16 + 16 + 4 + 8 + 2 * 3 * 44 + 2 * 44 + 128])
        o0 = 0
        b_adaT = small[:, o0:o0 + 96].rearrange("p (l c) -> p l c", l=2); o0 += 96
        g_attnT = small[:, o0:o0 + 16].rearrange("p (l c) -> p l c", l=2); o0 += 16
        g_ffnT = small[:, o0:o0 + 16].rearrange("p (l c) -> p l c", l=2); o0 += 16
        gqk = small[:, o0:o0 + 4]; o0 += 4
        sinkE = small[:, o0:o0 + 8].rearrange("p (l c) -> p l c", l=2); o0 += 8
        conv_wT = small[:, o0:o0 + 264].rearrange("p (l j c) -> p l j c", l=2, j=3); o0 += 264
        conv_bT = small[:, o0:o0 + 88].rearrange("p (l c) -> p l c", l=2); o0 += 88
        gq8 = small[:, o0:o0 + 2]; o0 += 2
        tmpc = sb("tmpc", [128, 128])
        onesrow = sb("onesrow", [1, 128], BF16)
        bsr = sb("bsr", [1, 2, 4 * 128], BF16)

        OVB = (int(nc.sbuf_bytes_remaining) // 32) * 32 - 64
        OV = sb("OV", [128, OVB], mybir.dt.uint8)

        class Carver:
            def __init__(self):
                self.off = 0

            def raw(self, nbytes):
                nb = (nbytes + 31) // 32 * 32
                assert self.off + nb <= OVB, ("overlay overflow", self.off + nb, OVB)
                o = self.off
                self.off += nb
                return o

            def view(self, off, shape, dt):
                esz = 2 if dt == BF16 else 4
                n = 1
                for s_ in shape:
                    n *= s_
                ap = OV[:, off:off + n * esz].bitcast(dt)
                if len(shape) == 1:
                    return ap
                names = " ".join("d%d" % i for i in range(len(shape)))
                kw = {"d%d" % i: shape[i] for i in range(len(shape))}
                return ap.rearrange("p (%s) -> p %s" % (names, names), **kw)

            def take(self, shape, dt):
                esz = 2 if dt == BF16 else 4
                n = 1
                for s_ in shape:
                    n *= s_
                return self.view(self.raw(n * esz), shape, dt)

        cm = Carver()
        WAb = [cm.take([8, 1536], BF16) for _ in range(2)]
        ca = Carver()
        WIN = ca.take([8, 1920], BF16)
        WOUT = ca.take([8, 1024], BF16)
        hA = ca.take([8, 256], BF16)
        NSL = 6
        qT = [ca.take([2, 4, 128], BF16)]
        uT = [ca.take([4, 256], BF16)]
        mixT = ca.take([8, 256], BF16)
        qsq = [ca.take([256], BF16) for _ in range(2)]
        qraw = [ca.take([256], BF16) for _ in range(2)]
        t1 = [ca.take([256], F32)] * 2
        t2 = [ca.take([256], F32)] * 2
        rsq = [ca.take([256], F32) for _ in range(2)]
        lnq = rsq
        rstdA = ca.take([256], F32)
        lnA = rstdA
        tmpA = [ca.take([256], F32) for _ in range(2)]
        kf32 = ca.take([2, 256], F32)
        kTr = ca.take([2, NSL, 128], BF16)
        vr = ca.take([NSL, 128], BF16)
        vf32 = ca.take([128], F32)
        gzz = [ca.take([2, 512], F32)]
        wstmp = gzz[0]
        vgf = ca.take([512], F32)
        odc = vgf
        vgb = [ca.take([512], BF16) for _ in range(2)]
        lnst = ca.take([32], F32)
        PTb = [ca.take([2, 512], BF16) for _ in range(2)]
        dsm = [ca.take([256], F32) for _ in range(2)]
        tabc = ca.take([256], F32)
        tabs = ca.take([256], F32)
        lng = ca.take([2, 512], F32)
        WsT = ca.take([2, 4, 128], BF16)
        aloff = ca.raw(12288)
        kTcS = ca.view(aloff, [2, 2048], BF16)
        vcS = ca.view(aloff + 8192, [16, 128], BF16)
        qT.append(ca.view(aloff, [2, 4, 128], BF16))
        uT.append(ca.view(aloff + 2048, [4, 256], BF16))
        gzz.append(ca.view(aloff + 4096, [2, 512], F32))
        xsqA = ca.view(aloff + 8192, [8, 256], BF16)
        cb = Carver()
        actT = cb.take([NF, 510], BF16)
        NWG, NWD = 6, 3
        WG = [cb.take([8, 256], BF16) for _ in range(NWG)]
        WD = [cb.take([NF, 128], BF16) for _ in range(NWD)]
        hF1 = cb.take([8, 640], BF16)
        hsave = cb.take([8, 2], BF16)
        lnF = cb.take([510], F32)
        rstdF = cb.take([510], F32)
        tmpF = [cb.take([510], F32) for _ in range(2)]
        ccoff = cb.raw(4 * 2560)
        cgt = [cb.view(ccoff + (2 * i) * 2560, [640], F32) for i in range(2)]
        cut = [cb.view(ccoff + (2 * i + 1) * 2560, [640], F32) for i in range(2)]
        xsqF = cb.view(ccoff, [8, 510], BF16)
        CCT = ["c0_0", "c1_0", "c0_1", "c1_1"]
        ext = [cb.take([16, 10], F32) for _ in range(2)]
        ccS = cb.take([44, 32], F32)
        cvs = cb.take([44, 32], F32)
        cvp = cb.take([44, 2], F32)

        PS = [st.enter_context(nc.psum_tensor("ps%d" % i, [128, 512], F32)) for i in range(8)]

        def fsz(ap):
            n = 1
            for d_ in tuple(ap.shape)[1:]:
                n *= int(d_)
            return n

        def dma(eng, out, in_, reads=(), writes=(), chan=None):
            nbytes = fsz(in_) * int(tuple(in_.shape)[0]) * 4
            S.op(eng, lambda e: e.dma_start(out=out, in_=in_), reads, writes, chan=chan,
                 cost=(1.0 if eng == "pool" else 0.15), lat=2.0 + nbytes / 2.5e5)

        def mm(out, lhsT, rhs, start, stop, reads, writes):
            S.op("pe", lambda e: e.matmul(out, lhsT, rhs, start=start, stop=stop), reads, writes, cost=0.035 + fsz(rhs) * 0.0006)

        def act(out, in_, func, reads, writes, bias=None, scale=None):
            kw = {}
            if bias is not None:
                kw["bias"] = bias
            if scale is not None:
                kw["scale"] = scale
            S.op("act", lambda e: e.activation(out, in_, func, **kw), reads, writes, cost=0.22 + fsz(out) * 0.00085)

        def tt(out, in0, in1, op, reads, writes, eng="dve"):
            S.op(eng, lambda e: e.tensor_tensor(out, in0, in1, op), reads, writes, cost=0.2 + fsz(out) * 0.00105)

        def ts(out, in0, s1, s2, op0, op1, reads, writes, eng="dve"):
            if op1 is None:
                S.op(eng, lambda e: e.tensor_scalar(out, in0, s1, s2, op0), reads, writes, cost=0.2 + fsz(out) * 0.00105)
            else:
                S.op(eng, lambda e: e.tensor_scalar(out, in0, s1, s2, op0, op1), reads, writes, cost=0.2 + fsz(out) * 0.00105)

        def stt(out, in0, scalar, in1, op0, op1, reads, writes, eng="dve"):
            S.op(eng, lambda e: e.scalar_tensor_tensor(out, in0, scalar, in1, op0, op1), reads, writes, cost=0.2 + fsz(out) * 0.00105)

        def cp(out, in_, reads, writes, eng="dve"):
            S.op(eng, lambda e: e.tensor_copy(out, in_), reads, writes, cost=0.2 + fsz(out) * 0.00105)

        def memset(ap, val, writes, eng="dve"):
            S.op(eng, lambda e: e.memset(ap, val), (), writes)

        def rsum(out, in_, reads, writes):
            S.op("dve", lambda e: e.reduce_sum(out, in_, AX.X), reads, writes, cost=0.2 + fsz(in_) * 0.00105)

        def rsqrt_act(out, lnbuf, in_, scale, reads, writes, lntok):
            if lnbuf is out or lntok is None:
                act(out, in_, AF.Ln, reads, writes, bias=EPS, scale=scale)
                act(out, out, AF.Exp, writes, writes, scale=-0.5)
            else:
                act(lnbuf, in_, AF.Ln, reads, [lntok], bias=EPS, scale=scale)
                act(out, lnbuf, AF.Exp, [lntok], writes, scale=-0.5)

        XSPLIT = 640
        dma("sp", cst[:], I["cT"][:, :], (), ["cst"], chan="cst")
        dma("sp", mcore[:], I["mcore"][:, :], (), ["mcore"], chan="cin")
        so = 0
        for nm, n in (("b_adaT", 96), ("g_attnT", 16), ("g_ffnT", 16), ("gqk", 4), ("sinkT", 8), ("conv_wT", 264), ("conv_bT", 88)):
            dma("sp", small[:, so:so + n], I[nm][:, :], (), ["small"], chan="cin")
            so += n
        dma("pool", msk[:, :, :], I["masks"].rearrange("p (m n) -> p m n", m=5), (), ["msk"], chan="cin2")
        dma("pool", OBb[:], I["OB"][:, :], (), ["OBb"], chan="cin2")
        dma("sp", tmpc[:], I["PT0"][:, :], (), ["tmpc"], chan="cin")
        for k in range(8):
            dma("sp", XT[:, k, 0:XSPLIT], I["xT"][:, k * NT:k * NT + XSPLIT], (), ["XTall"], chan="xin")
        memset(ones[:], 1.0, ["ones"])
        memset(onesrow[:], 1.0, ["onesrow"])
        act(scT[:, :, :], cst[:].rearrange("p (k n) -> p k n", k=8), AF.Silu, ["cst"], ["scT"])
        act(small[:, 132:140], small[:, 132:140], AF.Exp, ["small"], ["small"])
        for l in range(2):
            ts(gq8[:, l:l + 1], gqk[:, 2 * l:2 * l + 1], 0.125, None, ALU.mult, None, ["small"], ["small"])
            ts(PMt[:, 2 * l, :], tmpc[:], gq8[:, l:l + 1], None, ALU.mult, None, ["tmpc", "small"], ["PMt"])
            ts(PMt[:, 2 * l + 1, :], tmpc[:], gqk[:, 2 * l + 1:2 * l + 2], None, ALU.mult, None, ["tmpc", "small"], ["PMt"])


        def mod_finish(l):
            for k in range(8):
                ts(MOD[l][:, 8 + k, :], MOD[l][:, 8 + k, :], 1.0, g_attnT[:, l, k:k + 1], ALU.add, ALU.mult, ["MOD%d" % l, "small"], ["MOD%d" % l])
                ts(MOD[l][:, 32 + k, :], MOD[l][:, 32 + k, :], 1.0, g_ffnT[:, l, k:k + 1], ALU.add, ALU.mult, ["MOD%d" % l, "small"], ["MOD%d" % l])

        for pc in range(4):
            slot = pc % 2
            dma("pool", WAb[slot][:, :, :], I["w_ada0"][pc * 128:(pc + 1) * 128, :].rearrange("p (k c) -> p k c", k=8), (), ["WAb%d" % slot], chan="WAb%d" % slot)
            for cc in range(12):
                for k in range(8):
                    mm(PS[7][:, cc * 17:cc * 17 + 17], WAb[slot][:, k, cc * 128:cc * 128 + 128], scT[:, k, :], k == 0, k == 7,
                       ["WAb%d" % slot, "scT"], ["ps7"])
            tt(MOD[0][:, pc * 12:pc * 12 + 12, :], PS[7][:, 0:204].rearrange("p (c n) -> p c n", n=17),
               b_adaT[:, 0, pc * 12:pc * 12 + 12].rearrange("p (c o) -> p c o", o=1).to_broadcast([128, 12, 17]),
               ALU.add, ["ps7", "small"], ["MOD0"])
        mod_finish(0)
        S.barrier()

        modctr = [0]

        def mod1_piece():
            ch = modctr[0]
            if ch >= 48:
                return
            modctr[0] += 1
            slot = ch % 2
            dma("pool", WA[slot][:, :, :], I["w_ada1"][ch * 128:(ch + 1) * 128, :].rearrange("p (k c) -> p k c", k=8), (), ["WA%d" % slot], chan="WA%d" % slot)
            for k in range(8):
                mm(PS[7][:, 0:17], WA[slot][:, k, :], scT[:, k, :], k == 0, k == 7, ["WA%d" % slot, "scT"], ["ps7"])
            ts(MOD[1][:, ch, :], PS[7][:, 0:17], b_adaT[:, 1, ch:ch + 1], None, ALU.add, None, ["ps7", "small"], ["MOD1"])

        def norm_mod(l, xtok, t0, W, samp, base, hdst, hoff, htok, xsq, xsqtoks, lnb, rstd, tmp2, psb, pstok, mid=None, sdst=None):
            act(xsq[:, :, 0:W], XT[:, :, t0:t0 + W], AF.Square, [xtok], xsqtoks)
            if mid is not None:
                mid()
            for k in range(8):
                mm(psb[:, 0:W], ones[:], xsq[:, k, 0:W], k == 0, k == 7, ["ones"] + xsqtoks, [pstok])
            if lnb is rstd:
                rsqrt_act(rstd[:, 0:W], None, psb[:, 0:W], 1.0 / D, [pstok], ["rstd"], None)
            else:
                rsqrt_act(rstd[:, 0:W], lnb[:, 0:W], psb[:, 0:W], 1.0 / D, [pstok], ["rstd"], "lnn")
            for k in range(8):
                tmp = tmp2[k % 2]
                tk = "tmpn%d" % (k % 2)
                if not samp:
                    stt(tmp[:, 0:W], XT[:, k, t0:t0 + W], MOD[l][:, base + 8 + k, 0:1], rstd[:, 0:W], ALU.mult, ALU.mult,
                        [xtok, "rstd", "MOD%d" % l], [tk])
                    act(hdst[:, k, hoff:hoff + W], tmp[:, 0:W], AF.Identity, [tk, "MOD%d" % l], [htok], bias=MOD[l][:, base + k, 0:1])
                else:
                    tt(tmp[:, 0:W], XT[:, k, t0:t0 + W], rstd[:, 0:W], ALU.mult, [xtok, "rstd"], [tk])
                    t3 = tmp[:, 0:W].rearrange("p (s t) -> p s t", t=8)
                    tt(t3, t3, MOD[l][:, base + 8 + k, 1:17].rearrange("p (s o) -> p s o", o=1).to_broadcast([128, 16, 8]), ALU.mult,
                       [tk, "MOD%d" % l], [tk])
                    dst3 = sdst(k) if sdst is not None else hdst[:, k, hoff:hoff + W].rearrange("p (s t) -> p s t", t=8)
                    tt(dst3, t3,
                       MOD[l][:, base + k, 1:17].rearrange("p (s o) -> p s o", o=1).to_broadcast([128, 16, 8]), ALU.add,
                       [tk, "MOD%d" % l], [htok])

        def xupdate(l, xtok, m, gbase, psap, t0, W, samp, pstok, tmpb):
            if not samp:
                stt(XT[:, m, t0:t0 + W], psap, MOD[l][:, gbase + m, 0:1], XT[:, m, t0:t0 + W], ALU.mult, ALU.add,
                    [pstok, xtok, "MOD%d" % l], [xtok])
            else:
                ps3 = psap if len(tuple(psap.shape)) == 3 else psap.rearrange("p (s t) -> p s t", t=8)
                tt(tmpb[:, 0:W].rearrange("p (s t) -> p s t", t=8), ps3,
                   MOD[l][:, gbase + m, 1:17].rearrange("p (s o) -> p s o", o=1).to_broadcast([128, 16, 8]), ALU.mult,
                   [pstok, "MOD%d" % l], ["tmpx"])
                tt(XT[:, m, t0:t0 + W], XT[:, m, t0:t0 + W], tmpb[:, 0:W], ALU.add, ["tmpx", xtok], [xtok])

        pjctr = [0]
        chctr = [0]

        def attn_front(l, blocks, samp, kvonly, outs, bs, xsq, xsqtoks):
            nb = len(blocks)
            W = 128 * nb
            t0 = blocks[0] * 128
            xtok = "XTs" if samp else "XTa%d" % blocks[0]
            qTb, uTb, gzb = qT[bs], uT[bs], gzz[bs]
            sfb = "_b%d" % bs
            dma("sp", tabc[:, 0:W], I["cosT"][:, t0:t0 + W], (), ["tabc"], chan="tabc")
            dma("sp", tabs[:, 0:W], I["sinT"][:, t0:t0 + W], (), ["tabs"], chan="tabs")
            norm_mod(l, xtok, t0, W, samp, 0, hA, 0, "h", xsq, xsqtoks, lnA, rstdA, tmpA, PS[2], "ps2")
            yield

            def proj_fm(c0):
                par = pjctr[0] % 2
                pjctr[0] += 1
                pj = PS[par][:, 0:W]
                for k in range(8):
                    mm(pj, WIN[:, k, c0:c0 + 128], hA[:, k, 0:W], k == 0, k == 7, ["WIN", "h"], ["pj%d" % par])
                return pj, "pj%d" % par

            chunks = ([] if kvonly else [("q", c) for c in range(4)]) + [("k", 0), ("k", 1)]
            for (kind, c) in chunks:
                cp_ = chctr[0] % 2
                chctr[0] += 1
                sfx = "_%d" % cp_
                c0 = c * 128 if kind == "q" else 512 + c * 128
                pj, ptok = proj_fm(c0)
                spb = PS[2 + cp_]
                sptok = "ps%d" % (2 + cp_)
                act(qsq[cp_][:, 0:W], pj, AF.Square, [ptok], ["qsq" + sfx])
                act(qraw[cp_][:, 0:W], pj, AF.Copy, [ptok], ["qraw" + sfx])
                mm(spb[:, 0:W], OBb[:], qsq[cp_][:, 0:W], True, True, ["OBb", "qsq" + sfx], [sptok])
                mm(spb[:, 256:256 + W], PMt[:, 2 * l + (0 if kind == "q" else 1), :], qraw[cp_][:, 0:W], True, True, ["PMt", "qraw" + sfx], [sptok])
                rsqrt_act(rsq[cp_][:, 0:W], None, spb[:, 0:W], 1.0 / 64, [sptok], ["rsq" + sfx], None)
                gcol = gq8[:, l:l + 1] if kind == "q" else gqk[:, 2 * l + 1:2 * l + 2]
                stt(t1[cp_][:, 0:W], pj, gcol, tabc[:, 0:W], ALU.mult, ALU.mult, [ptok, "small", "tabc"], ["t1"])
                tt(t2[cp_][:, 0:W], spb[:, 256:256 + W], tabs[:, 0:W], ALU.mult, [sptok, "tabs"], ["t2"])
                tt(t1[cp_][:, 0:W], t1[cp_][:, 0:W], t2[cp_][:, 0:W], ALU.add, ["t1", "t2"], ["t1"])
                if kind == "q":
                    tt(qTb[:, 0:nb, c, :], t1[cp_][:, 0:W].rearrange("p (b q) -> p b q", q=128),
                       rsq[cp_][:, 0:W].rearrange("p (b q) -> p b q", q=128), ALU.mult, ["t1", "rsq" + sfx], ["qT" + sfb])
                else:
                    tt(kf32[:, c, 0:W], t1[cp_][:, 0:W], rsq[cp_][:, 0:W], ALU.mult, ["t1", "rsq" + sfx], ["kf32_%d" % c])
                    for bi, b in enumerate(blocks):
                        act(kTr[:, c, b % NSL, :], kf32[:, c, bi * 128:bi * 128 + 128], AF.Copy, ["kf32_%d" % c], ["kT%d" % (b % NSL)])
                yield
            if "k" in outs:
                bi = outs["k"][1]
                dma("sp", outs["k"][0], kf32[:, :, bi * 128:bi * 128 + 128], ["kf32_0", "kf32_1"], (), chan="okv")
            for bi, b in enumerate(blocks):
                sl = b % NSL
                par = pjctr[0] % 2
                pjctr[0] += 1
                pv = PS[par][:, 0:128]
                pvt = "pj%d" % par
                for k in range(8):
                    mm(pv, hA[:, k, bi * 128:bi * 128 + 128], WIN[:, k, 768:896], k == 0, k == 7, ["h", "WIN"], [pvt])
                act(vr[:, sl, :], pv, AF.Copy, [pvt], ["v%d" % sl])
                if "v" in outs and outs["v"][1] == bi:
                    cp(vf32[:], pv, [pvt], ["vf32"])
                    dma("sp", outs["v"][0], vf32[:], ["vf32"], (), chan="okv")
            yield
            if kvonly:
                return
            for c in range(4):
                pj, ptok = proj_fm(896 + c * 128)
                act(uTb[:, c, 0:W], pj, AF.Gelu_apprx_tanh, [ptok], ["uT" + sfb])
                if c % 2 == 1:
                    yield
            for bi, b in enumerate(blocks):
                zb = PS[2 + bi]
                zt = "ps%d" % (2 + bi)
                for k in range(8):
                    mm(zb[:, :], hA[:, k, bi * 128:bi * 128 + 128], WIN[:, k, 1408:1920], k == 0, k == 7, ["h", "WIN"], [zt])
                act(gzb[:, bi, :], zb[:, :], AF.Gelu_apprx_tanh, [zt], ["gz%d" % bi + sfb])
                yield

        def attn_back(l, blocks, samp, outs, bs):
            nb = len(blocks)
            W = 128 * nb
            t0 = blocks[0] * 128
            xtok = "XTs" if samp else "XTa%d" % blocks[0]
            qTb, uTb, gzb = qT[bs], uT[bs], gzz[bs]
            sfb = "_b%d" % bs
            for bi, b in enumerate(blocks):
                gz = gzb[:, bi, :]
                gt = "gz%d" % bi + sfb
                rsum(lnst[:, 0:4], gz.rearrange("p (g w) -> p g w", g=4), [gt], ["ln_s1"])
                tt(vgf[:], gz, gz, ALU.mult, [gt], ["vgf"])
                rsum(lnst[:, 4:8], vgf[:].rearrange("p (g w) -> p g w", g=4), ["vgf"], ["ln_s2"])
                ts(lnst[:, 8:12], lnst[:, 0:4], 1.0 / 128, None, ALU.mult, None, ["ln_s1"], ["ln_mean"])
                tt(lnst[:, 12:16], lnst[:, 8:12], lnst[:, 8:12], ALU.mult, ["ln_mean"], ["ln_m2"])
                stt(lnst[:, 16:20], lnst[:, 4:8], 1.0 / 128, lnst[:, 12:16], ALU.mult, ALU.subtract, ["ln_s2", "ln_m2"], ["ln_var"])
                rsqrt_act(lnst[:, 24:28], lnst[:, 20:24], lnst[:, 16:20], 1.0, ["ln_var"], ["ln_rstd"], "ln_ln")
                for g in range(4):
                    ts(gz[:, g * 128:g * 128 + 128], gz[:, g * 128:g * 128 + 128], lnst[:, 8 + g:9 + g], lnst[:, 24 + g:25 + g],
                       ALU.subtract, ALU.mult, [gt, "ln_mean", "ln_rstd"], [gt])
                tt(gz, gz, lng[:, 0, :], ALU.mult, [gt, "lng"], [gt])
                if "gv" in outs:
                    tt(vgf[:], gz, lng[:, 1, :], ALU.add, [gt, "lng"], ["vgf"])
                    act(vgb[bi][:], vgf[:], AF.Copy, ["vgf"], ["vgb%d" % bi])
                    dma("sp", outs["gv"], vgf[:], ["vgf"], (), chan="okv")
                else:
                    tt(vgb[bi][:], gz, lng[:, 1, :], ALU.add, [gt, "lng"], ["vgb%d" % bi])
                yield
            for bi, b in enumerate(blocks):
                sl = b % NSL
                kbs = ([] if samp else [(0, (b - 1) % NSL, 2 if b == 3 else 1)]) + [(1, sl, 3 if samp else 0)]
                kb0 = kbs[0][0]

                def scores(kv):
                    for (kb, ksl, mi) in kbs:
                        for p in range(2):
                            mm(PS[4 + p][:, kb * 256:kb * 256 + 256], kTr[64 * p:64 * p + 64, kv, ksl, :],
                               qTb[64 * p:64 * p + 64, bi, 2 * kv:2 * kv + 2, :], True, True, ["kT%d" % ksl, "qT" + sfb], ["ps%d" % (4 + p)])

                def expo(kv):
                    for p in range(2):
                        act(PTb[kv][:, kb0:2, p * 256:p * 256 + 256], PS[4 + p][:, kb0 * 256:512].rearrange("k (b n) -> k b n", n=256), AF.Exp,
                            ["ps%d" % (4 + p)], ["PT%d_%d" % (kb_, kv) for kb_ in range(kb0, 2)])

                def maskm(kv):
                    for (kb, ksl, mi) in kbs:
                        tt(PTb[kv][:, kb, :].rearrange("k (h q) -> k h q", h=4), PTb[kv][:, kb, :].rearrange("k (h q) -> k h q", h=4),
                           msk[:, mi:mi + 1, :].to_broadcast([128, 4, 128]), ALU.mult, ["PT%d_%d" % (kb, kv), "msk"], ["PT%d_%d" % (kb, kv)])

                def pv(kv):
                    odb = PS[6 + kv]
                    odt = "ps%d" % (6 + kv)
                    for p in range(2):
                        for i_, (kb, ksl, mi) in enumerate(kbs):
                            mm(odb[64 * p:64 * p + 64, 0:256], vr[:, ksl, 64 * kv:64 * kv + 64], PTb[kv][:, kb, p * 256:p * 256 + 256],
                               i_ == 0, i_ == len(kbs) - 1, ["v%d" % ksl, "PT%d_%d" % (kb, kv)], [odt])
                        for i_, (kb, ksl, mi) in enumerate(kbs):
                            mm(odb[64 * p:64 * p + 64, 256:512], ones[:, 0:64], PTb[kv][:, kb, p * 256:p * 256 + 256],
                               i_ == 0, i_ == len(kbs) - 1, ["ones", "PT%d_%d" % (kb, kv)], [odt])

                def cache_part(kv):
                    odb = PS[6 + kv]
                    odt = "ps%d" % (6 + kv)
                    pt0 = "PT0_%d" % kv
                    for sq in range(16):
                        for p in range(2):
                            for cc in range(2):
                                mm(PS[4 + p][:, sq * 16 + cc * 8:sq * 16 + cc * 8 + 8],
                                   kTcS[64 * p:64 * p + 64, kv, sq * 128:sq * 128 + 128],
                                   qTb[64 * p:64 * p + 64, 0, 2 * kv + cc, sq * 8:sq * 8 + 8], True, True, ["kTcS", "qT" + sfb], ["ps%d" % (4 + p)])
                    for p in range(2):
                        act(PTb[kv][:, 0, p * 256:p * 256 + 256], PS[4 + p][:, 0:256], AF.Exp, ["ps%d" % (4 + p)], [pt0])
                    ocb = PS[7 - kv]
                    oct_ = "ps%d" % (7 - kv)
                    tt(PTb[kv][:, 0, :].rearrange("k (h t) -> k h t", t=8), PTb[kv][:, 0, :].rearrange("k (h t) -> k h t", t=8),
                       msk[:, 4:5, 0:8].to_broadcast([128, 64, 8]), ALU.mult, [pt0, "msk"], [pt0])
                    for sq in range(16):
                        for p in range(2):
                            for cc in range(2):
                                rhs = PTb[kv][:, 0, p * 256 + sq * 16 + cc * 8:p * 256 + sq * 16 + cc * 8 + 8]
                                mm(ocb[64 * p:64 * p + 64, cc * 128 + sq * 8:cc * 128 + sq * 8 + 8], vcS[:, sq, 64 * kv:64 * kv + 64], rhs,
                                   True, True, ["vcS", pt0], [oct_])
                                mm(ocb[64 * p:64 * p + 64, 256 + cc * 128 + sq * 8:256 + cc * 128 + sq * 8 + 8], ones[:, 0:64], rhs,
                                   True, True, ["ones", pt0], [oct_])
                    act(odc[:], ocb[:, :], AF.Copy, [oct_, "vgf"], ["odc", "vgf"])
                    tt(odc[:], odb[:, :], odc[:], ALU.add, [odt, "odc"], ["odc", "vgf"])

                def normo(kv):
                    odb = PS[6 + kv]
                    odt = "ps%d" % (6 + kv)
                    if samp:
                        osrc, dsrc, otoks = odc[:, 0:256], odc[:, 256:512], ["odc"]
                    else:
                        osrc, dsrc, otoks = odb[:, 0:256], odb[:, 256:512], [odt]
                    dk = dsm[kv]
                    dtk = "dsm%d" % kv
                    tt(dk[:].rearrange("p (c q) -> p c q", c=2), dsrc.rearrange("p (c q) -> p c q", c=2),
                       sinkE[:, l, 2 * kv:2 * kv + 2].rearrange("p (c o) -> p c o", o=1).to_broadcast([128, 2, 128]), ALU.add,
                       otoks + ["small"], [dtk])
                    act(dk[:], dk[:], AF.Ln, [dtk], [dtk])
                    act(dk[:], dk[:], AF.Exp, [dtk], [dtk], scale=-1.0)
                    tt(mixT[:, 2 * kv:2 * kv + 2, bi * 128:bi * 128 + 128], osrc.rearrange("p (c q) -> p c q", c=2),
                       dk[:].rearrange("p (c q) -> p c q", c=2), ALU.mult, otoks + [dtk], ["mixT"])

                if samp:
                    for kv in range(2):
                        scores(kv); expo(kv); maskm(kv); pv(kv); cache_part(kv); normo(kv)
                        yield
                else:
                    scores(0); expo(0); scores(1); maskm(0); expo(1)
                    yield
                    pv(0); maskm(1); normo(0)
                    yield
                    pv(1); normo(1)
                    yield
                wi = 1 if samp else 0
                zb = PS[4 + bi]
                zt = "ps%d" % (4 + bi)
                for g in range(4):
                    mm(zb[:, g * 128:g * 128 + 128], vgb[bi][:, g * 128:g * 128 + 128], WsT[:, wi, g, :], True, False, ["vgb%d" % bi, "WsT"], [zt])
                    mm(zb[:, g * 128:g * 128 + 128], onesrow[:, :], bsr[:, wi, g * 128:g * 128 + 128], False, True,
                       ["onesrow", "bsr"], [zt])
                tt(mixT[:, 4:8, bi * 128:bi * 128 + 128], zb[:, :].rearrange("p (g t) -> p g t", g=4), uTb[:, :, bi * 128:bi * 128 + 128],
                   ALU.mult, [zt, "uT" + sfb], ["mixT"])
                yield
            for m in range(8):
                par = pjctr[0] % 2
                pjctr[0] += 1
                pj = PS[par][:, 0:W]
                for k in range(8):
                    mm(pj, WOUT[:, k, m * 128:m * 128 + 128], mixT[:, k, 0:W], k == 0, k == 7, ["WOUT", "mixT"], ["pj%d" % par])
                xupdate(l, xtok, m, 16, pj, t0, W, samp, "pj%d" % par, tmpA[0])
                if m % 2 == 1:
                    yield


        def run_gens(gens):
            gens = [g for g in gens if g is not None]
            while gens:
                for g in list(gens):
                    try:
                        next(g)
                    except StopIteration:
                        gens.remove(g)


        wgctr = [0]
        wdctr = [0]

        def ffn_norm(l, ti, t0, W, first, samp_too, mid=None):
            xtok = "XTf%d" % ti
            if first:
                norm_mod(l, xtok, t0 - 2, W + 2, False, 24, hF1, 0, "hF", xsqF, CCT, lnF, rstdF, tmpF, PS[6], "ps6", mid=mid)
                c0 = 384 - t0
                ts(hF1[:, :, c0:c0 + 2], hF1[:, :, c0:c0 + 2], mcore[:, 0:1], None, ALU.mult, None, ["hF", "mcore"], ["hF"])
            else:
                cp(hF1[:, :, 0:2], hsave[:, :, :], ["hsave"], ["hF"])
                norm_mod(l, xtok, t0, W, False, 24, hF1, 2, "hF", xsqF, CCT, lnF, rstdF, tmpF, PS[6], "ps6", mid=mid)
            cp(hsave[:, :, :], hF1[:, :, W:W + 2], ["hF"], ["hsave"])
            if samp_too:
                SB = W + 2
                memset(hF1[:, :, SB:SB + 160], 0.0, ["hF"])
                norm_mod(l, "XTs", SOFF, 128, True, 24, hF1, SB, "hF", xsqF, CCT, lnF, rstdF, tmpF, PS[6], "ps6",
                         sdst=(lambda k, SB=SB: hF1[:, k, SB:SB + 160].rearrange("p (s t) -> p s t", t=10)[:, :, 2:10]))

        def ffn_pass1(l, ti, t0, W, last, samp_too):
            NW = W + 2
            SB = NW
            NWS = NW + (160 if samp_too else 0)
            L = NWS - 2
            ht = "hF"
            for f in range(NF):
                if l == 0 and f % 2 == 0:
                    mod1_piece()
                ws = wgctr[0] % NWG
                wgctr[0] += 1
                wt = "WG%d" % ws
                dma("pool", WG[ws][:, :, :], I["w_ffn_in"][(l * NF + f) * 128:(l * NF + f + 1) * 128, :].rearrange("p (k c) -> p k c", k=8),
                    (), [wt], chan=wt)
                par = f % 2
                cg, cu = cgt[par], cut[par]
                for half, (psb, ct, dst) in enumerate(((PS[2 * par], "ps%d" % (2 * par), cg), (PS[2 * par + 1], "ps%d" % (2 * par + 1), cu))):
                    j = half * NF + f
                    for k in range(8):
                        mm(psb[:, 0:NWS], WG[ws][:, k, half * 128:half * 128 + 128], hF1[:, k, 0:NWS], k == 0, k == 7, [wt, ht], [ct])
                    dtok = "c%d_%d" % (half, par)
                    if samp_too:
                        pss = psb[:, SB:SB + 160].rearrange("p (s t) -> p s t", t=10)
                        act(pss[:, :, 0:2], ccS[:, j, :].rearrange("p (s r) -> p s r", r=2), AF.Copy, [ct, "ccS"], [ct])
                    act(dst[:, 0:L], psb[:, 2:NWS], AF.Identity, [ct, "small"], [dtok], bias=conv_bT[:, l, j:j + 1], scale=conv_wT[:, l, 2, j:j + 1])
                    stt(dst[:, 0:L], psb[:, 1:NWS - 1], conv_wT[:, l, 1, j:j + 1], dst[:, 0:L], ALU.mult, ALU.add, [ct, dtok, "small"], [dtok])
                    stt(dst[:, 0:L], psb[:, 0:NWS - 2], conv_wT[:, l, 0, j:j + 1], dst[:, 0:L], ALU.mult, ALU.add, [ct, dtok, "small"], [dtok])
                    if samp_too:
                        cp(cvs[:, j, :].rearrange("p (s r) -> p s r", r=2), pss[:, :, 8:10], [ct], ["cvs"])
                    if last:
                        cp(cvp[:, j, :], psb[:, NW - 2:NW], [ct], ["cvp"])
                act(cg[:, 0:L], cg[:, 0:L], AF.Silu, ["c0_%d" % par], ["c0_%d" % par])
                tt(actT[:, f, 0:L], cg[:, 0:L], cu[:, 0:L], ALU.mult, ["c0_%d" % par, "c1_%d" % par], ["actT%d" % f])

        def ffn_pass2(l, ti, t0, W, samp_too, m0=0, m1=8):
            xtok = "XTf%d" % ti
            WT = W + (160 if samp_too else 0)
            for m in range(m0, m1):
                ds_ = wdctr[0] % NWD
                wdctr[0] += 1
                dtk = "WD%d" % ds_
                dma("pool", WD[ds_][:, :, :], I["w_ffn_out"][(l * 8 + m) * 128:(l * 8 + m + 1) * 128, :].rearrange("p (f c) -> p f c", f=NF),
                    (), [dtk], chan=dtk)
                yb = PS[4 + m % 2]
                yt = "ps%d" % (4 + m % 2)
                for f in range(NF):
                    mm(yb[:, 0:WT], WD[ds_][:, f, :], actT[:, f, 0:WT], f == 0, f == NF - 1, [dtk, "actT%d" % f], [yt])
                xupdate(l, xtok, m, 40, yb[:, 0:W], t0, W, False, yt, None)
                if samp_too:
                    xupdate(l, "XTs", m, 40, yb[:, W + 2:W + 162].rearrange("p (s t) -> p s t", t=10)[:, :, 0:8], SOFF, 128, True, yt, tmpF[0])

        STAGE = int(os.environ.get("KSTAGE", "99"))
        for l in range(2):
            if STAGE < 1 + 4 * l:
                break
            S.reorder = True
            for k in range(0, 8, 2):
                dma("pool", WIN[:, k:k + 2, :], I["w_in"][l * 128:(l + 1) * 128, k * 1920:(k + 2) * 1920].rearrange("p (k c) -> p k c", k=2),
                    (), ["WIN"], chan="WIN")
            dma("pool", WOUT[:, :, :], I["w_out"][l * 128:(l + 1) * 128, :].rearrange("p (k c) -> p k c", k=8), (), ["WOUT"], chan="WOUT")
            dma("sp", lng[:, :, :], I["lngb"][:, l * 1024:(l + 1) * 1024].rearrange("p (a n) -> p a n", a=2), (), ["lng"], chan="lng")
            dma("sp", wstmp[:, 0, :], I["wsT"][:, l * 512:(l + 1) * 512], (), ["wstmp", "gz0_b0", "gz1_b0"], chan="wst")
            dma("sp", wstmp[:, 1, :], I["wbd"][:, l * 512:(l + 1) * 512], (), ["wstmp", "gz0_b0", "gz1_b0"], chan="wst")
            tt(WsT[:, 0, :, :], wstmp[:, 0, :].rearrange("p (g t) -> p g t", g=4), msk[:, 0:1, :].to_broadcast([128, 4, 128]), ALU.mult,
               ["wstmp", "msk", "gz0_b0", "gz1_b0"], ["WsT"])
            tt(WsT[:, 1, :, :], wstmp[:, 1, :].rearrange("p (g t) -> p g t", g=4), msk[:, 3:4, :].to_broadcast([128, 4, 128]), ALU.mult,
               ["wstmp", "msk", "gz0_b0", "gz1_b0"], ["WsT"])
            dma("pool", bsr[:, 0, :], I["bsrow"][:, l * 512:(l + 1) * 512], (), ["bsr"], chan="bsr")
            dma("pool", bsr[:, 1, :], I["bsrow_s"][:, l * 512:(l + 1) * 512], (), ["bsr"], chan="bsr")
            first_full = 1 + l
            if l == 0:
                late = ["XTa%d" % b_ for b_ in range(5, 19)] + ["XTs"]
                for k in range(8):
                    dma("sp", XT[:, k, XSPLIT:NT], I["xT"][:, k * NT + XSPLIT:(k + 1) * NT], (), late, chan="xin2")
            run_gens([attn_front(l, [l], False, True, {}, 0, xsqA, ["xsqA"])])
            ptiles = []
            b = first_full
            while b <= 18:
                blocks = [b] if b == 18 else [b, b + 1]
                outs = {}
                if blocks[-1] == 18:
                    bi = len(blocks) - 1
                    outs["k"] = (O["kpT"][l * 128:(l + 1) * 128, :].rearrange("p (a n) -> p a n", a=2), bi)
                    outs["v"] = (O["vp"][l * 128:(l + 1) * 128, :], bi)
                ptiles.append((blocks, outs))
                b += len(blocks)
            assert len(ptiles) % 2 == 1
            souts = {"k": (O["ksT"][l * 128:(l + 1) * 128, :].rearrange("p (a n) -> p a n", a=2), 0),
                     "v": (O["vs"][l * 128:(l + 1) * 128, :], 0), "gv": O["gvs"][l * 128:(l + 1) * 128, :]}
            run_gens([attn_front(l, ptiles[0][0], False, False, ptiles[0][1], 1, xsqA, ["xsqA"])])
            for i, (blocks, outs) in enumerate(ptiles):
                if i + 1 < len(ptiles):
                    nxt = attn_front(l, ptiles[i + 1][0], False, False, ptiles[i + 1][1], i % 2, xsqA, ["xsqA"])
                else:
                    nxt = attn_front(l, [19], True, False, souts, 0, xsqA, ["xsqA"])
                run_gens([attn_back(l, blocks, False, outs, (i + 1) % 2), nxt])
            dma("pool", kTcS[:, :, :], I["kTc"][l * 128:(l + 1) * 128, :].rearrange("p (a n) -> p a n", a=2), (),
                ["kTcS", "qT_b1", "uT_b1", "gz0_b1", "gz1_b1"], chan="kTcS")
            dma("pool", vcS[:, :, :], I["vc"][l * 128:(l + 1) * 128, :].rearrange("p (a n) -> p a n", a=16), (), ["vcS", "xsqA"], chan="vcS")
            run_gens([attn_back(l, [19], True, souts, 0)])
            S.barrier()
            if STAGE < 4 + 4 * l:
                break
            S.reorder = False
            dma("sp", ccS[:, :, :], I["ccT"][l * 128:(l + 1) * 128, :].rearrange("p (j n) -> p j n", j=44), (), ["ccS"], chan="ccS")
            tiles = []
            t = 255 if l == 0 else 384
            total = NPT - t
            NTL = 5
            wo = -(-(total + 162) // NTL)
            widths = [wo] * (NTL - 1) + [total - wo * (NTL - 1)]
            assert max(widths) <= 510 and 0 < widths[-1] <= 348, widths
            for W in widths:
                tiles.append((t, W))
                t += W
            nt_ = len(tiles)
            ffn_norm(l, 0, tiles[0][0], tiles[0][1], True, nt_ == 1)
            for ti, (t, W) in enumerate(tiles):
                last = (ti == nt_ - 1)
                ffn_pass1(l, ti, t, W, last, last)
                if not last:
                    ffn_norm(l, ti + 1, tiles[ti + 1][0], tiles[ti + 1][1], False, ti + 1 == nt_ - 1,
                             mid=(lambda l=l, ti=ti, t=t, W=W, last=last: ffn_pass2(l, ti, t, W, last, 0, 3)))
                    ffn_pass2(l, ti, t, W, last, 3, 8)
                else:
                    ffn_pass2(l, ti, t, W, last)
                if l == 1:
                    a_ = max(t, 384)
                    if a_ < t + W:
                        dma("sp", O["yT"].rearrange("p (k n) -> p k n", k=8)[:, :, a_ - 384:t + W - 384], XT[:, :, a_:t + W], ["XTf%d" % ti], (), chan="oy")
                    if last:
                        dma("sp", O["ysT"].rearrange("p (k n) -> p k n", k=8), XT[:, :, SOFF:NT], ["XTs"], (), chan="oy")
            if l == 0:
                while modctr[0] < 48:
                    mod1_piece()
                mod_finish(1)
            dma("sp", O["convp"][l * 128:(l + 1) * 128, :], cvp[:, :, :].rearrange("p j r -> p (j r)"), ["cvp"], (), chan="ocv")
            dma("sp", O["convs"][l * 128:(l + 1) * 128, :], cvs[:, :, :].rearrange("p j n -> p (j n)"), ["cvs"], (), chan="ocv")
            S.barrier()
        S.emit(st)
    return nc


def _consts(core):
    half = core % 2
    pos = np.zeros(NT, np.int64)
    j = np.arange(NPT)
    pos[:NPT] = np.maximum(j - 384, 0) if half == 0 else 1664 + j
    pos[NPT:] = 16384 + (np.arange(128) % 8)
    inv = (np.float32(500000.0) ** (-np.arange(0, 16, 2, dtype=np.float32) / np.float32(16))).astype(np.float32)
    ang = pos.astype(np.float32)[None, :] * inv[:, None]
    cosv, sinv = np.cos(ang).astype(np.float32), np.sin(ang).astype(np.float32)
    cosT = np.ones((128, NT), np.float32)
    sinT = np.zeros((128, NT), np.float32)
    PT0 = np.zeros((128, 128), np.float32)
    for r in range(128):
        d = r % 64
        if d < 16:
            cosT[r] = cosv[d % 8]
            sinT[r] = sinv[d % 8]
            if d < 8:
                PT0[r + 8, r] = -1.0
            else:
                PT0[r - 8, r] = 1.0
    OB = (np.arange(128)[:, None] // 64 == np.arange(128)[None, :] // 64).astype(np.float32)
    k = np.arange(128)[:, None]
    q = np.arange(128)[None, :]
    masks = np.zeros((128, 5, 128), np.float32)
    masks[:, 0] = (k <= q)
    masks[:, 1] = (k > q)
    masks[:, 2] = (k > q) if half == 1 else 0.0
    masks[:, 3] = ((k // 8 == q // 8) & (k % 8 <= q % 8))
    masks[:, 4, 0:8] = (k > np.arange(8)[None, :])
    mcore = np.full((128, 1), 1.0 if half == 1 else 0.0, np.float32)
    return cosT, sinT, PT0, OB, masks.reshape(128, 640), mcore


def _prep(inp):
    f = lambda a: np.ascontiguousarray(np.asarray(a, dtype=np.float32))
    x_prompt, x_sample = f(inp["x_prompt"]), f(inp["x_sample"])
    shared = {}
    wa = f(inp["w_ada"])
    shared["w_ada0"] = f(wa[0].reshape(8, 128, 4, 1536).transpose(2, 1, 0, 3).reshape(512, 8 * 1536))
    shared["w_ada1"] = f(wa[1].reshape(8, 128, 48, 128).transpose(2, 1, 0, 3).reshape(48 * 128, 1024))
    cidx = np.concatenate([np.arange(0, 512), np.arange(512, 576), np.arange(512, 576), np.arange(576, 640), np.arange(576, 640),
                           np.arange(640, 1792)])
    wi = f(inp["w_in"])[:, :, cidx]
    shared["w_in"] = f(wi.reshape(2, 8, 128, 1920).transpose(0, 2, 1, 3).reshape(256, 8 * 1920))
    shared["w_out"] = f(f(inp["w_out"]).reshape(2, 8, 128, 1024).transpose(0, 2, 1, 3).reshape(256, 8192))
    wfi = f(inp["w_ffn_in"]).reshape(2, 8, 128, 2, 22, 128)
    shared["w_ffn_in"] = f(wfi.transpose(0, 4, 2, 1, 3, 5).reshape(2 * 22 * 128, 8 * 256))
    wfo = f(inp["w_ffn_out"]).reshape(2, 22, 128, 8, 128)
    shared["w_ffn_out"] = f(wfo.transpose(0, 3, 2, 1, 4).reshape(2 * 8 * 128, 22 * 128))
    shared["b_adaT"] = f(f(inp["b_ada"]).reshape(2, 48, 128).transpose(2, 0, 1).reshape(128, 96))
    shared["g_attnT"] = f(f(inp["g_attn"]).reshape(2, 8, 128).transpose(2, 0, 1).reshape(128, 16))
    shared["g_ffnT"] = f(f(inp["g_ffn"]).reshape(2, 8, 128).transpose(2, 0, 1).reshape(128, 16))
    gq, gk = f(inp["g_q"]), f(inp["g_k"])
    gqk = np.zeros((128, 4), np.float32)
    for l in range(2):
        gqk[:, 2 * l] = np.tile(gq[l], 2)
        gqk[:, 2 * l + 1] = np.tile(gk[l], 2)
    shared["gqk"] = gqk
    sk = f(inp["sinks"])
    sinkT = np.zeros((128, 8), np.float32)
    for l in range(2):
        for c in range(4):
            sinkT[0:64, l * 4 + c] = sk[l, 2 * c]
            sinkT[64:128, l * 4 + c] = sk[l, 2 * c + 1]
    shared["sinkT"] = sinkT
    lngb = np.stack([f(inp["ln_g"]).reshape(2, 512), f(inp["ln_b"]).reshape(2, 512)], axis=1)
    shared["lngb"] = f(np.broadcast_to(lngb.reshape(1, 2048), (128, 2048)))
    ws = f(inp["w_s"])
    shared["wsT"] = f(ws.transpose(3, 0, 1, 2).reshape(128, 1024))
    wbd = np.zeros((128, 2, 4, 128), np.float32)
    for sq in range(16):
        wbd[sq * 8:sq * 8 + 8, :, :, sq * 8:sq * 8 + 8] = ws[:, :, 0:8, 0:8].transpose(3, 0, 1, 2)
    shared["wbd"] = wbd.reshape(128, 1024)
    bs = f(inp["b_s"])
    shared["bsrow"] = f(bs.reshape(1, 1024))
    shared["bsrow_s"] = f(np.tile(bs[:, :, 0:8], (1, 1, 16)).reshape(1, 1024))
    shared["conv_wT"] = f(f(inp["conv_w"]).reshape(2, 3, 44, 128).transpose(3, 0, 1, 2).reshape(128, 264))
    shared["conv_bT"] = f(f(inp["conv_b"]).reshape(2, 44, 128).transpose(2, 0, 1).reshape(128, 88))
    ck, cv, cc = f(inp["cache_k"]), f(inp["cache_v"]), f(inp["cache_conv"])
    cp_, cs_ = f(inp["c_prompt"]), f(inp["c_sample"])
    maps = []
    for core in range(8):
        b, half = core // 2, core % 2
        m = dict(shared)
        cosT, sinT, PT0, OB, masks, mcore = _consts(core)
        m.update(cosT=cosT, sinT=sinT, PT0=PT0, OB=OB, masks=masks, mcore=mcore)
        xs = np.zeros((NT, D), np.float32)
        if half == 0:
            xs[384:NPT] = x_prompt[b, 0:2048]
        else:
            xs[0:NPT] = x_prompt[b, 1664:4096]
        sq0 = core * 16
        xs[NPT:] = x_sample[sq0:sq0 + 16].reshape(128, D)
        m["xT"] = f(xs.reshape(NT, 8, 128).transpose(2, 1, 0).reshape(128, 8 * NT))
        cs = np.concatenate([cp_[b:b + 1], cs_[sq0:sq0 + 16]], axis=0)
        m["cT"] = f(cs.reshape(17, 8, 128).transpose(2, 1, 0).reshape(128, 136))
        kc = ck[:, sq0:sq0 + 16]
        kt = kc.transpose(0, 4, 3, 1, 2)
        kt = np.concatenate([kt, kt], axis=1)
        m["kTc"] = f(kt.reshape(256, 4096))
        vcs = cv[:, sq0:sq0 + 16].transpose(0, 2, 1, 3, 4)
        m["vc"] = f(vcs.reshape(256, 2048))
        ccs = cc[:, sq0:sq0 + 16].reshape(2, 16, 2, 44, 128).transpose(0, 4, 3, 1, 2)
        m["ccT"] = f(ccs.reshape(256, 44 * 32))
        maps.append(m)
    return maps


_NC = None


def kernel(**inputs):
    global _NC
    maps = _prep(inputs)
    if _NC is None:
        _NC = build_nc()
    res = run_bass_kernel_spmd(_NC, maps, core_ids=list(range(8)))
    R = [{k: np.asarray(v, dtype=np.float32) for k, v in r.items()} for r in res.results]
    y_prompt = np.zeros((4, 4096, D), np.float32)
    y_sample = np.zeros((128, 8, D), np.float32)
    nkp = np.zeros((2, 4, 128, 2, 64), np.float32)
    nvp = np.zeros((2, 4, 128, 2, 64), np.float32)
    ncp = np.zeros((2, 4, 2, 2 * DFF), np.float32)
    nks = np.zeros((2, 128, 8, 2, 64), np.float32)
    nvs = np.zeros((2, 128, 8, 2, 64), np.float32)
    ngs = np.zeros((2, 128, 8, 512), np.float32)
    ncs = np.zeros((2, 128, 2, 2 * DFF), np.float32)
    for core in range(8):
        b, half = core // 2, core % 2
        r = R[core]
        y_prompt[b, half * 2048:(half + 1) * 2048] = r["yT"].reshape(128, 8, 2048).transpose(2, 1, 0).reshape(2048, D)
        sq0 = core * 16
        y_sample[sq0:sq0 + 16] = r["ysT"].reshape(128, 8, 128).transpose(2, 1, 0).reshape(16, 8, D)
        ks = r["ksT"].reshape(2, 128, 2, 128)[:, 0:64]
        nks[:, sq0:sq0 + 16] = ks.transpose(0, 3, 2, 1).reshape(2, 16, 8, 2, 64)
        nvs[:, sq0:sq0 + 16] = r["vs"].reshape(2, 16, 8, 2, 64)
        ngs[:, sq0:sq0 + 16] = r["gvs"].reshape(2, 16, 8, 512)
        ncs[:, sq0:sq0 + 16] = r["convs"].reshape(2, 128, 44, 16, 2).transpose(0, 3, 4, 2, 1).reshape(2, 16, 2, 2 * DFF)
        if half == 1:
            kp = r["kpT"].reshape(2, 128, 2, 128)[:, 0:64]
            nkp[:, b] = kp.transpose(0, 3, 2, 1)
            nvp[:, b] = r["vp"].reshape(2, 128, 2, 64)
            ncp[:, b] = r["convp"].reshape(2, 128, 44, 2).transpose(0, 3, 2, 1).reshape(2, 2, 2 * DFF)
    return (y_prompt, y_sample, nkp, nvp, ncp, nks, nvs, ngs, ncs)
```

```python
import contextlib
import os
import numpy as np
import concourse.bass as bass
import concourse.mybir as mybir
from concourse.bass_utils import run_bass_kernel_spmd

F32 = mybir.dt.float32
BF16 = mybir.dt.bfloat16
AF = mybir.ActivationFunctionType
ALU = mybir.AluOpType
AX = mybir.AxisListType

D = 1024
NB = 19
NPT = NB * 128
NT = NPT + 128
SOFF = NPT
DFF = 2816
NF = 22
EPS = 1e-6
ENGS = ("pe", "act", "dve", "pool", "sp")


class _Op:
    __slots__ = ("idx", "eng", "fn", "sdeps", "chan", "chan_val", "flag", "seq", "dma_waits", "cost", "lat", "barrier", "pos", "tset")


class Sched:
    BANKS = {"pj0": 0, "pj1": 1, "ps2b": 2}
    BANKS.update({"ps%d" % i: i for i in range(8)})
    WINDOW = {"pe": 200, "act": 72, "dve": 72, "pool": 1, "sp": 1}

    def __init__(self, nc):
        self.nc = nc
        self.ops = []
        self.per_eng = {e: [] for e in ENGS}
        self.last_writer = {}
        self.readers = {}
        self.chan_count = {}
        self.last_dma = {}
        self.last_access = {}
        self.region_start = 0
        self.regions = []
        self.reorder = True
        self.cur_tset = None

    def op(self, eng, fn, reads=(), writes=(), chan=None, cost=0.3, lat=0.0, tset=None):
        o = _Op()
        o.tset = tset
        o.idx = len(self.ops)
        o.eng = eng
        o.fn = fn
        o.chan = chan
        o.chan_val = None
        o.flag = False
        o.seq = None
        o.cost = cost
        o.lat = lat
        o.barrier = False
        deps = set()
        for t in reads:
            w = self.last_writer.get(t)
            if w is not None:
                deps.add(w)
        for t in writes:
            w = self.last_writer.get(t)
            if w is not None:
                deps.add(w)
            deps.update(self.readers.get(t, ()))
        for t in list(reads) + list(writes):
            b = self.BANKS.get(t)
            if b is not None:
                p = self.last_access.get(b)
                if p is not None:
                    deps.add(p)
        for t in list(reads) + list(writes):
            b = self.BANKS.get(t)
            if b is not None:
                self.last_access[b] = o.idx
        deps.discard(o.idx)
        o.sdeps = set()
        o.dma_waits = []
        for d in deps:
            od = self.ops[d]
            if od.chan is not None:
                ld = self.last_dma[od.chan]
                o.sdeps.add(ld)
                o.dma_waits.append((od.chan, self.chan_count[od.chan]))
            else:
                o.sdeps.add(d)
        if chan is not None:
            self.chan_count[chan] = self.chan_count.get(chan, 0) + 16
            o.chan_val = self.chan_count[chan]
            self.last_dma[chan] = o.idx
        for t in reads:
            self.readers.setdefault(t, []).append(o.idx)
        for t in writes:
            self.last_writer[t] = o.idx
            self.readers[t] = []
        self.ops.append(o)
        self.per_eng[eng].append(o)
        return o

    def barrier(self):
        start, end = self.region_start, len(self.ops)
        self.regions.append((start, end, self.reorder))
        chans = [(c, v) for c, v in self.chan_count.items()]
        prev = [i for i in range(start, end) if self.ops[i].chan is None]
        for e in ENGS:
            o = _Op()
            o.idx = len(self.ops)
            o.eng = e
            o.fn = (lambda en: en.nop())
            o.chan = None
            o.chan_val = None
            o.flag = False
            o.seq = None
            o.cost = 0.05
            o.lat = 0.0
            o.barrier = True
            o.tset = None
            o.sdeps = set(prev)
            o.dma_waits = list(chans)
            self.ops.append(o)
            self.per_eng[e].append(o)
        self.region_start = len(self.ops)
        self.last_writer = {}
        self.readers = {}
        self.last_access = {}

    def _schedule_region(self, start, end, tstart, reorder):
        ops = self.ops
        queues = {e: [o for o in ops[start:end] if o.eng == e] for e in ENGS}
        head = {e: 0 for e in ENGS}
        done = {}
        fin = {}
        etime = {e: tstart for e in ENGS}
        out = {e: [] for e in ENGS}
        nleft = end - start
        indeg = {}
        for o in ops[start:end]:
            indeg[o.idx] = sum(1 for d in o.sdeps if d >= start)
        users = {}
        for o in ops[start:end]:
            for d in o.sdeps:
                if d >= start:
                    users.setdefault(d, []).append(o.idx)
        while nleft:
            best = None
            for e in ENGS:
                q = queues[e]
                h = head[e]
                n = len(q)
                while h < n and q[h].idx in done:
                    h += 1
                head[e] = h
                cnt = 0
                i = h
                win = self.WINDOW[e] if reorder else 1
                while i < n and cnt < win:
                    o = q[i]
                    i += 1
                    if o.idx in done:
                        continue
                    cnt += 1
                    if indeg[o.idx]:
                        continue
                    est = etime[e]
                    for d in o.sdeps:
                        if d >= start:
                            f = fin[d] + (0.12 if ops[d].eng != e else 0.0)
                            if f > est:
                                est = f
                    if o.tset is not None and o.tset != self.cur_tset:
                        est += 1.3
                    key = (est, o.idx)
                    if best is None or key < best[0]:
                        best = (key, o, e)
            assert best is not None, "scheduler stuck"
            (est, _), o, e = best
            if o.tset is not None:
                self.cur_tset = o.tset
            done[o.idx] = True
            if o.chan is not None:
                etime[e] = est + o.cost
                fin[o.idx] = est + o.cost + o.lat
            else:
                etime[e] = est + o.cost
                fin[o.idx] = etime[e]
            out[e].append(o)
            for u in users.get(o.idx, ()):
                indeg[u] -= 1
            nleft -= 1
        return out, max(etime.values())

    def schedule(self):
        if self.region_start < len(self.ops):
            self.regions.append((self.region_start, len(self.ops), self.reorder))
            self.region_start = len(self.ops)
        new = {e: [] for e in ENGS}
        t = 0.0
        ri = 0
        i = 0
        n = len(self.ops)
        for (start, end, reorder) in self.regions:
            out, t = self._schedule_region(start, end, t, reorder)
            for e in ENGS:
                new[e].extend(out[e])
            j = end
            while j < n and self.ops[j].barrier:
                new[self.ops[j].eng].append(self.ops[j])
                j += 1
        self.per_eng = new
        self.est_total = t

    def emit(self, st):
        nc = self.nc
        self.schedule()
        ops = self.ops
        for e in ENGS:
            for p, o in enumerate(self.per_eng[e]):
                o.pos = p
        need = {}
        for o in ops:
            latest = {}
            for d in o.sdeps:
                od = ops[d]
                if od.chan is not None:
                    continue
                if od.eng == o.eng and o.eng == "pe":
                    continue
                if latest.get(od.eng, -1) < od.pos:
                    latest[od.eng] = od.pos
            need[o.idx] = [self.per_eng[e][p] for e, p in latest.items()]
            for od in need[o.idx]:
                od.flag = True
        for e in ENGS:
            c = 0
            for o in self.per_eng[e]:
                if o.chan is None and o.flag:
                    c += 1
                    o.seq = c
        esem = {e: st.enter_context(nc.semaphore("s_" + e)) for e in ENGS}
        csem = {c: st.enter_context(nc.semaphore("c_" + str(c))) for c in self.chan_count}
        block = st.enter_context(nc.Block())

        def run(ename, e):
            known = {}
            for o in self.per_eng[ename]:
                waits = {}
                for od in need[o.idx]:
                    s = esem[od.eng]
                    v = od.seq
                    if known.get(s.name, 0) >= v:
                        continue
                    if waits.get(s.name, (None, 0))[1] < v:
                        waits[s.name] = (s, v)
                for (c, v) in o.dma_waits:
                    s = csem[c]
                    if known.get(s.name, 0) >= v:
                        continue
                    if waits.get(s.name, (None, 0))[1] < v:
                        waits[s.name] = (s, v)
                wl = list(waits.values())
                for (s, v) in wl:
                    known[s.name] = v
                embed = None
                if wl and o.chan is None:
                    embed = wl.pop()
                for (s, v) in wl:
                    e.wait_ge(s, v)
                ins = o.fn(e)
                if embed is not None:
                    ins._wait_ge(embed[0], embed[1])
                if o.chan is not None:
                    ins.then_inc(csem[o.chan], 16)
                elif o.flag:
                    ins.then_inc(esem[ename], 1)
            if ename == "sp":
                for c, s in csem.items():
                    if known.get(s.name, 0) < self.chan_count[c]:
                        e.wait_ge(s, self.chan_count[c])

        @block.tensor
        def _(e):
            run("pe", e)

        @block.scalar
        def _(e):
            run("act", e)

        @block.vector
        def _(e):
            run("dve", e)

        @block.gpsimd
        def _(e):
            run("pool", e)

        @block.sync
        def _(e):
            run("sp", e)


IN_SPECS = [
    ("xT", [128, 8 * NT]), ("cT", [128, 8 * 17]), ("cosT", [128, NT]), ("sinT", [128, NT]),
    ("masks", [128, 5 * 128]), ("mcore", [128, 1]),
    ("w_ada0", [4 * 128, 8 * 1536]), ("w_ada1", [48 * 128, 8 * 128]), ("b_adaT", [128, 2 * 48]), ("g_attnT", [128, 16]), ("g_ffnT", [128, 16]),
    ("gqk", [128, 4]), ("w_in", [2 * 128, 8 * 1920]), ("w_out", [2 * 128, 8 * 1024]),
    ("w_ffn_in", [2 * 22 * 128, 8 * 256]), ("w_ffn_out", [2 * 8 * 128, 22 * 128]),
    ("PT0", [128, 128]), ("OB", [128, 128]), ("sinkT", [128, 8]), ("lngb", [128, 2 * 2 * 512]),
    ("wsT", [128, 2 * 4 * 128]), ("wbd", [128, 2 * 4 * 128]), ("bsrow", [1, 2 * 4 * 128]), ("bsrow_s", [1, 2 * 4 * 128]),
    ("conv_wT", [128, 2 * 3 * 44]), ("conv_bT", [128, 2 * 44]),
    ("kTc", [2 * 128, 2 * 2048]), ("vc", [2 * 128, 16 * 128]), ("ccT", [2 * 128, 44 * 32]),
]
OUT_SPECS = [
    ("yT", [128, 8 * 2048]), ("ysT", [128, 8 * 128]),
    ("kpT", [2 * 128, 256]), ("vp", [2 * 128, 128]), ("convp", [2 * 128, 88]),
    ("ksT", [2 * 128, 256]), ("vs", [2 * 128, 128]), ("gvs", [2 * 128, 512]), ("convs", [2 * 128, 44 * 32]),
]


def build_nc():
    nc = bass.Bass("TRN2", target_bir_lowering=False)
    I = {n: nc.dram_tensor(n, s, F32, kind="ExternalInput").ap() for n, s in IN_SPECS}
    O = {n: nc.dram_tensor(n, s, F32, kind="ExternalOutput").ap() for n, s in OUT_SPECS}
    st = contextlib.ExitStack()
    with st:
        S = Sched(nc)

        def sb(name, shape, dt=F32):
            return st.enter_context(nc.sbuf_tensor("sb_" + name, shape, dt))

        XT = sb("XT", [128, 8, NT])
        MOD = [sb("MOD%d" % l, [128, 48, 17]) for l in range(2)]
        WA = [sb("WA%d" % i, [128, 8, 128], BF16) for i in range(2)]
        scT = sb("scT", [128, 8, 17], BF16)
        cst = sb("cst", [128, 8 * 17])
        ones = sb("ones", [128, 128], BF16)
        OBb = sb("OBb", [128, 128], BF16)
        PMt = sb("PMt", [128, 4, 128], BF16)
        msk = sb("msk", [128, 5, 128], BF16)
        mcore = sb("mcore", [128, 1])
        small = sb("small", [128, 2 * 48 + 16 + 16 + 4 + 8 + 2 * 3 * 44 + 2 * 44 + 128])
        o0 = 0
        b_adaT = small[:, o0:o0 + 96].rearrange("p (l c) -> p l c", l=2); o0 += 96
        g_attnT = small[:, o0:o0 + 16].rearrange("p (l c) -> p l c", l=2); o0 += 16
        g_ffnT = small[:, o0:o0 + 16].rearrange("p (l c) -> p l c", l=2); o0 += 16
        gqk = small[:, o0:o0 + 4]; o0 += 4
        sinkE = small[:, o0:o0 + 8].rearrange("p (l c) -> p l c", l=2); o0 += 8
        conv_wT = small[:, o0:o0 + 264].rearrange("p (l j c) -> p l j c", l=2, j=3); o0 += 264
        conv_bT = small[:, o0:o0 + 88].rearrange("p (l c) -> p l c", l=2); o0 += 88
        gq8 = small[:, o0:o0 + 2]; o0 += 2
        tmpc = sb("tmpc", [128, 128])
        onesrow = sb("onesrow", [1, 128], BF16)
        bsr = sb("bsr", [1, 2, 4 * 128], BF16)

        OVB = (int(nc.sbuf_bytes_remaining) // 32) * 32 - 64
        OV = sb("OV", [128, OVB], mybir.dt.uint8)

        class Carver:
            def __init__(self):
                self.off = 0

            def raw(self, nbytes):
                nb = (nbytes + 31) // 32 * 32
                assert self.off + nb <= OVB, ("overlay overflow", self.off + nb, OVB)
                o = self.off
                self.off += nb
                return o

            def view(self, off, shape, dt):
                esz = 2 if dt == BF16 else 4
                n = 1
                for s_ in shape:
                    n *= s_
                ap = OV[:, off:off + n * esz].bitcast(dt)
                if len(shape) == 1:
                    return ap
                names = " ".join("d%d" % i for i in range(len(shape)))
                kw = {"d%d" % i: shape[i] for i in range(len(shape))}
                return ap.rearrange("p (%s) -> p %s" % (names, names), **kw)

            def take(self, shape, dt):
                esz = 2 if dt == BF16 else 4
                n = 1
                for s_ in shape:
                    n *= s_
                return self.view(self.raw(n * esz), shape, dt)

        cm = Carver()
        WAb = [cm.take([8, 1536], BF16) for _ in range(2)]
        ca = Carver()
        WIN = ca.take([8, 1920], BF16)
        WOUT = ca.take([8, 1024], BF16)
        hA = ca.take([8, 256], BF16)
        NSL = 6
        qT = [ca.take([2, 4, 128], BF16)]
        uT = [ca.take([4, 256], BF16)]
        mixT = ca.take([8, 256], BF16)
        qsq = [ca.take([256], BF16) for _ in range(2)]
        qraw = [ca.take([256], BF16) for _ in range(2)]
        t1 = [ca.take([256], F32)] * 2
        t2 = [ca.take([256], F32)] * 2
        rsq = [ca.take([256], F32) for _ in range(2)]
        lnq = rsq
        rstdA = ca.take([256], F32)
        lnA = rstdA
        tmpA = [ca.take([256], F32) for _ in range(2)]
        kf32 = ca.take([2, 256], F32)
        kTr = ca.take([2, NSL, 128], BF16)
        vr = ca.take([NSL, 128], BF16)
        vf32 = ca.take([128], F32)
        gzz = [ca.take([2, 512], F32)]
        wstmp = gzz[0]
        vgf = ca.take([512], F32)
        odc = vgf
        vgb = [ca.take([512], BF16) for _ in range(2)]
        lnst = ca.take([32], F32)
        PTb = [ca.take([2, 512], BF16) for _ in range(2)]
        dsm = [ca.take([256], F32) for _ in range(2)]
        tabc = ca.take([256], F32)
        tabs = ca.take([256], F32)
        lng = ca.take([2, 512], F32)
        WsT = ca.take([2, 4, 128], BF16)
        aloff = ca.raw(12288)
        kTcS = ca.view(aloff, [2, 2048], BF16)
        vcS = ca.view(aloff + 8192, [16, 128], BF16)
        qT.append(ca.view(aloff, [2, 4, 128], BF16))
        uT.append(ca.view(aloff + 2048, [4, 256], BF16))
        gzz.append(ca.view(aloff + 4096, [2, 512], F32))
        xsqA = ca.view(aloff + 8192, [8, 256], BF16)
        cb = Carver()
        actT = cb.take([NF, 510], BF16)
        NWG, NWD = 6, 3
        WG = [cb.take([8, 256], BF16) for _ in range(NWG)]
        WD = [cb.take([NF, 128], BF16) for _ in range(NWD)]
        hF1 = cb.take([8, 640], BF16)
        hsave = cb.take([8, 2], BF16)
        lnF = cb.take([510], F32)
        rstdF = cb.take([510], F32)
        tmpF = [cb.take([510], F32) for _ in range(2)]
        ccoff = cb.raw(4 * 2560)
        cgt = [cb.view(ccoff + (2 * i) * 2560, [640], F32) for i in range(2)]
        cut = [cb.view(ccoff + (2 * i + 1) * 2560, [640], F32) for i in range(2)]
        xsqF = cb.view(ccoff, [8, 510], BF16)
        CCT = ["c0_0", "c1_0", "c0_1", "c1_1"]
        ext = [cb.take([16, 10], F32) for _ in range(2)]
        ccS = cb.take([44, 32], F32)
        cvs = cb.take([44, 32], F32)
        cvp = cb.take([44, 2], F32)

        PS = [st.enter_context(nc.psum_tensor("ps%d" % i, [128, 512], F32)) for i in range(8)]

        TSET = {AF.Exp: 6, AF.Ln: 6, AF.Gelu_apprx_tanh: 11, AF.Silu: 18}

        def fsz(ap):
            n = 1
            for d_ in tuple(ap.shape)[1:]:
                n *= int(d_)
            return n

        def dma(eng, out, in_, reads=(), writes=(), chan=None):
            nbytes = fsz(in_) * int(tuple(in_.shape)[0]) * 4
            S.op(eng, lambda e: e.dma_start(out=out, in_=in_), reads, writes, chan=chan,
                 cost=(1.0 if eng == "pool" else 0.15), lat=2.0 + nbytes / 2.5e5)

        def mm(out, lhsT, rhs, start, stop, reads, writes):
            S.op("pe", lambda e: e.matmul(out, lhsT, rhs, start=start, stop=stop), reads, writes, cost=0.035 + fsz(rhs) * 0.0006)

        def act(out, in_, func, reads, writes, bias=None, scale=None):
            kw = {}
            if bias is not None:
                kw["bias"] = bias
            if scale is not None:
                kw["scale"] = scale
            S.op("act", lambda e: e.activation(out, in_, func, **kw), reads, writes, cost=0.22 + fsz(out) * 0.00085,
                 tset=TSET.get(func))

        def tt(out, in0, in1, op, reads, writes, eng="dve"):
            S.op(eng, lambda e: e.tensor_tensor(out, in0, in1, op), reads, writes, cost=0.2 + fsz(out) * 0.00105)

        def ts(out, in0, s1, s2, op0, op1, reads, writes, eng="dve"):
            if op1 is None:
                S.op(eng, lambda e: e.tensor_scalar(out, in0, s1, s2, op0), reads, writes, cost=0.2 + fsz(out) * 0.00105)
            else:
                S.op(eng, lambda e: e.tensor_scalar(out, in0, s1, s2, op0, op1), reads, writes, cost=0.2 + fsz(out) * 0.00105)

        def stt(out, in0, scalar, in1, op0, op1, reads, writes, eng="dve"):
            S.op(eng, lambda e: e.scalar_tensor_tensor(out, in0, scalar, in1, op0, op1), reads, writes, cost=0.2 + fsz(out) * 0.00105)

        def cp(out, in_, reads, writes, eng="dve"):
            S.op(eng, lambda e: e.tensor_copy(out, in_), reads, writes, cost=0.2 + fsz(out) * 0.00105)

        def memset(ap, val, writes, eng="dve"):
            S.op(eng, lambda e: e.memset(ap, val), (), writes)

        def rsum(out, in_, reads, writes):
            S.op("dve", lambda e: e.reduce_sum(out, in_, AX.X), reads, writes, cost=0.2 + fsz(in_) * 0.00105)

        def rsqrt_act(out, lnbuf, in_, scale, reads, writes, lntok):
            if lnbuf is out or lntok is None:
                act(out, in_, AF.Ln, reads, writes, bias=EPS, scale=scale)
                act(out, out, AF.Exp, writes, writes, scale=-0.5)
            else:
                act(lnbuf, in_, AF.Ln, reads, [lntok], bias=EPS, scale=scale)
                act(out, lnbuf, AF.Exp, [lntok], writes, scale=-0.5)

        XSPLIT = 640
        dma("sp", cst[:], I["cT"][:, :], (), ["cst"], chan="cst")
        dma("sp", mcore[:], I["mcore"][:, :], (), ["mcore"], chan="cin")
        so = 0
        for nm, n in (("b_adaT", 96), ("g_attnT", 16), ("g_ffnT", 16), ("gqk", 4), ("sinkT", 8), ("conv_wT", 264), ("conv_bT", 88)):
            dma("sp", small[:, so:so + n], I[nm][:, :], (), ["small"], chan="cin")
            so += n
        dma("pool", msk[:, :, :], I["masks"].rearrange("p (m n) -> p m n", m=5), (), ["msk"], chan="cin2")
        dma("pool", OBb[:], I["OB"][:, :], (), ["OBb"], chan="cin2")
        dma("sp", tmpc[:], I["PT0"][:, :], (), ["tmpc"], chan="cin")
        for k in range(8):
            dma("sp", XT[:, k, 0:XSPLIT], I["xT"][:, k * NT:k * NT + XSPLIT], (), ["XTall"], chan="xin")
        memset(ones[:], 1.0, ["ones"])
        memset(onesrow[:], 1.0, ["onesrow"])
        act(scT[:, :, :], cst[:].rearrange("p (k n) -> p k n", k=8), AF.Silu, ["cst"], ["scT"])
        act(small[:, 132:140], small[:, 132:140], AF.Exp, ["small"], ["small"])
        for l in range(2):
            ts(gq8[:, l:l + 1], gqk[:, 2 * l:2 * l + 1], 0.125, None, ALU.mult, None, ["small"], ["small"])
            ts(PMt[:, 2 * l, :], tmpc[:], gq8[:, l:l + 1], None, ALU.mult, None, ["tmpc", "small"], ["PMt"])
            ts(PMt[:, 2 * l + 1, :], tmpc[:], gqk[:, 2 * l + 1:2 * l + 2], None, ALU.mult, None, ["tmpc", "small"], ["PMt"])


        def mod_finish(l):
            for k in range(8):
                ts(MOD[l][:, 8 + k, :], MOD[l][:, 8 + k, :], 1.0, g_attnT[:, l, k:k + 1], ALU.add, ALU.mult, ["MOD%d" % l, "small"], ["MOD%d" % l])
                ts(MOD[l][:, 32 + k, :], MOD[l][:, 32 + k, :], 1.0, g_ffnT[:, l, k:k + 1], ALU.add, ALU.mult, ["MOD%d" % l, "small"], ["MOD%d" % l])

        for pc in range(4):
            slot = pc % 2
            dma("pool", WAb[slot][:, :, :], I["w_ada0"][pc * 128:(pc + 1) * 128, :].rearrange("p (k c) -> p k c", k=8), (), ["WAb%d" % slot], chan="WAb%d" % slot)
            for cc in range(12):
                for k in range(8):
                    mm(PS[7][:, cc * 17:cc * 17 + 17], WAb[slot][:, k, cc * 128:cc * 128 + 128], scT[:, k, :], k == 0, k == 7,
                       ["WAb%d" % slot, "scT"], ["ps7"])
            tt(MOD[0][:, pc * 12:pc * 12 + 12, :], PS[7][:, 0:204].rearrange("p (c n) -> p c n", n=17),
               b_adaT[:, 0, pc * 12:pc * 12 + 12].rearrange("p (c o) -> p c o", o=1).to_broadcast([128, 12, 17]),
               ALU.add, ["ps7", "small"], ["MOD0"])
        mod_finish(0)
        S.barrier()

        modctr = [0]

        def mod1_piece():
            ch = modctr[0]
            if ch >= 48:
                return
            modctr[0] += 1
            slot = ch % 2
            dma("pool", WA[slot][:, :, :], I["w_ada1"][ch * 128:(ch + 1) * 128, :].rearrange("p (k c) -> p k c", k=8), (), ["WA%d" % slot], chan="WA%d" % slot)
            for k in range(8):
                mm(PS[7][:, 0:17], WA[slot][:, k, :], scT[:, k, :], k == 0, k == 7, ["WA%d" % slot, "scT"], ["ps7"])
            ts(MOD[1][:, ch, :], PS[7][:, 0:17], b_adaT[:, 1, ch:ch + 1], None, ALU.add, None, ["ps7", "small"], ["MOD1"])

        def norm_mod(l, xtok, t0, W, samp, base, hdst, hoff, htok, xsq, xsqtoks, lnb, rstd, tmp2, psb, pstok, mid=None, sdst=None):
            act(xsq[:, :, 0:W], XT[:, :, t0:t0 + W], AF.Square, [xtok], xsqtoks)
            if mid is not None:
                mid()
            for k in range(8):
                mm(psb[:, 0:W], ones[:], xsq[:, k, 0:W], k == 0, k == 7, ["ones"] + xsqtoks, [pstok])
            if lnb is rstd:
                rsqrt_act(rstd[:, 0:W], None, psb[:, 0:W], 1.0 / D, [pstok], ["rstd"], None)
            else:
                rsqrt_act(rstd[:, 0:W], lnb[:, 0:W], psb[:, 0:W], 1.0 / D, [pstok], ["rstd"], "lnn")
            for k in range(8):
                tmp = tmp2[k % 2]
                tk = "tmpn%d" % (k % 2)
                if not samp:
                    stt(tmp[:, 0:W], XT[:, k, t0:t0 + W], MOD[l][:, base + 8 + k, 0:1], rstd[:, 0:W], ALU.mult, ALU.mult,
                        [xtok, "rstd", "MOD%d" % l], [tk])
                    act(hdst[:, k, hoff:hoff + W], tmp[:, 0:W], AF.Identity, [tk, "MOD%d" % l], [htok], bias=MOD[l][:, base + k, 0:1])
                else:
                    tt(tmp[:, 0:W], XT[:, k, t0:t0 + W], rstd[:, 0:W], ALU.mult, [xtok, "rstd"], [tk])
                    t3 = tmp[:, 0:W].rearrange("p (s t) -> p s t", t=8)
                    tt(t3, t3, MOD[l][:, base + 8 + k, 1:17].rearrange("p (s o) -> p s o", o=1).to_broadcast([128, 16, 8]), ALU.mult,
                       [tk, "MOD%d" % l], [tk])
                    dst3 = sdst(k) if sdst is not None else hdst[:, k, hoff:hoff + W].rearrange("p (s t) -> p s t", t=8)
                    tt(dst3, t3,
                       MOD[l][:, base + k, 1:17].rearrange("p (s o) -> p s o", o=1).to_broadcast([128, 16, 8]), ALU.add,
                       [tk, "MOD%d" % l], [htok])

        def xupdate(l, xtok, m, gbase, psap, t0, W, samp, pstok, tmpb):
            if not samp:
                stt(XT[:, m, t0:t0 + W], psap, MOD[l][:, gbase + m, 0:1], XT[:, m, t0:t0 + W], ALU.mult, ALU.add,
                    [pstok, xtok, "MOD%d" % l], [xtok])
            else:
                ps3 = psap if len(tuple(psap.shape)) == 3 else psap.rearrange("p (s t) -> p s t", t=8)
                tt(tmpb[:, 0:W].rearrange("p (s t) -> p s t", t=8), ps3,
                   MOD[l][:, gbase + m, 1:17].rearrange("p (s o) -> p s o", o=1).to_broadcast([128, 16, 8]), ALU.mult,
                   [pstok, "MOD%d" % l], ["tmpx"])
                tt(XT[:, m, t0:t0 + W], XT[:, m, t0:t0 + W], tmpb[:, 0:W], ALU.add, ["tmpx", xtok], [xtok])

        pjctr = [0]
        chctr = [0]

        def attn_front(l, blocks, samp, kvonly, outs, bs, xsq, xsqtoks):
            nb = len(blocks)
            W = 128 * nb
            t0 = blocks[0] * 128
            xtok = "XTs" if samp else "XTa%d" % blocks[0]
            qTb, uTb, gzb = qT[bs], uT[bs], gzz[bs]
            sfb = "_b%d" % bs
            dma("sp", tabc[:, 0:W], I["cosT"][:, t0:t0 + W], (), ["tabc"], chan="tabc")
            dma("sp", tabs[:, 0:W], I["sinT"][:, t0:t0 + W], (), ["tabs"], chan="tabs")
            norm_mod(l, xtok, t0, W, samp, 0, hA, 0, "h", xsq, xsqtoks, lnA, rstdA, tmpA, PS[2], "ps2")
            yield

            def proj_fm(c0):
                par = pjctr[0] % 2
                pjctr[0] += 1
                pj = PS[par][:, 0:W]
                for k in range(8):
                    mm(pj, WIN[:, k, c0:c0 + 128], hA[:, k, 0:W], k == 0, k == 7, ["WIN", "h"], ["pj%d" % par])
                return pj, "pj%d" % par

            chunks = ([] if kvonly else [("q", c) for c in range(4)]) + [("k", 0), ("k", 1)]
            for (kind, c) in chunks:
                cp_ = chctr[0] % 2
                chctr[0] += 1
                sfx = "_%d" % cp_
                c0 = c * 128 if kind == "q" else 512 + c * 128
                pj, ptok = proj_fm(c0)
                spb = PS[2 + cp_]
                sptok = "ps%d" % (2 + cp_)
                act(qsq[cp_][:, 0:W], pj, AF.Square, [ptok], ["qsq" + sfx])
                act(qraw[cp_][:, 0:W], pj, AF.Copy, [ptok], ["qraw" + sfx])
                mm(spb[:, 0:W], OBb[:], qsq[cp_][:, 0:W], True, True, ["OBb", "qsq" + sfx], [sptok])
                mm(spb[:, 256:256 + W], PMt[:, 2 * l + (0 if kind == "q" else 1), :], qraw[cp_][:, 0:W], True, True, ["PMt", "qraw" + sfx], [sptok])
                rsqrt_act(rsq[cp_][:, 0:W], None, spb[:, 0:W], 1.0 / 64, [sptok], ["rsq" + sfx], None)
                gcol = gq8[:, l:l + 1] if kind == "q" else gqk[:, 2 * l + 1:2 * l + 2]
                stt(t1[cp_][:, 0:W], pj, gcol, tabc[:, 0:W], ALU.mult, ALU.mult, [ptok, "small", "tabc"], ["t1"])
                tt(t2[cp_][:, 0:W], spb[:, 256:256 + W], tabs[:, 0:W], ALU.mult, [sptok, "tabs"], ["t2"])
                tt(t1[cp_][:, 0:W], t1[cp_][:, 0:W], t2[cp_][:, 0:W], ALU.add, ["t1", "t2"], ["t1"])
                if kind == "q":
                    tt(qTb[:, 0:nb, c, :], t1[cp_][:, 0:W].rearrange("p (b q) -> p b q", q=128),
                       rsq[cp_][:, 0:W].rearrange("p (b q) -> p b q", q=128), ALU.mult, ["t1", "rsq" + sfx], ["qT" + sfb])
                else:
                    tt(kf32[:, c, 0:W], t1[cp_][:, 0:W], rsq[cp_][:, 0:W], ALU.mult, ["t1", "rsq" + sfx], ["kf32_%d" % c])
                    for bi, b in enumerate(blocks):
                        act(kTr[:, c, b % NSL, :], kf32[:, c, bi * 128:bi * 128 + 128], AF.Copy, ["kf32_%d" % c], ["kT%d" % (b % NSL)])
                yield
            if "k" in outs:
                bi = outs["k"][1]
                dma("sp", outs["k"][0], kf32[:, :, bi * 128:bi * 128 + 128], ["kf32_0", "kf32_1"], (), chan="okv")
            for bi, b in enumerate(blocks):
                sl = b % NSL
                par = pjctr[0] % 2
                pjctr[0] += 1
                pv = PS[par][:, 0:128]
                pvt = "pj%d" % par
                for k in range(8):
                    mm(pv, hA[:, k, bi * 128:bi * 128 + 128], WIN[:, k, 768:896], k == 0, k == 7, ["h", "WIN"], [pvt])
                act(vr[:, sl, :], pv, AF.Copy, [pvt], ["v%d" % sl])
                if "v" in outs and outs["v"][1] == bi:
                    cp(vf32[:], pv, [pvt], ["vf32"])
                    dma("sp", outs["v"][0], vf32[:], ["vf32"], (), chan="okv")
            yield
            if kvonly:
                return
            for c in range(4):
                pj, ptok = proj_fm(896 + c * 128)
                act(uTb[:, c, 0:W], pj, AF.Gelu_apprx_tanh, [ptok], ["uT" + sfb])
                if c % 2 == 1:
                    yield
            for bi, b in enumerate(blocks):
                zb = PS[2 + bi]
                zt = "ps%d" % (2 + bi)
                for k in range(8):
                    mm(zb[:, :], hA[:, k, bi * 128:bi * 128 + 128], WIN[:, k, 1408:1920], k == 0, k == 7, ["h", "WIN"], [zt])
                act(gzb[:, bi, :], zb[:, :], AF.Gelu_apprx_tanh, [zt], ["gz%d" % bi + sfb])
                yield

        def attn_back(l, blocks, samp, outs, bs):
            nb = len(blocks)
            W = 128 * nb
            t0 = blocks[0] * 128
            xtok = "XTs" if samp else "XTa%d" % blocks[0]
            qTb, uTb, gzb = qT[bs], uT[bs], gzz[bs]
            sfb = "_b%d" % bs
            for bi, b in enumerate(blocks):
                gz = gzb[:, bi, :]
                gt = "gz%d" % bi + sfb
                rsum(lnst[:, 0:4], gz.rearrange("p (g w) -> p g w", g=4), [gt], ["ln_s1"])
                tt(vgf[:], gz, gz, ALU.mult, [gt], ["vgf"])
                rsum(lnst[:, 4:8], vgf[:].rearrange("p (g w) -> p g w", g=4), ["vgf"], ["ln_s2"])
                ts(lnst[:, 8:12], lnst[:, 0:4], 1.0 / 128, None, ALU.mult, None, ["ln_s1"], ["ln_mean"])
                tt(lnst[:, 12:16], lnst[:, 8:12], lnst[:, 8:12], ALU.mult, ["ln_mean"], ["ln_m2"])
                stt(lnst[:, 16:20], lnst[:, 4:8], 1.0 / 128, lnst[:, 12:16], ALU.mult, ALU.subtract, ["ln_s2", "ln_m2"], ["ln_var"])
                rsqrt_act(lnst[:, 24:28], lnst[:, 20:24], lnst[:, 16:20], 1.0, ["ln_var"], ["ln_rstd"], "ln_ln")
                for g in range(4):
                    ts(gz[:, g * 128:g * 128 + 128], gz[:, g * 128:g * 128 + 128], lnst[:, 8 + g:9 + g], lnst[:, 24 + g:25 + g],
                       ALU.subtract, ALU.mult, [gt, "ln_mean", "ln_rstd"], [gt])
                tt(gz, gz, lng[:, 0, :], ALU.mult, [gt, "lng"], [gt])
                if "gv" in outs:
                    tt(vgf[:], gz, lng[:, 1, :], ALU.add, [gt, "lng"], ["vgf"])
                    act(vgb[bi][:], vgf[:], AF.Copy, ["vgf"], ["vgb%d" % bi])
                    dma("sp", outs["gv"], vgf[:], ["vgf"], (), chan="okv")
                else:
                    tt(vgb[bi][:], gz, lng[:, 1, :], ALU.add, [gt, "lng"], ["vgb%d" % bi])
                yield
            for bi, b in enumerate(blocks):
                sl = b % NSL
                kbs = ([] if samp else [(0, (b - 1) % NSL, 2 if b == 3 else 1)]) + [(1, sl, 3 if samp else 0)]
                kb0 = kbs[0][0]

                def scores(kv):
                    for (kb, ksl, mi) in kbs:
                        for p in range(2):
                            mm(PS[4 + p][:, kb * 256:kb * 256 + 256], kTr[64 * p:64 * p + 64, kv, ksl, :],
                               qTb[64 * p:64 * p + 64, bi, 2 * kv:2 * kv + 2, :], True, True, ["kT%d" % ksl, "qT" + sfb], ["ps%d" % (4 + p)])

                def expo(kv):
                    for p in range(2):
                        act(PTb[kv][:, kb0:2, p * 256:p * 256 + 256], PS[4 + p][:, kb0 * 256:512].rearrange("k (b n) -> k b n", n=256), AF.Exp,
                            ["ps%d" % (4 + p)], ["PT%d_%d" % (kb_, kv) for kb_ in range(kb0, 2)])

                def maskm(kv):
                    for (kb, ksl, mi) in kbs:
                        tt(PTb[kv][:, kb, :].rearrange("k (h q) -> k h q", h=4), PTb[kv][:, kb, :].rearrange("k (h q) -> k h q", h=4),
                           msk[:, mi:mi + 1, :].to_broadcast([128, 4, 128]), ALU.mult, ["PT%d_%d" % (kb, kv), "msk"], ["PT%d_%d" % (kb, kv)])

                def pv(kv):
                    odb = PS[6 + kv]
                    odt = "ps%d" % (6 + kv)
                    for p in range(2):
                        for i_, (kb, ksl, mi) in enumerate(kbs):
                            mm(odb[64 * p:64 * p + 64, 0:256], vr[:, ksl, 64 * kv:64 * kv + 64], PTb[kv][:, kb, p * 256:p * 256 + 256],
                               i_ == 0, i_ == len(kbs) - 1, ["v%d" % ksl, "PT%d_%d" % (kb, kv)], [odt])
                        for i_, (kb, ksl, mi) in enumerate(kbs):
                            mm(odb[64 * p:64 * p + 64, 256:512], ones[:, 0:64], PTb[kv][:, kb, p * 256:p * 256 + 256],
                               i_ == 0, i_ == len(kbs) - 1, ["ones", "PT%d_%d" % (kb, kv)], [odt])

                def cache_part(kv):
                    odb = PS[6 + kv]
                    odt = "ps%d" % (6 + kv)
                    pt0 = "PT0_%d" % kv
                    for sq in range(16):
                        for p in range(2):
                            for cc in range(2):
                                mm(PS[4 + p][:, sq * 16 + cc * 8:sq * 16 + cc * 8 + 8],
                                   kTcS[64 * p:64 * p + 64, kv, sq * 128:sq * 128 + 128],
                                   qTb[64 * p:64 * p + 64, 0, 2 * kv + cc, sq * 8:sq * 8 + 8], True, True, ["kTcS", "qT" + sfb], ["ps%d" % (4 + p)])
                    for p in range(2):
                        act(PTb[kv][:, 0, p * 256:p * 256 + 256], PS[4 + p][:, 0:256], AF.Exp, ["ps%d" % (4 + p)], [pt0])
                    ocb = PS[7 - kv]
                    oct_ = "ps%d" % (7 - kv)
                    tt(PTb[kv][:, 0, :].rearrange("k (h t) -> k h t", t=8), PTb[kv][:, 0, :].rearrange("k (h t) -> k h t", t=8),
                       msk[:, 4:5, 0:8].to_broadcast([128, 64, 8]), ALU.mult, [pt0, "msk"], [pt0])
                    for sq in range(16):
                        for p in range(2):
                            for cc in range(2):
                                rhs = PTb[kv][:, 0, p * 256 + sq * 16 + cc * 8:p * 256 + sq * 16 + cc * 8 + 8]
                                mm(ocb[64 * p:64 * p + 64, cc * 128 + sq * 8:cc * 128 + sq * 8 + 8], vcS[:, sq, 64 * kv:64 * kv + 64], rhs,
                                   True, True, ["vcS", pt0], [oct_])
                                mm(ocb[64 * p:64 * p + 64, 256 + cc * 128 + sq * 8:256 + cc * 128 + sq * 8 + 8], ones[:, 0:64], rhs,
                                   True, True, ["ones", pt0], [oct_])
                    act(odc[:], ocb[:, :], AF.Copy, [oct_, "vgf"], ["odc", "vgf"])
                    tt(odc[:], odb[:, :], odc[:], ALU.add, [odt, "odc"], ["odc", "vgf"])

                def normo(kv):
                    odb = PS[6 + kv]
                    odt = "ps%d" % (6 + kv)
                    if samp:
                        osrc, dsrc, otoks = odc[:, 0:256], odc[:, 256:512], ["odc"]
                    else:
                        osrc, dsrc, otoks = odb[:, 0:256], odb[:, 256:512], [odt]
                    dk = dsm[kv]
                    dtk = "dsm%d" % kv
                    tt(dk[:].rearrange("p (c q) -> p c q", c=2), dsrc.rearrange("p (c q) -> p c q", c=2),
                       sinkE[:, l, 2 * kv:2 * kv + 2].rearrange("p (c o) -> p c o", o=1).to_broadcast([128, 2, 128]), ALU.add,
                       otoks + ["small"], [dtk])
                    act(dk[:], dk[:], AF.Ln, [dtk], [dtk])
                    act(dk[:], dk[:], AF.Exp, [dtk], [dtk], scale=-1.0)
                    tt(mixT[:, 2 * kv:2 * kv + 2, bi * 128:bi * 128 + 128], osrc.rearrange("p (c q) -> p c q", c=2),
                       dk[:].rearrange("p (c q) -> p c q", c=2), ALU.mult, otoks + [dtk], ["mixT"])

                if samp:
                    for kv in range(2):
                        scores(kv); expo(kv); maskm(kv); pv(kv); cache_part(kv); normo(kv)
                        yield
                else:
                    scores(0); expo(0); scores(1); maskm(0); expo(1)
                    yield
                    pv(0); maskm(1); normo(0)
                    yield
                    pv(1); normo(1)
                    yield
                wi = 1 if samp else 0
                zb = PS[4 + bi]
                zt = "ps%d" % (4 + bi)
                for g in range(4):
                    mm(zb[:, g * 128:g * 128 + 128], vgb[bi][:, g * 128:g * 128 + 128], WsT[:, wi, g, :], True, False, ["vgb%d" % bi, "WsT"], [zt])
                    mm(zb[:, g * 128:g * 128 + 128], onesrow[:, :], bsr[:, wi, g * 128:g * 128 + 128], False, True,
                       ["onesrow", "bsr"], [zt])
                tt(mixT[:, 4:8, bi * 128:bi * 128 + 128], zb[:, :].rearrange("p (g t) -> p g t", g=4), uTb[:, :, bi * 128:bi * 128 + 128],
                   ALU.mult, [zt, "uT" + sfb], ["mixT"])
                yield
            for m in range(8):
                par = pjctr[0] % 2
                pjctr[0] += 1
                pj = PS[par][:, 0:W]
                for k in range(8):
                    mm(pj, WOUT[:, k, m * 128:m * 128 + 128], mixT[:, k, 0:W], k == 0, k == 7, ["WOUT", "mixT"], ["pj%d" % par])
                xupdate(l, xtok, m, 16, pj, t0, W, samp, "pj%d" % par, tmpA[0])
                if m % 2 == 1:
                    yield


        def run_gens(gens):
            gens = [g for g in gens if g is not None]
            while gens:
                for g in list(gens):
                    try:
                        next(g)
                    except StopIteration:
                        gens.remove(g)


        wgctr = [0]
        wdctr = [0]

        def ffn_norm(l, ti, t0, W, first, samp_too, mid=None):
            xtok = "XTf%d" % ti
            if first:
                norm_mod(l, xtok, t0 - 2, W + 2, False, 24, hF1, 0, "hF", xsqF, CCT, lnF, rstdF, tmpF, PS[6], "ps6", mid=mid)
                c0 = 384 - t0
                ts(hF1[:, :, c0:c0 + 2], hF1[:, :, c0:c0 + 2], mcore[:, 0:1], None, ALU.mult, None, ["hF", "mcore"], ["hF"])
            else:
                cp(hF1[:, :, 0:2], hsave[:, :, :], ["hsave"], ["hF"])
                norm_mod(l, xtok, t0, W, False, 24, hF1, 2, "hF", xsqF, CCT, lnF, rstdF, tmpF, PS[6], "ps6", mid=mid)
            cp(hsave[:, :, :], hF1[:, :, W:W + 2], ["hF"], ["hsave"])
            if samp_too:
                SB = W + 2
                memset(hF1[:, :, SB:SB + 160], 0.0, ["hF"])
                norm_mod(l, "XTs", SOFF, 128, True, 24, hF1, SB, "hF", xsqF, CCT, lnF, rstdF, tmpF, PS[6], "ps6",
                         sdst=(lambda k, SB=SB: hF1[:, k, SB:SB + 160].rearrange("p (s t) -> p s t", t=10)[:, :, 2:10]))

        def ffn_pass1(l, ti, t0, W, last, samp_too):
            NW = W + 2
            SB = NW
            NWS = NW + (160 if samp_too else 0)
            L = NWS - 2
            ht = "hF"
            for f in range(NF):
                if l == 0 and f % 2 == 0:
                    mod1_piece()
                ws = wgctr[0] % NWG
                wgctr[0] += 1
                wt = "WG%d" % ws
                dma("pool", WG[ws][:, :, :], I["w_ffn_in"][(l * NF + f) * 128:(l * NF + f + 1) * 128, :].rearrange("p (k c) -> p k c", k=8),
                    (), [wt], chan=wt)
                par = f % 2
                cg, cu = cgt[par], cut[par]
                for half, (psb, ct, dst) in enumerate(((PS[2 * par], "ps%d" % (2 * par), cg), (PS[2 * par + 1], "ps%d" % (2 * par + 1), cu))):
                    j = half * NF + f
                    for k in range(8):
                        mm(psb[:, 0:NWS], WG[ws][:, k, half * 128:half * 128 + 128], hF1[:, k, 0:NWS], k == 0, k == 7, [wt, ht], [ct])
                    dtok = "c%d_%d" % (half, par)
                    if samp_too:
                        pss = psb[:, SB:SB + 160].rearrange("p (s t) -> p s t", t=10)
                        act(pss[:, :, 0:2], ccS[:, j, :].rearrange("p (s r) -> p s r", r=2), AF.Copy, [ct, "ccS"], [ct])
                    act(dst[:, 0:L], psb[:, 2:NWS], AF.Identity, [ct, "small"], [dtok], bias=conv_bT[:, l, j:j + 1], scale=conv_wT[:, l, 2, j:j + 1])
                    stt(dst[:, 0:L], psb[:, 1:NWS - 1], conv_wT[:, l, 1, j:j + 1], dst[:, 0:L], ALU.mult, ALU.add, [ct, dtok, "small"], [dtok])
                    stt(dst[:, 0:L], psb[:, 0:NWS - 2], conv_wT[:, l, 0, j:j + 1], dst[:, 0:L], ALU.mult, ALU.add, [ct, dtok, "small"], [dtok])
                    if samp_too:
                        cp(cvs[:, j, :].rearrange("p (s r) -> p s r", r=2), pss[:, :, 8:10], [ct], ["cvs"])
                    if last:
                        cp(cvp[:, j, :], psb[:, NW - 2:NW], [ct], ["cvp"])
                act(cg[:, 0:L], cg[:, 0:L], AF.Silu, ["c0_%d" % par], ["c0_%d" % par])
                tt(actT[:, f, 0:L], cg[:, 0:L], cu[:, 0:L], ALU.mult, ["c0_%d" % par, "c1_%d" % par], ["actT%d" % f])

        def ffn_pass2(l, ti, t0, W, samp_too, m0=0, m1=8):
            xtok = "XTf%d" % ti
            WT = W + (160 if samp_too else 0)
            for m in range(m0, m1):
                ds_ = wdctr[0] % NWD
                wdctr[0] += 1
                dtk = "WD%d" % ds_
                dma("pool", WD[ds_][:, :, :], I["w_ffn_out"][(l * 8 + m) * 128:(l * 8 + m + 1) * 128, :].rearrange("p (f c) -> p f c", f=NF),
                    (), [dtk], chan=dtk)
                yb = PS[4 + m % 2]
                yt = "ps%d" % (4 + m % 2)
                for f in range(NF):
                    mm(yb[:, 0:WT], WD[ds_][:, f, :], actT[:, f, 0:WT], f == 0, f == NF - 1, [dtk, "actT%d" % f], [yt])
                xupdate(l, xtok, m, 40, yb[:, 0:W], t0, W, False, yt, None)
                if samp_too:
                    xupdate(l, "XTs", m, 40, yb[:, W + 2:W + 162].rearrange("p (s t) -> p s t", t=10)[:, :, 0:8], SOFF, 128, True, yt, tmpF[0])

        STAGE = int(os.environ.get("KSTAGE", "99"))
        for l in range(2):
            if STAGE < 1 + 4 * l:
                break
            S.reorder = True
            for k in range(0, 8, 2):
                dma("pool", WIN[:, k:k + 2, :], I["w_in"][l * 128:(l + 1) * 128, k * 1920:(k + 2) * 1920].rearrange("p (k c) -> p k c", k=2),
                    (), ["WIN"], chan="WIN")
            dma("pool", WOUT[:, :, :], I["w_out"][l * 128:(l + 1) * 128, :].rearrange("p (k c) -> p k c", k=8), (), ["WOUT"], chan="WOUT")
            dma("sp", lng[:, :, :], I["lngb"][:, l * 1024:(l + 1) * 1024].rearrange("p (a n) -> p a n", a=2), (), ["lng"], chan="lng")
            dma("sp", wstmp[:, 0, :], I["wsT"][:, l * 512:(l + 1) * 512], (), ["wstmp", "gz0_b0", "gz1_b0"], chan="wst")
            dma("sp", wstmp[:, 1, :], I["wbd"][:, l * 512:(l + 1) * 512], (), ["wstmp", "gz0_b0", "gz1_b0"], chan="wst")
            tt(WsT[:, 0, :, :], wstmp[:, 0, :].rearrange("p (g t) -> p g t", g=4), msk[:, 0:1, :].to_broadcast([128, 4, 128]), ALU.mult,
               ["wstmp", "msk", "gz0_b0", "gz1_b0"], ["WsT"])
            tt(WsT[:, 1, :, :], wstmp[:, 1, :].rearrange("p (g t) -> p g t", g=4), msk[:, 3:4, :].to_broadcast([128, 4, 128]), ALU.mult,
               ["wstmp", "msk", "gz0_b0", "gz1_b0"], ["WsT"])
            dma("pool", bsr[:, 0, :], I["bsrow"][:, l * 512:(l + 1) * 512], (), ["bsr"], chan="bsr")
            dma("pool", bsr[:, 1, :], I["bsrow_s"][:, l * 512:(l + 1) * 512], (), ["bsr"], chan="bsr")
            first_full = 1 + l
            if l == 0:
                late = ["XTa%d" % b_ for b_ in range(5, 19)] + ["XTs"]
                for k in range(8):
                    dma("sp", XT[:, k, XSPLIT:NT], I["xT"][:, k * NT + XSPLIT:(k + 1) * NT], (), late, chan="xin2")
            run_gens([attn_front(l, [l], False, True, {}, 0, xsqA, ["xsqA"])])
            ptiles = []
            b = first_full
            while b <= 18:
                blocks = [b] if b == 18 else [b, b + 1]
                outs = {}
                if blocks[-1] == 18:
                    bi = len(blocks) - 1
                    outs["k"] = (O["kpT"][l * 128:(l + 1) * 128, :].rearrange("p (a n) -> p a n", a=2), bi)
                    outs["v"] = (O["vp"][l * 128:(l + 1) * 128, :], bi)
                ptiles.append((blocks, outs))
                b += len(blocks)
            run_gens([attn_front(l, ptiles[0][0], False, False, ptiles[0][1], 0, xsqA, ["xsqA"])])
            for i, (blocks, outs) in enumerate(ptiles):
                nxt = None
                if i + 1 < len(ptiles):
                    nxt = attn_front(l, ptiles[i + 1][0], False, False, ptiles[i + 1][1], (i + 1) % 2, xsqA, ["xsqA"])
                run_gens([attn_back(l, blocks, False, outs, i % 2), nxt])
            S.barrier()
            dma("pool", kTcS[:, :, :], I["kTc"][l * 128:(l + 1) * 128, :].rearrange("p (a n) -> p a n", a=2), (), ["kTcS"], chan="kTcS")
            dma("pool", vcS[:, :, :], I["vc"][l * 128:(l + 1) * 128, :].rearrange("p (a n) -> p a n", a=16), (), ["vcS"], chan="vcS")
            outs = {"k": (O["ksT"][l * 128:(l + 1) * 128, :].rearrange("p (a n) -> p a n", a=2), 0),
                    "v": (O["vs"][l * 128:(l + 1) * 128, :], 0), "gv": O["gvs"][l * 128:(l + 1) * 128, :]}
            run_gens([attn_front(l, [19], True, False, outs, 0, mixT, ["mixT"])])
            run_gens([attn_back(l, [19], True, outs, 0)])
            S.barrier()
            if STAGE < 4 + 4 * l:
                break
            S.reorder = False
            dma("sp", ccS[:, :, :], I["ccT"][l * 128:(l + 1) * 128, :].rearrange("p (j n) -> p j n", j=44), (), ["ccS"], chan="ccS")
            tiles = []
            t = 255 if l == 0 else 384
            total = NPT - t
            NTL = 5
            wo = -(-(total + 162) // NTL)
            widths = [wo] * (NTL - 1) + [total - wo * (NTL - 1)]
            assert max(widths) <= 510 and 0 < widths[-1] <= 348, widths
            for W in widths:
                tiles.append((t, W))
                t += W
            nt_ = len(tiles)
            ffn_norm(l, 0, tiles[0][0], tiles[0][1], True, nt_ == 1)
            for ti, (t, W) in enumerate(tiles):
                last = (ti == nt_ - 1)
                ffn_pass1(l, ti, t, W, last, last)
                if not last:
                    ffn_norm(l, ti + 1, tiles[ti + 1][0], tiles[ti + 1][1], False, ti + 1 == nt_ - 1,
                             mid=(lambda l=l, ti=ti, t=t, W=W, last=last: ffn_pass2(l, ti, t, W, last, 0, 3)))
                    ffn_pass2(l, ti, t, W, last, 3, 8)
                else:
                    ffn_pass2(l, ti, t, W, last)
                if l == 1:
                    a_ = max(t, 384)
                    if a_ < t + W:
                        dma("sp", O["yT"].rearrange("p (k n) -> p k n", k=8)[:, :, a_ - 384:t + W - 384], XT[:, :, a_:t + W], ["XTf%d" % ti], (), chan="oy")
                    if last:
                        dma("sp", O["ysT"].rearrange("p (k n) -> p k n", k=8), XT[:, :, SOFF:NT], ["XTs"], (), chan="oy")
            if l == 0:
                while modctr[0] < 48:
                    mod1_piece()
                mod_finish(1)
            dma("sp", O["convp"][l * 128:(l + 1) * 128, :], cvp[:, :, :].rearrange("p j r -> p (j r)"), ["cvp"], (), chan="ocv")
            dma("sp", O["convs"][l * 128:(l + 1) * 128, :], cvs[:, :, :].rearrange("p j n -> p (j n)"), ["cvs"], (), chan="ocv")
            S.barrier()
        S.emit(st)
    return nc


def _consts(core):
    half = core % 2
    pos = np.zeros(NT, np.int64)
    j = np.arange(NPT)
    pos[:NPT] = np.maximum(j - 384, 0) if half == 0 else 1664 + j
    pos[NPT:] = 16384 + (np.arange(128) % 8)
    inv = (np.float32(500000.0) ** (-np.arange(0, 16, 2, dtype=np.float32) / np.float32(16))).astype(np.float32)
    ang = pos.astype(np.float32)[None, :] * inv[:, None]
    cosv, sinv = np.cos(ang).astype(np.float32), np.sin(ang).astype(np.float32)
    cosT = np.ones((128, NT), np.float32)
    sinT = np.zeros((128, NT), np.float32)
    PT0 = np.zeros((128, 128), np.float32)
    for r in range(128):
        d = r % 64
        if d < 16:
            cosT[r] = cosv[d % 8]
            sinT[r] = sinv[d % 8]
            if d < 8:
                PT0[r + 8, r] = -1.0
            else:
                PT0[r - 8, r] = 1.0
    OB = (np.arange(128)[:, None] // 64 == np.arange(128)[None, :] // 64).astype(np.float32)
    k = np.arange(128)[:, None]
    q = np.arange(128)[None, :]
    masks = np.zeros((128, 5, 128), np.float32)
    masks[:, 0] = (k <= q)
    masks[:, 1] = (k > q)
    masks[:, 2] = (k > q) if half == 1 else 0.0
    masks[:, 3] = ((k // 8 == q // 8) & (k % 8 <= q % 8))
    masks[:, 4, 0:8] = (k > np.arange(8)[None, :])
    mcore = np.full((128, 1), 1.0 if half == 1 else 0.0, np.float32)
    return cosT, sinT, PT0, OB, masks.reshape(128, 640), mcore


def _prep(inp):
    f = lambda a: np.ascontiguousarray(np.asarray(a, dtype=np.float32))
    x_prompt, x_sample = f(inp["x_prompt"]), f(inp["x_sample"])
    shared = {}
    wa = f(inp["w_ada"])
    shared["w_ada0"] = f(wa[0].reshape(8, 128, 4, 1536).transpose(2, 1, 0, 3).reshape(512, 8 * 1536))
    shared["w_ada1"] = f(wa[1].reshape(8, 128, 48, 128).transpose(2, 1, 0, 3).reshape(48 * 128, 1024))
    cidx = np.concatenate([np.arange(0, 512), np.arange(512, 576), np.arange(512, 576), np.arange(576, 640), np.arange(576, 640),
                           np.arange(640, 1792)])
    wi = f(inp["w_in"])[:, :, cidx]
    shared["w_in"] = f(wi.reshape(2, 8, 128, 1920).transpose(0, 2, 1, 3).reshape(256, 8 * 1920))
    shared["w_out"] = f(f(inp["w_out"]).reshape(2, 8, 128, 1024).transpose(0, 2, 1, 3).reshape(256, 8192))
    wfi = f(inp["w_ffn_in"]).reshape(2, 8, 128, 2, 22, 128)
    shared["w_ffn_in"] = f(wfi.transpose(0, 4, 2, 1, 3, 5).reshape(2 * 22 * 128, 8 * 256))
    wfo = f(inp["w_ffn_out"]).reshape(2, 22, 128, 8, 128)
    shared["w_ffn_out"] = f(wfo.transpose(0, 3, 2, 1, 4).reshape(2 * 8 * 128, 22 * 128))
    shared["b_adaT"] = f(f(inp["b_ada"]).reshape(2, 48, 128).transpose(2, 0, 1).reshape(128, 96))
    shared["g_attnT"] = f(f(inp["g_attn"]).reshape(2, 8, 128).transpose(2, 0, 1).reshape(128, 16))
    shared["g_ffnT"] = f(f(inp["g_ffn"]).reshape(2, 8, 128).transpose(2, 0, 1).reshape(128, 16))
    gq, gk = f(inp["g_q"]), f(inp["g_k"])
    gqk = np.zeros((128, 4), np.float32)
    for l in range(2):
        gqk[:, 2 * l] = np.tile(gq[l], 2)
        gqk[:, 2 * l + 1] = np.tile(gk[l], 2)
    shared["gqk"] = gqk
    sk = f(inp["sinks"])
    sinkT = np.zeros((128, 8), np.float32)
    for l in range(2):
        for c in range(4):
            sinkT[0:64, l * 4 + c] = sk[l, 2 * c]
            sinkT[64:128, l * 4 + c] = sk[l, 2 * c + 1]
    shared["sinkT"] = sinkT
    lngb = np.stack([f(inp["ln_g"]).reshape(2, 512), f(inp["ln_b"]).reshape(2, 512)], axis=1)
    shared["lngb"] = f(np.broadcast_to(lngb.reshape(1, 2048), (128, 2048)))
    ws = f(inp["w_s"])
    shared["wsT"] = f(ws.transpose(3, 0, 1, 2).reshape(128, 1024))
    wbd = np.zeros((128, 2, 4, 128), np.float32)
    for sq in range(16):
        wbd[sq * 8:sq * 8 + 8, :, :, sq * 8:sq * 8 + 8] = ws[:, :, 0:8, 0:8].transpose(3, 0, 1, 2)
    shared["wbd"] = wbd.reshape(128, 1024)
    bs = f(inp["b_s"])
    shared["bsrow"] = f(bs.reshape(1, 1024))
    shared["bsrow_s"] = f(np.tile(bs[:, :, 0:8], (1, 1, 16)).reshape(1, 1024))
    shared["conv_wT"] = f(f(inp["conv_w"]).reshape(2, 3, 44, 128).transpose(3, 0, 1, 2).reshape(128, 264))
    shared["conv_bT"] = f(f(inp["conv_b"]).reshape(2, 44, 128).transpose(2, 0, 1).reshape(128, 88))
    ck, cv, cc = f(inp["cache_k"]), f(inp["cache_v"]), f(inp["cache_conv"])
    cp_, cs_ = f(inp["c_prompt"]), f(inp["c_sample"])
    maps = []
    for core in range(8):
        b, half = core // 2, core % 2
        m = dict(shared)
        cosT, sinT, PT0, OB, masks, mcore = _consts(core)
        m.update(cosT=cosT, sinT=sinT, PT0=PT0, OB=OB, masks=masks, mcore=mcore)
        xs = np.zeros((NT, D), np.float32)
        if half == 0:
            xs[384:NPT] = x_prompt[b, 0:2048]
        else:
            xs[0:NPT] = x_prompt[b, 1664:4096]
        sq0 = core * 16
        xs[NPT:] = x_sample[sq0:sq0 + 16].reshape(128, D)
        m["xT"] = f(xs.reshape(NT, 8, 128).transpose(2, 1, 0).reshape(128, 8 * NT))
        cs = np.concatenate([cp_[b:b + 1], cs_[sq0:sq0 + 16]], axis=0)
        m["cT"] = f(cs.reshape(17, 8, 128).transpose(2, 1, 0).reshape(128, 136))
        kc = ck[:, sq0:sq0 + 16]
        kt = kc.transpose(0, 4, 3, 1, 2)
        kt = np.concatenate([kt, kt], axis=1)
        m["kTc"] = f(kt.reshape(256, 4096))
        vcs = cv[:, sq0:sq0 + 16].transpose(0, 2, 1, 3, 4)
        m["vc"] = f(vcs.reshape(256, 2048))
        ccs = cc[:, sq0:sq0 + 16].reshape(2, 16, 2, 44, 128).transpose(0, 4, 3, 1, 2)
        m["ccT"] = f(ccs.reshape(256, 44 * 32))
        maps.append(m)
    return maps


_NC = None


def kernel(**inputs):
    global _NC
    maps = _prep(inputs)
    if _NC is None:
        _NC = build_nc()
    res = run_bass_kernel_spmd(_NC, maps, core_ids=list(range(8)))
    R = [{k: np.asarray(v, dtype=np.float32) for k, v in r.items()} for r in res.results]
    y_prompt = np.zeros((4, 4096, D), np.float32)
    y_sample = np.zeros((128, 8, D), np.float32)
    nkp = np.zeros((2, 4, 128, 2, 64), np.float32)
    nvp = np.zeros((2, 4, 128, 2, 64), np.float32)
    ncp = np.zeros((2, 4, 2, 2 * DFF), np.float32)
    nks = np.zeros((2, 128, 8, 2, 64), np.float32)
    nvs = np.zeros((2, 128, 8, 2, 64), np.float32)
    ngs = np.zeros((2, 128, 8, 512), np.float32)
    ncs = np.zeros((2, 128, 2, 2 * DFF), np.float32)
    for core in range(8):
        b, half = core // 2, core % 2
        r = R[core]
        y_prompt[b, half * 2048:(half + 1) * 2048] = r["yT"].reshape(128, 8, 2048).transpose(2, 1, 0).reshape(2048, D)
        sq0 = core * 16
        y_sample[sq0:sq0 + 16] = r["ysT"].reshape(128, 8, 128).transpose(2, 1, 0).reshape(16, 8, D)
        ks = r["ksT"].reshape(2, 128, 2, 128)[:, 0:64]
        nks[:, sq0:sq0 + 16] = ks.transpose(0, 3, 2, 1).reshape(2, 16, 8, 2, 64)
        nvs[:, sq0:sq0 + 16] = r["vs"].reshape(2, 16, 8, 2, 64)
        ngs[:, sq0:sq0 + 16] = r["gvs"].reshape(2, 16, 8, 512)
        ncs[:, sq0:sq0 + 16] = r["convs"].reshape(2, 128, 44, 16, 2).transpose(0, 3, 4, 2, 1).reshape(2, 16, 2, 2 * DFF)
        if half == 1:
            kp = r["kpT"].reshape(2, 128, 2, 128)[:, 0:64]
            nkp[:, b] = kp.transpose(0, 3, 2, 1)
            nvp[:, b] = r["vp"].reshape(2, 128, 2, 64)
            ncp[:, b] = r["convp"].reshape(2, 128, 44, 2).transpose(0, 3, 2, 1).reshape(2, 2, 2 * DFF)
    return (y_prompt, y_sample, nkp, nvp, ncp, nks, nvs, ngs, ncs)
```
